# Optimizing a Trainium2 kernel written in Bass

```python
import math
import jax, jax.numpy as jnp
from jax import lax
import numpy as np

D_MODEL = 2048
BATCH = 8
SEQ = 2048
DEPTH = 1

CHUNK = 128
A_WIDTH = D_MODEL
A_GROUPS = 8
A_GROUP_DIM = A_WIDTH // A_GROUPS
B_HEADS = 8
B_HEAD_DIM = D_MODEL // B_HEADS // 2
B_QK = B_HEADS * 2 * B_HEAD_DIM
B_V = B_HEADS * 2 * B_HEAD_DIM
Q_BLOCK = 128
EPS = 1e-6
SUBLN_EPS = 1e-5
SPLIT_SIZES = (A_WIDTH, A_WIDTH, A_WIDTH, B_QK, B_QK, B_V, B_V, D_MODEL, D_MODEL)
IN_COLS = sum(SPLIT_SIZES)
SPLIT_POINTS = tuple(int(v) for v in np.cumsum(SPLIT_SIZES)[:-1])

kernel_name = "hybrid_gmlp_diffattn_gated_block"


def _rmsnorm(x, g, eps=EPS):
    xf = x.astype(jnp.float32)
    y = xf * lax.rsqrt(jnp.mean(xf * xf, axis=-1, keepdims=True) + eps)
    return (y * g.astype(jnp.float32)).astype(x.dtype)


def _layernorm(x, g, b, eps=EPS):
    xf = x.astype(jnp.float32)
    mu = jnp.mean(xf, axis=-1, keepdims=True)
    xc = xf - mu
    y = xc * lax.rsqrt(jnp.mean(xc * xc, axis=-1, keepdims=True) + eps)
    return (y * g.astype(jnp.float32) + b.astype(jnp.float32)).astype(x.dtype)


def _alibi_slopes(n_heads):
    return jnp.asarray([2.0 ** (-8.0 * (i + 1) / n_heads) for i in range(n_heads)], dtype=jnp.float32)


def _lambda_init(layer_idx):
    return 0.8 - 0.6 * math.exp(-0.3 * layer_idx)


def _gmlp_branch(u, v, z, ln_g, ln_b, w_s, b_s):
    bsz, seq, _ = u.shape
    n_chunks = seq // CHUNK
    vn = _layernorm(v, ln_g, ln_b)
    vc = vn.reshape(bsz, n_chunks, CHUNK, A_GROUPS, A_GROUP_DIM)
    causal = jnp.tril(jnp.ones((CHUNK, CHUNK), dtype=w_s.dtype))
    ws = w_s * causal[None]
    sv = jnp.einsum('gts,bnsgd->bntgd', ws, vc) + jnp.transpose(b_s)[None, None, :, :, None]
    y = u * sv.reshape(bsz, seq, A_WIDTH)
    return y * jax.nn.silu(z)


def _diff_attn_branch(q, k, v, z, lam, lam_init, subln_g):
    bsz, seq, _ = q.shape
    n_blocks = seq // Q_BLOCK
    q = q.reshape(bsz, seq, B_HEADS, 2, B_HEAD_DIM)
    k = k.reshape(bsz, seq, B_HEADS, 2, B_HEAD_DIM)
    vh = v.reshape(bsz, seq, B_HEADS, 2 * B_HEAD_DIM)
    qb = jnp.transpose(q.reshape(bsz, n_blocks, Q_BLOCK, B_HEADS, 2, B_HEAD_DIM), (1, 0, 2, 3, 4, 5))
    slopes = _alibi_slopes(B_HEADS)
    scale = 1.0 / math.sqrt(B_HEAD_DIM)
    key_pos = jnp.arange(seq)

    def block(args):
        qi, bi = args
        t_pos = bi * Q_BLOCK + jnp.arange(Q_BLOCK)
        dist = (t_pos[:, None] - key_pos[None, :]).astype(jnp.float32)
        bias = -slopes[:, None, None] * dist[None]
        sc = jnp.einsum('bthcd,bshcd->bhcts', qi, k).astype(jnp.float32) * scale
        sc = sc + bias[None, :, None]
        sc = jnp.where((dist >= 0)[None, None, None], sc, -jnp.inf)
        p = jax.nn.softmax(sc, axis=-1)
        w = p[:, :, 0] - lam * p[:, :, 1]
        return jnp.einsum('bhts,bshe->bthe', w.astype(vh.dtype), vh)

    o = lax.map(block, (qb, jnp.arange(n_blocks)))
    o = jnp.transpose(o, (1, 0, 2, 3, 4)).reshape(bsz, seq, B_HEADS, 2 * B_HEAD_DIM)
    o = _rmsnorm(o, subln_g, SUBLN_EPS) * (1.0 - lam_init)
    return o.reshape(bsz, seq, B_V) * jax.nn.silu(z)


def setup_inputs(seed: int = 0) -> dict:
    key = jax.random.key(seed)
    ks = jax.random.split(key, 20)
    f32 = jnp.float32
    L = DEPTH
    nrm = lambda k, shape, s: jax.random.normal(k, shape, f32) * s
    return {
        "x": jax.random.normal(ks[0], (BATCH, SEQ, D_MODEL), f32),
        "c": jax.random.normal(ks[1], (BATCH, D_MODEL), f32),
        "w_ada": nrm(ks[2], (L, D_MODEL, 3 * D_MODEL), 0.5 * D_MODEL ** -0.5),
        "b_ada": nrm(ks[3], (L, 3 * D_MODEL), 0.01),
        "norm_gain": 1.0 + nrm(ks[4], (L, D_MODEL), 0.02),
        "w_in": nrm(ks[5], (L, D_MODEL, IN_COLS), D_MODEL ** -0.5),
        "ln_v_gain": 1.0 + nrm(ks[6], (L, A_WIDTH), 0.02),
        "ln_v_bias": nrm(ks[7], (L, A_WIDTH), 0.01),
        "w_spatial": nrm(ks[8], (L, A_GROUPS, CHUNK, CHUNK), CHUNK ** -0.5),
        "b_spatial": 1.0 + nrm(ks[9], (L, A_GROUPS, CHUNK), 0.02),
        "lambda_q1": nrm(ks[10], (L, B_HEAD_DIM), 0.1),
        "lambda_k1": nrm(ks[11], (L, B_HEAD_DIM), 0.1),
        "lambda_q2": nrm(ks[12], (L, B_HEAD_DIM), 0.1),
        "lambda_k2": nrm(ks[13], (L, B_HEAD_DIM), 0.1),
        "subln_gain": 1.0 + nrm(ks[14], (L, 2 * B_HEAD_DIM), 0.02),
        "w_branch_a": nrm(ks[15], (L, A_WIDTH, D_MODEL), A_WIDTH ** -0.5),
        "w_branch_b": nrm(ks[16], (L, B_V, D_MODEL), B_V ** -0.5),
        "w_out": nrm(ks[17], (L, D_MODEL, D_MODEL), D_MODEL ** -0.5),
        "final_norm_gain": 1.0 + nrm(ks[18], (D_MODEL,), 0.02),
    }


def reference(x, c, w_ada, b_ada, norm_gain, w_in, ln_v_gain, ln_v_bias, w_spatial, b_spatial,
              lambda_q1, lambda_k1, lambda_q2, lambda_k2, subln_gain, w_branch_a, w_branch_b,
              w_out, final_norm_gain):
    for l in range(DEPTH):
        mod = jax.nn.silu(c) @ w_ada[l] + b_ada[l]
        shift, scale, gate = jnp.split(mod, 3, axis=-1)
        h = _rmsnorm(x, norm_gain[l]) * (1.0 + scale[:, None, :]) + shift[:, None, :]

        proj = jnp.einsum('bsd,de->bse', h, w_in[l])
        a_u, a_v, a_z, b_q, b_k, b_v, b_z, g_a, g_b = jnp.split(proj, SPLIT_POINTS, axis=-1)

        y_a = _gmlp_branch(jax.nn.gelu(a_u), jax.nn.gelu(a_v), a_z, ln_v_gain[l], ln_v_bias[l],
                           w_spatial[l], b_spatial[l])

        lam_init = _lambda_init(l)
        lam = (jnp.exp(jnp.sum(lambda_q1[l].astype(jnp.float32) * lambda_k1[l].astype(jnp.float32)))
               - jnp.exp(jnp.sum(lambda_q2[l].astype(jnp.float32) * lambda_k2[l].astype(jnp.float32)))
               + lam_init)
        y_b = _diff_attn_branch(b_q, b_k, b_v, b_z, lam, lam_init, subln_gain[l])

        m = (jax.nn.sigmoid(g_a) * (y_a @ w_branch_a[l])
             + jax.nn.sigmoid(g_b) * (y_b @ w_branch_b[l]))
        out = m @ w_out[l]
        x = x + gate[:, None, :] * out
    return _rmsnorm(x, final_norm_gain)
```

```python
import math
from contextlib import ExitStack

import numpy as np
import concourse.bass as bass
import concourse.mybir as mybir
from concourse.bass_utils import run_bass_kernel_spmd

F32 = mybir.dt.float32
BF16 = mybir.dt.bfloat16
AF = mybir.ActivationFunctionType
ALU = mybir.AluOpType
AX = mybir.AxisListType

LAM_INIT = 0.8 - 0.6 * math.exp(-0.3 * 0)
EPS = 1e-6
SUBLN_EPS = 1e-5
NEG = -30000.0
LOFF = 384
LLEN = 2432


class Buf:
    _n = 0

    def __init__(self, name=""):
        Buf._n += 1
        self.id = Buf._n
        self.name = name
        self.w = None
        self.r = []
        self.dma_sem = None
        self.dma_cnt = 0


class Sch:
    def __init__(self, nc, stack):
        self.nc = nc
        self.stack = stack
        self.eng = {"pe": nc.tensor, "act": nc.scalar, "dve": nc.vector,
                    "pool": nc.gpsimd, "sp": nc.sync}
        self.sems = {}
        self.cnt = {}
        for e in ("pe", "act", "dve", "pool"):
            self.sems[e] = stack.enter_context(nc.semaphore("c_" + e))
            self.cnt[e] = 0
        self.seen = {e: {} for e in self.eng}
        self.owners = []
        self.ninst = 0
        self.nwait = 0

    def _deps(self, reads, writes):
        d = {}

        def add(ev):
            if ev is None:
                return
            k, v = ev
            if d.get(k, -1) < v:
                d[k] = v
        for b in reads:
            add(b.w)
        for b in writes:
            add(b.w)
            for ev in b.r:
                add(ev)
        return d

    def _wait(self, e, deps):
        eng = self.eng[e]
        seen = self.seen[e]
        for k, v in deps.items():
            if k == e and e == "pe":
                continue
            if seen.get(k, 0) >= v:
                continue
            eng.wait_ge(self.sems[k], v)
            seen[k] = v
            self.nwait += 1

    def _record(self, ev, reads, writes):
        for b in reads:
            b.r.append(ev)
            if len(b.r) > 64:
                best = {}
                for k, v in b.r:
                    if best.get(k, -1) < v:
                        best[k] = v
                b.r = list(best.items())
        for b in writes:
            b.w = ev
            b.r = []

    def op(self, e, fn, reads=(), writes=()):
        self._wait(e, self._deps(reads, writes))
        ins = fn(self.eng[e])
        self.cnt[e] += 1
        ins.then_inc(self.sems[e], 1)
        ev = (e, self.cnt[e])
        self._record(ev, reads, writes)
        self.ninst += 1
        return ev

    def group(self, e, fns, reads=(), writes=()):
        self._wait(e, self._deps(reads, writes))
        ins = None
        for fn in fns:
            ins = fn(self.eng[e])
            self.ninst += 1
        self.cnt[e] += 1
        ins.then_inc(self.sems[e], 1)
        ev = (e, self.cnt[e])
        self._record(ev, reads, writes)
        return ev

    def dma(self, q, out, in_, reads=(), writes=(), owner=None, **kw):
        if owner is None:
            owner = writes[0] if writes else reads[0]
        if owner.dma_sem is None:
            owner.dma_sem = self.stack.enter_context(self.nc.semaphore("d_%d" % owner.id))
            self.sems[("d", owner.id)] = owner.dma_sem
            self.owners.append(owner)
        self._wait(q, self._deps(reads, writes))
        ins = self.eng[q].dma_start(out=out, in_=in_, **kw)
        owner.dma_cnt += 1
        ins.then_inc(owner.dma_sem, 16)
        ev = (("d", owner.id), 16 * owner.dma_cnt)
        self._record(ev, reads, writes)
        self.ninst += 1
        return ev

    def barrier(self, engines=("pe", "act", "dve", "pool", "sp")):
        d = {}
        for e in ("pe", "act", "dve", "pool"):
            if self.cnt[e] > 0:
                d[e] = self.cnt[e]
        for o in self.owners:
            d[("d", o.id)] = 16 * o.dma_cnt
        for e in engines:
            eng = self.eng[e]
            seen = self.seen[e]
            for k, v in d.items():
                if k == e:
                    continue
                if seen.get(k, 0) >= v:
                    continue
                eng.wait_ge(self.sems[k], v)
                seen[k] = v
                self.nwait += 1


def build_program(S, D, debug=False):
    NT = S // 128
    NB = S // 512
    DC = D // 128
    ND = D // 512
    H = D // 256
    G = D // 256
    SLOPES = [2.0 ** (-8.0 * (i + 1) / H) for i in range(H)]
    assert S % 512 == 0 and D % 512 == 0 and NB <= 4
    nc = bass.Bass("TRN2", target_bir_lowering=False)

    def din(name, shape):
        return nc.dram_tensor(name, list(shape), F32, kind="ExternalInput").ap()

    x = din("x", [S, D])
    cT = din("cT", [128, DC])
    w_ada = din("w_ada", [D, 3 * D])
    b_adaT = din("b_adaT", [128, 3 * DC])
    ngT = din("ngT", [128, DC])
    w_in = din("w_in", [D, 9 * D])
    lngT = din("lngT", [128, DC])
    lnbT = din("lnbT", [128, DC])
    w_sp = din("w_sp", [G, 128, 128])
    b_sp = din("b_sp", [1, G * 128])
    lamv = din("lamv", [1, 4 * 128])
    sublng = din("sublng", [1, 256])
    w_a = din("w_a", [D, D])
    w_b = din("w_b", [D, D])
    w_o = din("w_o", [D, D])
    fng = din("fng", [1, D])
    c_ident = din("c_ident", [128, 128])
    c_tril = din("c_tril", [128, 128])
    c_maskT = din("c_maskT", [128, 128])
    c_laug = din("c_laug", [128, LLEN])
    c_kaug = din("c_kaug", [128, H * 128])
    c_btab = din("c_btab", [128, H * 16])
    y = nc.dram_tensor("y", [S, D], F32, kind="ExternalOutput").ap()
    skind = "ExternalOutput" if debug else "Internal"
    yaT = nc.dram_tensor("yaT", [D, S], BF16, kind=skind).ap()
    ybT = nc.dram_tensor("ybT", [D, S], BF16, kind=skind).ap()
    sgT = [nc.dram_tensor("sgT%d" % i, [D, S], BF16, kind=skind).ap() for i in range(2)]
    mT = nc.dram_tensor("mT", [D, S], BF16, kind=skind).ap()

    with ExitStack() as g:
        S_ = Sch(nc, g)
        sb = lambda st, name, shape, dt: st.enter_context(nc.sbuf_tensor(name, list(shape), dt))

        psall = g.enter_context(nc.psum_tensor("psall", [128, 8, 512], F32))
        ps = [psall[:, i, :] for i in range(8)]
        b_ps = [Buf("ps%d" % i) for i in range(8)]
        pstate = {"i": 0}

        def bank():
            i = pstate["i"]
            pstate["i"] = (i + 1) % 8
            return ps[i], b_ps[i]

        ring = [sb(g, "ring%d" % i, [128, 8192], BF16) for i in range(2)]
        b_ring = [Buf("ring%d" % i) for i in range(2)]
        rstate = {"i": 0}

        def ring_next():
            i = rstate["i"]
            rstate["i"] = (i + 1) % len(ring)
            return ring[i], b_ring[i]

        def slab_view(t):
            return t[:, 0:DC * 512].rearrange("p (k n) -> p k n", k=DC)

        def wsrc(w, c0, n):
            return w[:, c0:c0 + n].rearrange("(k p) n -> p k n", p=128)

        pre = {"slots": []}

        def preload(load, count=1):
            for i in range(count):
                sl = ring_next()
                load(i, *sl)
                pre["slots"].append(sl)

        def stream(n, load, compute):
            depth = len(ring) - 1
            slots = {}
            nxt = 0
            for sl in pre["slots"]:
                slots[nxt] = sl
                nxt += 1
            pre["slots"] = []
            for i in range(n):
                while nxt < n and nxt <= i + depth:
                    slots[nxt] = ring_next()
                    load(nxt, *slots[nxt])
                    nxt += 1
                compute(i, *slots[i])
                del slots[i]

        def ada_load(i, t, b):
            S_.dma("pool", slab_view(t), wsrc(w_ada, i * 512, 512), writes=[b])

        def v_load(i, t, b):
            S_.dma("pool", slab_view(t), wsrc(w_in, D + i * 512, 512), writes=[b])

        def a_load(i, t, b):
            sv_ = slab_view(t)
            S_.dma("pool", sv_[:, :, 0:256], wsrc(w_in, i * 256, 256), writes=[b])
            S_.dma("pool", sv_[:, :, 256:512], wsrc(w_in, 2 * D + i * 256, 256), writes=[b])

        gjobs = []
        for i in range(2 * ND):
            gjobs.append(("g", i))
            if i % 2 == 1:
                gjobs.append(("ada", 2 * ND + i // 2))

        def g_load(i, t, b):
            kind, i = gjobs[i]
            if kind == "ada":
                return ada_load(i, t, b)
            f_, cb = divmod(i, ND)
            S_.dma("pool", slab_view(t), wsrc(w_in, (7 + f_) * D + cb * 512, 512), writes=[b])

        def b_load(i, t, b):
            h, which = divmod(i, 2)
            sv_ = slab_view(t)
            f0 = 3 if which == 0 else 5
            S_.dma("pool", sv_[:, :, 0:256], wsrc(w_in, f0 * D + h * 256, 256), writes=[b])
            S_.dma("pool", sv_[:, :, 256:512], wsrc(w_in, (f0 + 1) * D + h * 256, 256), writes=[b])

        def m_load(i, t, b):
            sv_ = slab_view(t)
            S_.dma("pool", sv_[:, :, 0:256], wsrc(w_a, i * 256, 256), writes=[b])
            S_.dma("pool", sv_[:, :, 256:512], wsrc(w_b, i * 256, 256), writes=[b])

        b_c = Buf("consts")
        ident_f = sb(g, "ident_f", [128, 128], F32)
        junk = sb(g, "junk", [128, D], BF16)
        modT = sb(g, "modT", [128, 3 * DC], F32)
        Acoef = sb(g, "Acoef", [128, DC], F32)
        mhalf = sb(g, "mhalf", [128, 16], F32)
        b_small = Buf("small")
        S_.dma("sp", ident_f[:], c_ident, writes=[b_c])
        S_.op("dve", lambda e: e.memset(mhalf[:], -0.5), writes=[b_small])

        b_mod = Buf("mod")
        b_mod2 = Buf("mod_gate")
        b_A = Buf("Acoef")

        with ExitStack() as st_h:
            ident_b = sb(st_h, "ident_b", [128, 128], BF16)
            maskT_b = sb(st_h, "maskT_b", [128, 128], BF16)
            laug = sb(st_h, "laug", [128, LLEN], BF16)
            kaug = sb(st_h, "kaug", [128, H * 128], BF16)
            C2T = sb(st_h, "C2T", [128, DC, 128], F32)
            wsT = sb(st_h, "wsT", [128, G, 128], BF16)
            G2 = sb(st_h, "G2", [128, 256], F32)
            lng_s = sb(st_h, "lng_s", [128, DC], F32)
            lnb_s = sb(st_h, "lnb_s", [128, DC], F32)
            neglam = sb(st_h, "neglam", [128, 1], F32)
            btab = sb(st_h, "btab", [128, H * 16], F32)
            S_.dma("sp", btab[:], c_btab, writes=[b_c])
            b_cp = Buf("consts_pool")
            S_.dma("pool", ident_b[:], c_ident, writes=[b_cp])
            S_.dma("pool", maskT_b[:], c_maskT, writes=[b_cp])
            S_.dma("pool", laug[:], c_laug, writes=[b_cp])
            S_.dma("pool", kaug[:], c_kaug, writes=[b_cp])
            S_.dma("sp", lng_s[:], lngT, writes=[b_c])
            S_.dma("sp", lnb_s[:], lnbT, writes=[b_c])
            scb = sb(st_h, "scb", [128, DC], BF16)
            badT = sb(st_h, "badT", [128, 3 * DC], F32)
            hT = sb(st_h, "hT", [128, DC, S], BF16)
            b_hT = [Buf("hT%d" % i) for i in range(NB)]

            with ExitStack() as ph:
                cs = sb(ph, "cs", [128, DC], F32)
                ng_s = sb(ph, "ng_s", [128, DC], F32)
                lvec = sb(ph, "lvec", [128, 4 * 128], F32)
                lj = sb(ph, "lj", [128, 128], F32)
                ld = sb(ph, "ld", [128, 4], F32)
                wtmp = sb(ph, "wtmp", [128, 128], F32)
                wsTf = sb(ph, "wsTf", [128, 128], F32)
                tril = sb(ph, "tril", [128, 128], F32)
                ones_f = sb(ph, "ones_f", [128, 128], F32)
                bs_row = sb(ph, "bs_row", [1, 128], F32)
                b_bsr = Buf("bs_row")
                BSb = sb(ph, "BSb", [128, 128], F32)
                NXT = 2
                xt = [sb(ph, "xt%d" % i, [128, D], F32) for i in range(NXT)]
                xs_all = sb(ph, "xs_all", [128, NT, D], BF16)
                ss = sb(ph, "ss", [128, NT], F32)
                rstd = sb(ph, "rstd", [128, NT], F32)
                b_xt = [Buf("xt%d" % i) for i in range(NXT)]
                b_xs = [Buf("xs%d" % i) for i in range(NT)]
                b_p0 = Buf("p0c")
                b_sc = Buf("sc")
                b_ss = Buf("ss")

                S_.dma("sp", cs[:], cT, writes=[b_p0])
                S_.dma("sp", badT[:], b_adaT, writes=[b_p0])
                S_.dma("sp", ng_s[:], ngT, writes=[b_p0])
                S_.dma("sp", lvec[:], lamv.partition_broadcast(128), writes=[b_p0])
                S_.dma("sp", tril[:], c_tril, writes=[b_p0])
                S_.dma("sp", G2[:], sublng.partition_broadcast(128), writes=[b_p0])
                S_.op("act", lambda e: e.activation(out=scb[:], in_=cs[:], func=AF.Silu),
                      reads=[b_p0], writes=[b_sc])
                S_.op("dve", lambda e: e.memset(ones_f[:], 1.0), writes=[b_small])

                def load_xt(tt):
                    S_.dma("sp", xt[tt % NXT][:], x[tt * 128:(tt + 1) * 128, :], writes=[b_xt[tt % NXT]])
                for tt in range(min(NXT, NT)):
                    load_xt(tt)

                def p1a(tt):
                    xb, bx = xt[tt % NXT], b_xt[tt % NXT]
                    S_.op("act", lambda e: e.activation(out=junk[:], in_=xb[:], func=AF.Square, accum_out=ss[:, tt:tt + 1]),
                          reads=[bx], writes=[b_ss])
                    S_.op("dve", lambda e: e.tensor_scalar(out=ss[:, tt:tt + 1], in0=ss[:, tt:tt + 1], scalar1=1.0 / D, scalar2=EPS,
                                                           op0=ALU.mult, op1=ALU.add), reads=[b_ss], writes=[b_ss])
                    S_.op("pool", lambda e: e.tensor_tensor(out=rstd[:, tt:tt + 1], in0=ss[:, tt:tt + 1], in1=mhalf[:, 0:1], op=ALU.pow),
                          reads=[b_ss, b_small], writes=[b_ss])
                    S_.op("dve", lambda e: e.tensor_scalar(out=xs_all[:, tt, :], in0=xb[:], scalar1=rstd[:, tt:tt + 1],
                                                           scalar2=None, op0=ALU.mult), reads=[bx, b_ss], writes=[b_xs[tt]])
                    if tt + NXT < NT:
                        load_xt(tt + NXT)
                p1a_state = {"tt": 0}
                p1a_per_job = -(-NT // (2 * ND))

                def ada_compute_into(bmod, with_p1a=False):
                    def ada_compute(i, t, b):
                        sv_ = slab_view(t)
                        pm, b_pm = bank()
                        fns = []
                        for jj in range(4):
                            for k in range(DC):
                                fns.append(lambda e, jj=jj, k=k: e.matmul(
                                    pm[:, jj:jj + 1], lhsT=sv_[:, k, jj * 128:(jj + 1) * 128],
                                    rhs=scb[:, k:k + 1], start=(k == 0), stop=(k == DC - 1)))
                        S_.group("pe", fns, reads=[b, b_sc], writes=[b_pm])
                        S_.op("dve", lambda e: e.tensor_tensor(out=modT[:, i * 4:(i + 1) * 4], in0=pm[:, 0:4],
                                                               in1=badT[:, i * 4:(i + 1) * 4], op=ALU.add),
                              reads=[b_pm, b_p0], writes=[bmod])
                        if with_p1a:
                            for _ in range(p1a_per_job):
                                if p1a_state["tt"] < NT:
                                    p1a(p1a_state["tt"])
                                    p1a_state["tt"] += 1
                    return ada_compute

                b_l = Buf("lam")
                for i in range(2):
                    S_.op("dve", lambda e, i=i: e.scalar_tensor_tensor(
                        out=lj[:], in0=lvec[:, (2 * i) * 128:(2 * i + 1) * 128], scalar=1.0,
                        in1=lvec[:, (2 * i + 1) * 128:(2 * i + 2) * 128],
                        op0=ALU.mult, op1=ALU.mult, accum_out=ld[:, i:i + 1]), reads=[b_p0], writes=[b_l])
                S_.op("act", lambda e: e.activation(out=ld[:, 2:4], in_=ld[:, 0:2], func=AF.Exp),
                      reads=[b_l], writes=[b_l])
                S_.op("dve", lambda e: e.tensor_tensor(out=neglam[:], in0=ld[:, 3:4], in1=ld[:, 2:3], op=ALU.subtract),
                      reads=[b_l], writes=[b_l])
                S_.op("dve", lambda e: e.tensor_scalar(out=neglam[:], in0=neglam[:], scalar1=-LAM_INIT, scalar2=None,
                                                       op0=ALU.add), reads=[b_l], writes=[b_l])
                S_.op("dve", lambda e: e.tensor_scalar(out=G2[:], in0=G2[:], scalar1=1.0 - LAM_INIT, scalar2=None,
                                                       op0=ALU.mult), reads=[b_p0, b_l], writes=[b_l])

                b_w = Buf("wsp")
                b_ws = Buf("wsT")
                for gi in range(G):
                    S_.dma("sp", wtmp[:], w_sp[gi], reads=[], writes=[b_w])
                    S_.op("dve", lambda e: e.tensor_tensor(out=wtmp[:], in0=wtmp[:], in1=tril[:], op=ALU.mult),
                          reads=[b_p0], writes=[b_w])
                    pt, b_pt = bank()
                    S_.group("pe", [lambda e, pt=pt: e.transpose(pt[:, 0:128], wtmp[:], ident_f[:])],
                             reads=[b_w, b_c], writes=[b_pt])
                    S_.op("dve", lambda e, pt=pt: e.tensor_copy(out=wsTf[:], in_=pt[:, 0:128]),
                          reads=[b_pt], writes=[b_ws])
                    S_.op("act", lambda e, pt=pt, gi=gi: e.activation(out=wsT[:, gi, :], in_=pt[:, 0:128], func=AF.Copy),
                          reads=[b_pt], writes=[b_ws])
                    pr, b_pr = bank()
                    S_.dma("sp", bs_row[:], b_sp[0:1, gi * 128:(gi + 1) * 128], writes=[b_bsr])
                    S_.group("pe", [lambda e, pr=pr: e.matmul(pr[:, 0:128], lhsT=ones_f[:], rhs=wsTf[:], start=True, stop=True),
                                    lambda e, pr=pr, gi=gi: e.matmul(pr[:, 128:256], lhsT=ones_f[0:1, :],
                                                                    rhs=bs_row[0:1, :],
                                                                    start=True, stop=True)],
                             reads=[b_ws, b_small, b_p0, b_bsr], writes=[b_pr])
                    S_.op("dve", lambda e, pr=pr: e.tensor_copy(out=BSb[:], in_=pr[:, 128:256]),
                          reads=[b_pr], writes=[b_w])
                    for ci in range(2):
                        c = gi * 2 + ci
                        S_.op("dve", lambda e, pr=pr, c=c: e.scalar_tensor_tensor(
                            out=C2T[:, c, :], in0=pr[:, 0:128], scalar=lnb_s[:, c:c + 1], in1=BSb[:],
                            op0=ALU.mult, op1=ALU.add), reads=[b_pr, b_w, b_c], writes=[b_ws])

                ada_order = []
                for q in range(ND):
                    ada_order += [q, ND + q]
                b_Aq = [Buf("A%d" % q) for q in range(ND)]
                psb = psall[:].bitcast(BF16)
                p1a_per_job = -(-NT // 2)

                def p1b_quarter(q):
                    for k in range(4 * q, 4 * q + 4):
                        for tb in range(NB):
                            bi_ = pstate["i"]
                            pt, b_pt = bank()
                            ptb = psb[:, bi_, :]
                            S_.group("pe", [lambda e, r=r: e.transpose(ptb[:, r * 128:(r + 1) * 128],
                                                                       xs_all[:, tb * 4 + r, k * 128:(k + 1) * 128], ident_b[:])
                                            for r in range(4)],
                                     reads=[b_xs[tb * 4 + r] for r in range(4)] + [b_cp], writes=[b_pt])
                            if (k + tb) % 2 == 0:
                                S_.op("dve", lambda e: e.tensor_scalar(
                                    out=hT[:, k, tb * 512:(tb + 1) * 512], in0=ptb[:, 0:512], scalar1=Acoef[:, k:k + 1],
                                    scalar2=modT[:, k:k + 1], op0=ALU.mult, op1=ALU.add),
                                    reads=[b_pt, b_Aq[q]], writes=[b_hT[tb]])
                            else:
                                S_.op("act", lambda e: e.activation(
                                    out=hT[:, k, tb * 512:(tb + 1) * 512], in_=ptb[:, 0:512], func=AF.Identity,
                                    scale=Acoef[:, k:k + 1], bias=modT[:, k:k + 1]),
                                    reads=[b_pt, b_Aq[q]], writes=[b_hT[tb]])

                p1b_done = {"q": 0}

                def ada_job_load(i, t, b):
                    ada_load(ada_order[i], t, b)

                def ada_job_compute(i, t, b):
                    q = i // 2
                    ada_compute_into(b_mod, with_p1a=True)(ada_order[i], t, b)
                    if i % 2 == 1:
                        sl = slice(4 * q, 4 * q + 4)
                        S_.op("dve", lambda e: e.scalar_tensor_tensor(out=Acoef[:, sl], in0=modT[:, DC + 4 * q:DC + 4 * q + 4],
                                                                      scalar=1.0, in1=ng_s[:, sl], op0=ALU.add, op1=ALU.mult),
                              reads=[b_mod, b_p0], writes=[b_Aq[q]])
                    last = (i == 2 * ND - 1)
                    if last:
                        while p1a_state["tt"] < NT:
                            p1a(p1a_state["tt"])
                            p1a_state["tt"] += 1
                        preload(v_load)
                    while p1b_done["q"] < ND and (2 * p1b_done["q"] + 3 <= i or last):
                        p1b_quarter(p1b_done["q"])
                        p1b_done["q"] += 1

                stream(2 * ND, ada_job_load, ada_job_compute)
            S_.barrier()

            with ExitStack() as ph:
                XY = sb(ph, "XY", [128, NT, D], BF16)
                b_xy = [Buf("xy%d" % i) for i in range(NT)]
                gsum = sb(ph, "gsum", [128, NT * ND], F32)
                gsq = sb(ph, "gsq", [128, NT], F32)
                gsq2 = sb(ph, "gsq2", [128, NT * ND], F32)
                mean = sb(ph, "mean", [128, NT], F32)
                var = sb(ph, "var", [128, NT], F32)
                rs2 = sb(ph, "rs2", [128, NT], F32)
                nb2 = sb(ph, "nb2", [128, NT], F32)
                gu = [sb(ph, "gu%d" % i, [128, 512], F32) for i in range(2)]
                sz = [sb(ph, "sz%d" % i, [128, 512], F32) for i in range(2)]
                svb = [sb(ph, "svb%d" % i, [128, 512], F32) for i in range(2)]
                yst = [sb(ph, "yst%d" % i, [128, S], BF16) for i in range(2)]
                b_gu = [Buf() for _ in range(2)]
                b_sz = [Buf() for _ in range(2)]
                b_svb = [Buf() for _ in range(2)]
                b_yst = [Buf() for _ in range(2)]
                b_st = Buf("stats")

                def v_compute(i, t, b):
                    sv_ = slab_view(t)
                    for tt in range(NT):
                        p_, bp = bank()
                        S_.group("pe", [lambda e, k=k, tt=tt, p_=p_: e.matmul(
                            p_[:], lhsT=hT[:, k, tt * 128:(tt + 1) * 128], rhs=sv_[:, k, :],
                            start=(k == 0), stop=(k == DC - 1)) for k in range(DC)],
                            reads=[b, b_hT[tt // 4]], writes=[bp])
                        S_.op("act", lambda e, tt=tt, p_=p_, i=i: e.activation(
                            out=XY[:, tt, i * 512:(i + 1) * 512], in_=p_[:], func=AF.Gelu_apprx_tanh,
                            accum_out=gsum[:, tt * ND + i:tt * ND + i + 1]),
                            reads=[bp], writes=[b_xy[tt], b_st])
                        S_.op("act", lambda e, tt=tt, i=i: e.activation(
                            out=junk[:, 0:512], in_=XY[:, tt, i * 512:(i + 1) * 512], func=AF.Square,
                            accum_out=gsq2[:, tt * ND + i:tt * ND + i + 1]),
                            reads=[b_xy[tt]], writes=[b_st])

                stream(ND, v_load, v_compute)
                preload(a_load, 1)
                S_.op("dve", lambda e: e.tensor_reduce(out=gsq[:], in_=gsq2[:].rearrange("p (t c) -> p t c", c=ND),
                                                       axis=AX.X, op=ALU.add), reads=[b_st], writes=[b_st])
                S_.op("dve", lambda e: e.tensor_reduce(out=mean[:], in_=gsum[:].rearrange("p (t c) -> p t c", c=ND),
                                                       axis=AX.X, op=ALU.add), reads=[b_st], writes=[b_st])
                S_.op("dve", lambda e: e.tensor_scalar(out=mean[:], in0=mean[:], scalar1=1.0 / D, scalar2=None, op0=ALU.mult),
                      reads=[b_st], writes=[b_st])
                S_.op("dve", lambda e: e.tensor_tensor(out=var[:], in0=mean[:], in1=mean[:], op=ALU.mult),
                      reads=[b_st], writes=[b_st])
                S_.op("dve", lambda e: e.scalar_tensor_tensor(out=var[:], in0=gsq[:], scalar=1.0 / D, in1=var[:],
                                                              op0=ALU.mult, op1=ALU.subtract), reads=[b_st], writes=[b_st])
                S_.op("dve", lambda e: e.tensor_scalar(out=var[:], in0=var[:], scalar1=EPS, scalar2=None, op0=ALU.add),
                      reads=[b_st], writes=[b_st])
                S_.op("pool", lambda e: e.tensor_tensor(out=rs2[:], in0=var[:], in1=mhalf[:, 0:NT], op=ALU.pow),
                      reads=[b_st, b_small], writes=[b_st])
                S_.op("dve", lambda e: e.scalar_tensor_tensor(out=nb2[:], in0=mean[:], scalar=-1.0, in1=rs2[:],
                                                              op0=ALU.mult, op1=ALU.mult), reads=[b_st], writes=[b_st])
                for tt in range(NT):
                    if tt % 2 == 0:
                        S_.op("act", lambda e, tt=tt: e.activation(out=XY[:, tt, :], in_=XY[:, tt, :], func=AF.Identity,
                                                                   scale=rs2[:, tt:tt + 1], bias=nb2[:, tt:tt + 1]),
                              reads=[b_st], writes=[b_xy[tt]])
                    else:
                        S_.op("dve", lambda e, tt=tt: e.tensor_scalar(out=XY[:, tt, :], in0=XY[:, tt, :], scalar1=rs2[:, tt:tt + 1],
                                                                      scalar2=nb2[:, tt:tt + 1], op0=ALU.mult, op1=ALU.add),
                              reads=[b_st], writes=[b_xy[tt]])

                NP = D // 256
                cnt = {"t": 0, "y": 0}

                def a_compute(i, t, b):
                    sv_ = slab_view(t)
                    for ci in range(2):
                        c = 2 * i + ci
                        gi = c // 2
                        yi = cnt["y"] % 2
                        cnt["y"] += 1
                        for hf in range(0, NB, 2):
                            tbs = list(range(hf, min(hf + 2, NB)))
                            banks = {}
                            for kind in ("u", "z"):
                                off = ci * 128 + (256 if kind == "z" else 0)
                                for tb in tbs:
                                    p_, bp = bank()
                                    banks[(kind, tb)] = (p_, bp)
                                    S_.group("pe", [lambda e, k=k, tb=tb, p_=p_, off=off: e.matmul(
                                        p_[:], lhsT=sv_[:, k, off:off + 128], rhs=hT[:, k, tb * 512:(tb + 1) * 512],
                                        start=(k == 0), stop=(k == DC - 1)) for k in range(DC)],
                                        reads=[b, b_hT[tb]], writes=[bp])
                            for tb in tbs:
                                p_, bp = bank()
                                banks[("s", tb)] = (p_, bp)
                                S_.group("pe", [lambda e, n=n, tb=tb, p_=p_: e.matmul(
                                    p_[:, n * 128:(n + 1) * 128], lhsT=XY[:, tb * 4 + n, c * 128:(c + 1) * 128],
                                    rhs=wsT[:, gi, :], start=True, stop=True) for n in range(4)],
                                    reads=[b_xy[tb * 4 + n] for n in range(4)] + [b_ws], writes=[bp])
                            slots = {}
                            for tb in tbs:
                                slots[tb] = cnt["t"] % 2
                                cnt["t"] += 1
                            for tb in tbs:
                                p_, bp = banks[("u", tb)]
                                s_ = slots[tb]
                                S_.op("act", lambda e, p_=p_, s_=s_: e.activation(out=gu[s_][:], in_=p_[:], func=AF.Gelu_apprx_tanh),
                                      reads=[bp], writes=[b_gu[s_]])
                            for tb in tbs:
                                p_, bp = banks[("z", tb)]
                                s_ = slots[tb]
                                S_.op("act", lambda e, p_=p_, s_=s_: e.activation(out=sz[s_][:], in_=p_[:], func=AF.Silu),
                                      reads=[bp], writes=[b_sz[s_]])
                            for tb in tbs:
                                p_, bp = banks[("s", tb)]
                                s_ = slots[tb]
                                S_.op("dve", lambda e, p_=p_, s_=s_: e.scalar_tensor_tensor(
                                    out=svb[s_][:].rearrange("p (n t) -> p n t", n=4),
                                    in0=p_[:].rearrange("p (n t) -> p n t", n=4), scalar=lng_s[:, c:c + 1],
                                    in1=C2T[:, c:c + 1, :].to_broadcast([128, 4, 128]),
                                    op0=ALU.mult, op1=ALU.add), reads=[bp, b_ws, b_c], writes=[b_svb[s_]])
                                S_.op("dve", lambda e, s_=s_: e.tensor_tensor(out=gu[s_][:], in0=gu[s_][:], in1=sz[s_][:], op=ALU.mult),
                                      reads=[b_sz[s_]], writes=[b_gu[s_]])
                                S_.op("dve", lambda e, s_=s_, tb=tb: e.tensor_tensor(
                                    out=yst[yi][:, tb * 512:(tb + 1) * 512], in0=gu[s_][:], in1=svb[s_][:], op=ALU.mult),
                                    reads=[b_gu[s_], b_svb[s_]], writes=[b_yst[yi]])
                        S_.dma("sp", yaT[c * 128:(c + 1) * 128, :], yst[yi][:], reads=[b_yst[yi]], owner=b_yst[yi])

                stream(NP, a_load, a_compute)
            preload(g_load, 1)
            S_.barrier()
            st_gb = ExitStack()
            ring.append(sb(st_gb, "ring2", [128, 8192], BF16))
            b_ring.append(Buf("ring2"))
            rstate["i"] = (b_ring.index(pre["slots"][0][1]) + 1) % 3

            if True:
                gst = [sb(st_gb, "gst%d" % i, [128, S], BF16) for i in range(2)]
                b_gst = [Buf() for _ in range(2)]
                cnt = {"y": 0}

                ada_gate = ada_compute_into(b_mod2)

                def g_compute(i, t, b):
                    kind, i = gjobs[i]
                    if kind == "ada":
                        return ada_gate(i, t, b)
                    f_, cb = divmod(i, ND)
                    sv_ = slab_view(t)
                    for jj in range(4):
                        n_ = cb * 4 + jj
                        yi = cnt["y"] % 2
                        cnt["y"] += 1
                        for tb in range(NB):
                            p_, bp = bank()
                            S_.group("pe", [lambda e, k=k, tb=tb, p_=p_, jj=jj: e.matmul(
                                p_[:], lhsT=sv_[:, k, jj * 128:(jj + 1) * 128], rhs=hT[:, k, tb * 512:(tb + 1) * 512],
                                start=(k == 0), stop=(k == DC - 1)) for k in range(DC)],
                                reads=[b, b_hT[tb]], writes=[bp])
                            S_.op("act", lambda e, p_=p_, tb=tb, yi=yi: e.activation(
                                out=gst[yi][:, tb * 512:(tb + 1) * 512], in_=p_[:], func=AF.Sigmoid),
                                reads=[bp], writes=[b_gst[yi]])
                        S_.dma("sp", sgT[f_][n_ * 128:(n_ + 1) * 128, :], gst[yi][:], reads=[b_gst[yi]], owner=b_gst[yi])

                stream(len(gjobs), g_load, g_compute)

            with ExitStack() as ph:
                qT = sb(ph, "qT", [128, 2, S], BF16)
                kT = sb(ph, "kT", [128, 2, S], BF16)
                vau = sb(ph, "vau", [128, NT, 258], BF16)
                szT = sb(ph, "szT", [128, 2, S], BF16)
                E = [sb(ph, "E%d" % i, [128, 512], BF16) for i in range(5)]
                o0 = sb(ph, "o0", [128, 4, 256], F32)
                od = sb(ph, "od", [128, 4, 256], F32)
                ybn = sb(ph, "ybn", [128, 4, 256], BF16)
                jf = sb(ph, "jf", [128, 256], F32)
                rinv = sb(ph, "rinv", [128, 8], F32)
                ssq = sb(ph, "ssq", [128, 4], F32)
                rsb = sb(ph, "rsb", [128, 4], F32)
                ybst = [sb(ph, "ybst%d" % i, [128, S], BF16) for i in range(2)]
                b_q = [Buf() for _ in range(NB)]
                b_k = [Buf() for _ in range(NB)]
                b_v = [Buf() for _ in range(NT)]
                b_szT = [Buf() for _ in range(NB)]
                b_E = [Buf() for _ in range(5)]
                b_o0 = [Buf() for _ in range(4)]; b_od = [Buf() for _ in range(4)]; b_ybn = Buf(); b_ri = Buf(); b_sq = Buf()
                b_ybst = [Buf() for _ in range(2)]
                b_one = Buf()
                S_.op("dve", lambda e: e.memset(vau[:, :, 256:258], 1.0), writes=[b_one])
                ecnt = {"e": 0, "s": 0}
                qscale = 1.0 / math.sqrt(128.0)

                def proj_fm(sv_, b, off, tb):
                    p_, bp = bank()
                    S_.group("pe", [lambda e, k=k, p_=p_: e.matmul(
                        p_[:], lhsT=sv_[:, k, off:off + 128], rhs=hT[:, k, tb * 512:(tb + 1) * 512],
                        start=(k == 0), stop=(k == DC - 1)) for k in range(DC)],
                        reads=[b, b_hT[tb]], writes=[bp])
                    return p_, bp

                LOOK = 3
                NE = 5

                def attention(h):
                    tiles = [(tb, c, j) for tb in range(NB) for c in range(2) for j in range(4 * tb + 4)]
                    n = len(tiles)
                    info = {}
                    pending = []

                    def sbank():
                        i_ = 4 + ecnt["s"] % 4
                        ecnt["s"] += 1
                        return ps[i_], b_ps[i_]

                    def emit_S(idx):
                        tb, c, j = tiles[idx]
                        r0 = max(0, j - 4 * tb)
                        c0 = r0 * 128
                        off = LOFF - 128 * (j - 4 * tb)
                        p_, bp = sbank()
                        diag = j >= 4 * tb
                        use_aug = SLOPES[h] > 1.0 / 16.0 + 1e-9
                        fns = [lambda e: e.matmul(
                            p_[:, c0:512], lhsT=kT[:, c, j * 128:(j + 1) * 128],
                            rhs=qT[:, c, tb * 512 + c0:(tb + 1) * 512], start=True, stop=not (diag or use_aug))]
                        if diag:
                            fns.append(lambda e: e.matmul(
                                p_[:, c0:c0 + 128], lhsT=ident_b[:], rhs=maskT_b[:], start=False, stop=not use_aug))
                        if use_aug:
                            fns.append(lambda e: e.matmul(
                                p_[:, c0:512], lhsT=kaug[:, h * 128:(h + 1) * 128],
                                rhs=laug[:, off + c0:off + 512], start=False, stop=True))
                        S_.group("pe", fns, reads=[b_k[j // 4], b_q[tb], b_cp], writes=[bp])
                        ei = ecnt["e"] % NE
                        ecnt["e"] += 1
                        if use_aug:
                            S_.op("act", lambda e: e.activation(out=E[ei][:, c0:512], in_=p_[:, c0:512], func=AF.Exp),
                                  reads=[bp], writes=[b_E[ei]])
                        else:
                            bc = h * 16 + (j - 4 * tb) + 12
                            S_.op("act", lambda e: e.activation(out=E[ei][:, c0:512], in_=p_[:, c0:512], func=AF.Exp,
                                                                bias=btab[:, bc:bc + 1], scale=1.0),
                                  reads=[bp, b_c], writes=[b_E[ei]])
                        info[idx] = (ei, r0)

                    def emit_PV(idx):
                        tb, c, j = tiles[idx]
                        ei, r0 = info.pop(idx)
                        if j == 0:
                            for r in range(r0, 4):
                                S_.group("pe", [lambda e, r=r: e.matmul(
                                    ps[r][:, 0:257], lhsT=E[ei][:, r * 128:(r + 1) * 128], rhs=vau[:, j, 0:257],
                                    start=True, stop=(j == 4 * tb + r))],
                                    reads=[b_E[ei], b_v[j], b_one], writes=[b_ps[r]])
                        else:
                            S_.group("pe", [lambda e, r=r: e.matmul(
                                ps[r][:, 0:257], lhsT=E[ei][:, r * 128:(r + 1) * 128], rhs=vau[:, j, 0:257],
                                start=(j == 0), stop=(j == 4 * tb + r)) for r in range(r0, 4)],
                                reads=[b_E[ei], b_v[j], b_one], writes=[b_ps[r] for r in range(r0, 4)])
                        if j == 4 * tb + 3:
                            evac(idx, tb, c)

                    def evac(idx, tb, c):
                        bpo = [b_ps[r] for r in range(4)]
                        cs = slice(c * 4, c * 4 + 4)
                        S_.op("dve", lambda e: e.reciprocal(out=rinv[:, cs], in_=psall[:, 0:4, 256]),
                              reads=bpo, writes=[b_ri])
                        if c == 1:
                            S_.op("dve", lambda e: e.tensor_scalar(out=rinv[:, cs], in0=rinv[:, cs], scalar1=neglam[:, 0:1],
                                                                   scalar2=None, op0=ALU.mult), reads=[b_ri, b_l], writes=[b_ri])
                        for r in range(4):
                            ci = c * 4 + r
                            if c == 0:
                                S_.op("dve", lambda e, r=r, ci=ci: e.tensor_scalar(
                                    out=o0[:, r, :], in0=ps[r][:, 0:256], scalar1=rinv[:, ci:ci + 1], scalar2=None, op0=ALU.mult),
                                    reads=[b_ps[r], b_ri], writes=[b_o0[r]])
                            else:
                                S_.op("dve", lambda e, r=r, ci=ci: e.scalar_tensor_tensor(
                                    out=od[:, r, :], in0=ps[r][:, 0:256], scalar=rinv[:, ci:ci + 1], in1=o0[:, r, :],
                                    op0=ALU.mult, op1=ALU.add), reads=[b_ps[r], b_ri, b_o0[r]], writes=[b_od[r]])
                        if c == 0:
                            return
                        for r in range(4):
                            S_.op("dve", lambda e, r=r: e.scalar_tensor_tensor(
                                out=jf[:], in0=od[:, r, :], scalar=1.0, in1=od[:, r, :],
                                op0=ALU.mult, op1=ALU.mult, accum_out=ssq[:, r:r + 1]), reads=[b_od[r]], writes=[b_sq])
                        S_.op("dve", lambda e: e.tensor_scalar(out=ssq[:], in0=ssq[:], scalar1=1.0 / 256.0, scalar2=SUBLN_EPS,
                                                               op0=ALU.mult, op1=ALU.add), reads=[b_sq], writes=[b_sq])
                        S_.op("pool", lambda e: e.tensor_tensor(out=rsb[:], in0=ssq[:], in1=mhalf[:, 0:4], op=ALU.pow),
                              reads=[b_sq, b_small], writes=[b_sq])
                        for r in range(4):
                            S_.op("dve", lambda e, r=r: e.scalar_tensor_tensor(
                                out=ybn[:, r, :], in0=od[:, r, :], scalar=rsb[:, r:r + 1], in1=G2[:],
                                op0=ALU.mult, op1=ALU.mult), reads=[b_od[r], b_sq, b_l], writes=[b_ybn])

                        def transposes(tb=tb):
                            for e2 in range(2):
                                pt, b_pt = sbank()
                                ptb = pt.bitcast(BF16)
                                S_.group("pe", [lambda e, r=r: e.transpose(
                                    ptb[:, r * 128:(r + 1) * 128], ybn[:, r, e2 * 128:(e2 + 1) * 128], ident_b[:]) for r in range(4)],
                                    reads=[b_ybn, b_cp], writes=[b_pt])
                                S_.op("dve", lambda e: e.tensor_tensor(
                                    out=ybst[e2][:, tb * 512:(tb + 1) * 512], in0=ptb[:, 0:512], in1=szT[:, e2, tb * 512:(tb + 1) * 512],
                                    op=ALU.mult), reads=[b_pt, b_szT[tb]], writes=[b_ybst[e2]])
                        pending.append((idx + LOOK + 10, transposes))

                    for step in range(n + LOOK):
                        if step == n // 2 and h == H - 1:
                            rstate["i"] = 0
                            preload(m_load, 1)
                        if step < n:
                            emit_S(step)
                        if step - LOOK >= 0:
                            emit_PV(step - LOOK)
                        while pending and pending[0][0] <= step:
                            pending.pop(0)[1]()
                    pstate["i"] = 0

                    def tail():
                        while pending:
                            pending.pop(0)[1]()
                        for e2 in range(2):
                            r_ = h * 256 + e2 * 128
                            S_.dma("sp", ybT[r_:r_ + 128, :], ybst[e2][:], reads=[b_ybst[e2]], owner=b_ybst[e2])
                    carry.append(tail)

                carry = []

                def b_compute(i, t, b):
                    h, which = divmod(i, 2)
                    sv_ = slab_view(t)
                    if which == 0:
                        for c in range(2):
                            for tb in range(NB):
                                p_, bp = proj_fm(sv_, b, c * 128, tb)
                                S_.op("dve", lambda e, p_=p_, c=c, tb=tb: e.tensor_scalar(
                                    out=qT[:, c, tb * 512:(tb + 1) * 512], in0=p_[:], scalar1=qscale, scalar2=None, op0=ALU.mult),
                                    reads=[bp], writes=[b_q[tb]])
                            if c == 0:
                                while carry:
                                    carry.pop(0)()
                        for c in range(2):
                            for tb in range(NB):
                                p_, bp = proj_fm(sv_, b, 256 + c * 128, tb)
                                S_.op("dve", lambda e, p_=p_, c=c, tb=tb: e.tensor_copy(
                                    out=kT[:, c, tb * 512:(tb + 1) * 512], in_=p_[:]), reads=[bp], writes=[b_k[tb]])
                    else:
                        for tt in range(NT):
                            p_, bp = bank()
                            S_.group("pe", [lambda e, k=k, p_=p_, tt=tt: e.matmul(
                                p_[:, 0:256], lhsT=hT[:, k, tt * 128:(tt + 1) * 128], rhs=sv_[:, k, 0:256],
                                start=(k == 0), stop=(k == DC - 1)) for k in range(DC)],
                                reads=[b, b_hT[tt // 4]], writes=[bp])
                            S_.op("dve", lambda e, p_=p_, tt=tt: e.tensor_copy(out=vau[:, tt, 0:256], in_=p_[:, 0:256]),
                                  reads=[bp], writes=[b_v[tt]])
                        for e2 in range(2):
                            for tb in range(NB):
                                p_, bp = proj_fm(sv_, b, 256 + e2 * 128, tb)
                                S_.op("act", lambda e, p_=p_, e2=e2, tb=tb: e.activation(
                                    out=szT[:, e2, tb * 512:(tb + 1) * 512], in_=p_[:], func=AF.Silu),
                                    reads=[bp], writes=[b_szT[tb]])
                        attention(h)

                stream(2 * H, b_load, b_compute)
                while carry:
                    carry.pop(0)()
            ring.pop()
            b_ring.pop()
            rstate["i"] = 1
            st_gb.close()
        S_.barrier()

        with ExitStack() as ph:
            ya_s = sb(ph, "ya_s", [128, DC, S], BF16)
            yb_s = sb(ph, "yb_s", [128, DC, S], BF16)
            b_ya = [Buf() for _ in range(NB)]
            b_yb = [Buf() for _ in range(NB)]
            sg_s = [[sb(ph, "sg%d_%d" % (f_, i), [128, S], BF16) for i in range(2)] for f_ in range(2)]
            b_sg = [[Buf() for _ in range(2)] for _ in range(2)]
            t1 = [sb(ph, "t1_%d" % i, [128, 512], F32) for i in range(2)]
            t2 = [sb(ph, "t2_%d" % i, [128, 512], F32) for i in range(2)]
            b_t1 = [Buf() for _ in range(2)]
            b_t2 = [Buf() for _ in range(2)]
            mst = [sb(ph, "mst%d" % i, [128, S], BF16) for i in range(2)]
            b_mst = [Buf() for _ in range(2)]
            yaTv = yaT.rearrange("(k p) t -> p k t", p=128)
            ybTv = ybT.rearrange("(k p) t -> p k t", p=128)
            b_ch = [Buf("chain%d" % i) for i in range(3)]
            for tb in range(NB):
                S_.dma("sp", ya_s[:, :, tb * 512:(tb + 1) * 512], yaTv[:, :, tb * 512:(tb + 1) * 512],
                       writes=[b_ya[tb], b_ch[0]], owner=b_ya[tb])
                S_.dma("act", yb_s[:, :, tb * 512:(tb + 1) * 512], ybTv[:, :, tb * 512:(tb + 1) * 512],
                       writes=[b_yb[tb], b_ch[1]], owner=b_yb[tb])
            cnt = {"y": 0, "t": 0}

            def m_compute(i, t, b):
                sv_ = slab_view(t)
                for jj in range(2):
                    n_ = i * 2 + jj
                    for f2 in range(2):
                        S_.dma("sp", sg_s[f2][jj][:], sgT[f2][n_ * 128:(n_ + 1) * 128, :], writes=[b_sg[f2][jj]])
                for tb in range(NB):
                    for jj in range(2):
                        pa, bpa = bank()
                        S_.group("pe", [lambda e, k=k: e.matmul(
                            pa[:], lhsT=sv_[:, k, jj * 128:(jj + 1) * 128], rhs=ya_s[:, k, tb * 512:(tb + 1) * 512],
                            start=(k == 0), stop=(k == DC - 1)) for k in range(DC)],
                            reads=[b, b_ya[tb]], writes=[bpa])
                        pb, bpb = bank()
                        S_.group("pe", [lambda e, k=k: e.matmul(
                            pb[:], lhsT=sv_[:, k, 256 + jj * 128:256 + (jj + 1) * 128], rhs=yb_s[:, k, tb * 512:(tb + 1) * 512],
                            start=(k == 0), stop=(k == DC - 1)) for k in range(DC)],
                            reads=[b, b_yb[tb]], writes=[bpb])
                        ti = cnt["t"] % 2
                        cnt["t"] += 1
                        S_.op("dve", lambda e: e.tensor_tensor(
                            out=t1[ti][:], in0=pa[:], in1=sg_s[0][jj][:, tb * 512:(tb + 1) * 512], op=ALU.mult),
                            reads=[bpa, b_sg[0][jj]], writes=[b_t1[ti]])
                        S_.op("dve", lambda e: e.tensor_tensor(
                            out=t2[ti][:], in0=pb[:], in1=sg_s[1][jj][:, tb * 512:(tb + 1) * 512], op=ALU.mult),
                            reads=[bpb, b_sg[1][jj]], writes=[b_t2[ti]])
                        S_.op("pool", lambda e: e.tensor_tensor(
                            out=mst[jj][:, tb * 512:(tb + 1) * 512], in0=t1[ti][:], in1=t2[ti][:], op=ALU.add),
                            reads=[b_t1[ti], b_t2[ti]], writes=[b_mst[jj]])
                for jj in range(2):
                    n_ = i * 2 + jj
                    S_.dma("sp", mT[n_ * 128:(n_ + 1) * 128, :], mst[jj][:], reads=[b_mst[jj]], owner=b_mst[jj])

            stream(D // 256, m_load, m_compute)
        S_.barrier()

        with ExitStack() as ph:
            m_s = sb(ph, "m_s", [128, DC, S], BF16)
            wo_s = sb(ph, "wo_s", [128, DC, D], BF16)
            b_m = [Buf() for _ in range(NB)]
            b_wo = [Buf() for _ in range(ND)]
            gate_bc = sb(ph, "gate_bc", [128, D], F32)
            fng_bc = sb(ph, "fng_bc", [128, D], F32)
            xo = [sb(ph, "xo%d" % i, [128, D], F32)[:] for i in range(2)]
            for rt in ring:
                rf = rt[:].bitcast(F32)
                for q_ in range(min(2, 4096 // D)):
                    xo.append(rf[:, q_ * D:(q_ + 1) * D])
            NXO = len(xo)
            b_xo = [Buf() for _ in range(NXO)]
            tm = [sb(ph, "tm%d" % i, [128, 512], F32) for i in range(2)]
            b_tm = [Buf() for _ in range(2)]
            dg = sb(ph, "dg", [128, 128], F32)
            ones2 = sb(ph, "ones2", [128, 128], F32)
            s2 = sb(ph, "s2", [128, NT], F32)
            r2 = sb(ph, "r2", [128, NT], F32)
            b_dg = Buf(); b_gb = Buf(); b_s2 = Buf(); b_fg = Buf()
            mTv = mT.rearrange("(k p) t -> p k t", p=128)
            b_ch2 = [Buf("ochain%d" % i) for i in range(2)]
            for cb in range(ND):
                S_.dma("pool", wo_s[:, :, cb * 512:(cb + 1) * 512], wsrc(w_o, cb * 512, 512),
                       writes=[b_wo[cb], b_ch2[0]], owner=b_wo[cb])
            for tb in range(NB):
                S_.dma("act", m_s[:, :, tb * 512:(tb + 1) * 512], mTv[:, :, tb * 512:(tb + 1) * 512],
                       writes=[b_m[tb], b_ch2[1]], owner=b_m[tb])
            S_.dma("sp", fng_bc[:], fng.partition_broadcast(128), writes=[b_fg])
            S_.op("dve", lambda e: e.memset(ones2[:], 1.0), writes=[b_dg])
            for k in range(DC):
                S_.op("dve", lambda e, k=k: e.tensor_scalar(out=dg[:], in0=ident_f[:], scalar1=modT[:, 2 * DC + k:2 * DC + k + 1],
                                                            scalar2=None, op0=ALU.mult), reads=[b_c, b_mod2], writes=[b_dg])
                if k % 4 == 0:
                    pg, b_pg = bank()
                S_.group("pe", [lambda e, k=k, pg=pg: e.matmul(pg[:, (k % 4) * 128:(k % 4 + 1) * 128], lhsT=ones2[:], rhs=dg[:],
                                                              start=True, stop=True)], reads=[b_dg], writes=[b_pg])
                if k % 4 == 3:
                    kb = k // 4
                    S_.op("dve", lambda e, pg=pg, kb=kb: e.tensor_copy(out=gate_bc[:, kb * 512:(kb + 1) * 512], in_=pg[:]),
                          reads=[b_pg], writes=[b_gb])
            XA = min(3, NXO - 2)

            def load_xo(tt):
                S_.dma("sp", xo[tt % NXO][:], x[tt * 128:(tt + 1) * 128, :], writes=[b_xo[tt % NXO]])

            def epilogue(tt):
                xi = tt % NXO
                S_.op("act", lambda e: e.activation(out=junk[:], in_=xo[xi][:], func=AF.Square,
                                                    accum_out=s2[:, tt:tt + 1]), reads=[b_xo[xi]], writes=[b_s2])
                S_.op("dve", lambda e: e.tensor_scalar(out=s2[:, tt:tt + 1], in0=s2[:, tt:tt + 1], scalar1=1.0 / D, scalar2=EPS,
                                                       op0=ALU.mult, op1=ALU.add), reads=[b_s2], writes=[b_s2])
                S_.op("pool", lambda e: e.tensor_tensor(out=r2[:, tt:tt + 1], in0=s2[:, tt:tt + 1], in1=mhalf[:, 0:1], op=ALU.pow),
                      reads=[b_s2, b_small], writes=[b_s2])
                S_.op("dve", lambda e: e.scalar_tensor_tensor(
                    out=xo[xi][:], in0=xo[xi][:], scalar=r2[:, tt:tt + 1], in1=fng_bc[:], op0=ALU.mult, op1=ALU.mult),
                    reads=[b_s2, b_fg], writes=[b_xo[xi]])
                S_.dma("sp", y[tt * 128:(tt + 1) * 128, :], xo[xi][:], reads=[b_xo[xi]], owner=b_xo[xi])

            for tt in range(min(XA, NT)):
                load_xo(tt)
            for tt in range(NT):
                xi = tt % NXO
                if tt + XA < NT:
                    load_xo(tt + XA)
                for cb in range(ND):
                    p_, bp = bank()
                    S_.group("pe", [lambda e, k=k: e.matmul(
                        p_[:], lhsT=m_s[:, k, tt * 128:(tt + 1) * 128], rhs=wo_s[:, k, cb * 512:(cb + 1) * 512],
                        start=(k == 0), stop=(k == DC - 1)) for k in range(DC)],
                        reads=[b_m[tt // 4], b_wo[cb]], writes=[bp])
                    ti = (tt * ND + cb) % 2
                    S_.op("dve", lambda e: e.tensor_tensor(
                        out=tm[ti][:], in0=p_[:], in1=gate_bc[:, cb * 512:(cb + 1) * 512], op=ALU.mult),
                        reads=[bp, b_gb], writes=[b_tm[ti]])
                    S_.op("pool", lambda e: e.tensor_tensor(
                        out=xo[xi][:, cb * 512:(cb + 1) * 512], in0=xo[xi][:, cb * 512:(cb + 1) * 512], in1=tm[ti][:], op=ALU.add),
                        reads=[b_tm[ti]], writes=[b_xo[xi]])
                if tt >= 1:
                    epilogue(tt - 1)
            epilogue(NT - 1)
        S_.barrier(engines=("sp", "pe", "act", "dve", "pool"))
        build_program.stats = (S_.ninst, S_.nwait, len(S_.sems))
    return nc


def _consts(H):
    ident = np.eye(128, dtype=np.float32)
    tril = np.tril(np.ones((128, 128), dtype=np.float32))
    s_i = np.arange(128)[:, None]
    t_i = np.arange(128)[None, :]
    maskT = np.where(s_i <= t_i, 0.0, NEG).astype(np.float32)
    idx = np.arange(LLEN)
    N = LOFF - idx
    a = np.floor_divide(N, 256)
    b = N - 256 * a
    laug = np.zeros((128, LLEN), dtype=np.float32)
    laug[0] = 1.0
    laug[1] = 256.0 * a
    laug[2] = b
    slopes = [2.0 ** (-8.0 * (i + 1) / H) for i in range(H)]
    jv = np.arange(128, dtype=np.float64)
    kaug = np.zeros((128, H * 128), dtype=np.float32)
    for hh, sl in enumerate(slopes):
        kaug[0, hh * 128:(hh + 1) * 128] = sl * jv
        kaug[1, hh * 128:(hh + 1) * 128] = sl
        kaug[2, hh * 128:(hh + 1) * 128] = sl
    btab = np.zeros((128, H * 16), dtype=np.float32)
    for hh, sl in enumerate(slopes):
        for dl in range(-12, 4):
            btab[:, hh * 16 + dl + 12] = sl * (jv + 128.0 * dl)
    return ident, tril, maskT, laug, kaug, btab


def make_in_maps(inp, S, D):
    B = inp["x"].shape[0]
    DC = D // 128
    H = D // 256
    f = lambda a: np.ascontiguousarray(np.asarray(a, dtype=np.float32))
    colT = lambda v: f(np.asarray(v).reshape(-1, 128).T)
    ident, tril, maskT, laug, kaug, btab = _consts(H)
    shared = {
        "w_ada": f(inp["w_ada"][0]), "b_adaT": colT(inp["b_ada"][0]), "ngT": colT(inp["norm_gain"][0]),
        "w_in": f(inp["w_in"][0]), "lngT": colT(inp["ln_v_gain"][0]), "lnbT": colT(inp["ln_v_bias"][0]),
        "w_sp": f(inp["w_spatial"][0]), "b_sp": f(np.asarray(inp["b_spatial"][0]).reshape(1, -1)),
        "lamv": f(np.concatenate([np.asarray(inp[k][0]).reshape(-1) for k in
                                  ("lambda_q1", "lambda_k1", "lambda_q2", "lambda_k2")]).reshape(1, -1)),
        "sublng": f(np.asarray(inp["subln_gain"][0]).reshape(1, -1)),
        "w_a": f(inp["w_branch_a"][0]), "w_b": f(inp["w_branch_b"][0]), "w_o": f(inp["w_out"][0]),
        "fng": f(np.asarray(inp["final_norm_gain"]).reshape(1, -1)),
        "c_ident": ident, "c_tril": tril, "c_maskT": maskT, "c_laug": laug, "c_kaug": kaug, "c_btab": btab,
    }
    maps = []
    for b in range(B):
        m = dict(shared)
        m["x"] = f(inp["x"][b])
        m["cT"] = colT(inp["c"][b])
        maps.append(m)
    return maps


_CACHE = {}


def kernel(**inputs):
    x = np.asarray(inputs["x"])
    B, S, D = x.shape
    key = (S, D)
    if key not in _CACHE:
        _CACHE[key] = build_program(S, D)
    nc = _CACHE[key]
    in_maps = make_in_maps(inputs, S, D)
    res = run_bass_kernel_spmd(nc, in_maps, core_ids=list(range(B)))
    return np.stack([np.asarray(r["y"], dtype=np.float32) for r in res.results], axis=0)
```

```python
import math
from contextlib import ExitStack

import numpy as np
import concourse.bass as bass
import concourse.mybir as mybir
from concourse.bass_utils import run_bass_kernel_spmd

F32 = mybir.dt.float32
BF16 = mybir.dt.bfloat16
AF = mybir.ActivationFunctionType
ALU = mybir.AluOpType
AX = mybir.AxisListType

LAM_INIT = 0.8 - 0.6 * math.exp(-0.3 * 0)
EPS = 1e-6
SUBLN_EPS = 1e-5
NEG = -30000.0
LOFF = 384
LLEN = 2432


class Buf:
    _n = 0

    def __init__(self, name=""):
        Buf._n += 1
        self.id = Buf._n
        self.name = name
        self.w = None
        self.r = []
        self.dma_sem = None
        self.dma_cnt = 0


class Sch:
    def __init__(self, nc, stack):
        self.nc = nc
        self.stack = stack
        self.eng = {"pe": nc.tensor, "act": nc.scalar, "dve": nc.vector,
                    "pool": nc.gpsimd, "sp": nc.sync}
        self.sems = {}
        self.cnt = {}
        for e in ("pe", "act", "dve", "pool"):
            self.sems[e] = stack.enter_context(nc.semaphore("c_" + e))
            self.cnt[e] = 0
        self.seen = {e: {} for e in self.eng}
        self.owners = []
        self.ninst = 0
        self.nwait = 0

    def _deps(self, reads, writes):
        d = {}

        def add(ev):
            if ev is None:
                return
            k, v = ev
            if d.get(k, -1) < v:
                d[k] = v
        for b in reads:
            add(b.w)
        for b in writes:
            add(b.w)
            for ev in b.r:
                add(ev)
        return d

    def _wait(self, e, deps):
        eng = self.eng[e]
        seen = self.seen[e]
        for k, v in deps.items():
            if k == e and e == "pe":
                continue
            if seen.get(k, 0) >= v:
                continue
            eng.wait_ge(self.sems[k], v)
            seen[k] = v
            self.nwait += 1

    def _record(self, ev, reads, writes):
        for b in reads:
            b.r.append(ev)
            if len(b.r) > 64:
                best = {}
                for k, v in b.r:
                    if best.get(k, -1) < v:
                        best[k] = v
                b.r = list(best.items())
        for b in writes:
            b.w = ev
            b.r = []

    def op(self, e, fn, reads=(), writes=()):
        self._wait(e, self._deps(reads, writes))
        ins = fn(self.eng[e])
        self.cnt[e] += 1
        ins.then_inc(self.sems[e], 1)
        ev = (e, self.cnt[e])
        self._record(ev, reads, writes)
        self.ninst += 1
        return ev

    def group(self, e, fns, reads=(), writes=()):
        self._wait(e, self._deps(reads, writes))
        ins = None
        for fn in fns:
            ins = fn(self.eng[e])
            self.ninst += 1
        self.cnt[e] += 1
        ins.then_inc(self.sems[e], 1)
        ev = (e, self.cnt[e])
        self._record(ev, reads, writes)
        return ev

    def dma(self, q, out, in_, reads=(), writes=(), owner=None, **kw):
        if owner is None:
            owner = writes[0] if writes else reads[0]
        if owner.dma_sem is None:
            owner.dma_sem = self.stack.enter_context(self.nc.semaphore("d_%d" % owner.id))
            self.sems[("d", owner.id)] = owner.dma_sem
            self.owners.append(owner)
        self._wait(q, self._deps(reads, writes))
        ins = self.eng[q].dma_start(out=out, in_=in_, **kw)
        owner.dma_cnt += 1
        ins.then_inc(owner.dma_sem, 16)
        ev = (("d", owner.id), 16 * owner.dma_cnt)
        self._record(ev, reads, writes)
        self.ninst += 1
        return ev

    def barrier(self, engines=("pe", "act", "dve", "pool", "sp")):
        d = {}
        for e in ("pe", "act", "dve", "pool"):
            if self.cnt[e] > 0:
                d[e] = self.cnt[e]
        for o in self.owners:
            d[("d", o.id)] = 16 * o.dma_cnt
        for e in engines:
            eng = self.eng[e]
            seen = self.seen[e]
            for k, v in d.items():
                if k == e:
                    continue
                if seen.get(k, 0) >= v:
                    continue
                eng.wait_ge(self.sems[k], v)
                seen[k] = v
                self.nwait += 1


def build_program(S, D, debug=False):
    NT = S // 128
    NB = S // 512
    DC = D // 128
    ND = D // 512
    H = D // 256
    G = D // 256
    SLOPES = [2.0 ** (-8.0 * (i + 1) / H) for i in range(H)]
    assert S % 512 == 0 and D % 512 == 0 and NB <= 4
    nc = bass.Bass("TRN2", target_bir_lowering=False)

    def din(name, shape):
        return nc.dram_tensor(name, list(shape), F32, kind="ExternalInput").ap()

    x = din("x", [S, D])
    cT = din("cT", [128, DC])
    w_ada = din("w_ada", [D, 3 * D])
    b_adaT = din("b_adaT", [128, 3 * DC])
    ngT = din("ngT", [128, DC])
    w_in = din("w_in", [D, 9 * D])
    lngT = din("lngT", [128, DC])
    lnbT = din("lnbT", [128, DC])
    w_sp = din("w_sp", [G, 128, 128])
    b_sp = din("b_sp", [1, G * 128])
    lamv = din("lamv", [1, 4 * 128])
    sublng = din("sublng", [1, 256])
    w_a = din("w_a", [D, D])
    w_b = din("w_b", [D, D])
    w_o = din("w_o", [D, D])
    fng = din("fng", [1, D])
    c_ident = din("c_ident", [128, 128])
    c_tril = din("c_tril", [128, 128])
    c_maskT = din("c_maskT", [128, 128])
    c_laug = din("c_laug", [128, LLEN])
    c_kaug = din("c_kaug", [128, H * 128])
    c_btab = din("c_btab", [128, H * 16])
    y = nc.dram_tensor("y", [S, D], F32, kind="ExternalOutput").ap()
    skind = "ExternalOutput" if debug else "Internal"
    yaT = nc.dram_tensor("yaT", [D, S], BF16, kind=skind).ap()
    ybT = nc.dram_tensor("ybT", [D, S], BF16, kind=skind).ap()
    sgT = [nc.dram_tensor("sgT%d" % i, [D, S], BF16, kind=skind).ap() for i in range(2)]
    mT = nc.dram_tensor("mT", [D, S], BF16, kind=skind).ap()

    with ExitStack() as g:
        S_ = Sch(nc, g)
        sb = lambda st, name, shape, dt: st.enter_context(nc.sbuf_tensor(name, list(shape), dt))

        psall = g.enter_context(nc.psum_tensor("psall", [128, 8, 512], F32))
        ps = [psall[:, i, :] for i in range(8)]
        b_ps = [Buf("ps%d" % i) for i in range(8)]
        pstate = {"i": 0}

        def bank():
            i = pstate["i"]
            pstate["i"] = (i + 1) % 8
            return ps[i], b_ps[i]

        ring = [sb(g, "ring%d" % i, [128, 8192], BF16) for i in range(2)]
        b_ring = [Buf("ring%d" % i) for i in range(2)]
        rstate = {"i": 0}

        def ring_next():
            i = rstate["i"]
            rstate["i"] = (i + 1) % len(ring)
            return ring[i], b_ring[i]

        def slab_view(t):
            return t[:, 0:DC * 512].rearrange("p (k n) -> p k n", k=DC)

        def wsrc(w, c0, n):
            return w[:, c0:c0 + n].rearrange("(k p) n -> p k n", p=128)

        pre = {"slots": []}

        def preload(load, count=1):
            for i in range(count):
                sl = ring_next()
                load(i, *sl)
                pre["slots"].append(sl)

        def stream(n, load, compute):
            depth = len(ring) - 1
            slots = {}
            nxt = 0
            for sl in pre["slots"]:
                slots[nxt] = sl
                nxt += 1
            pre["slots"] = []
            for i in range(n):
                while nxt < n and nxt <= i + depth:
                    slots[nxt] = ring_next()
                    load(nxt, *slots[nxt])
                    nxt += 1
                compute(i, *slots[i])
                del slots[i]

        def ada_load(i, t, b):
            S_.dma("pool", slab_view(t), wsrc(w_ada, i * 512, 512), writes=[b])

        def v_load(i, t, b):
            S_.dma("pool", slab_view(t), wsrc(w_in, D + i * 512, 512), writes=[b])

        def a_load(i, t, b):
            sv_ = slab_view(t)
            S_.dma("pool", sv_[:, :, 0:256], wsrc(w_in, i * 256, 256), writes=[b])
            S_.dma("pool", sv_[:, :, 256:512], wsrc(w_in, 2 * D + i * 256, 256), writes=[b])

        gjobs = []
        for i in range(2 * ND):
            gjobs.append(("g", i))
            if i % 2 == 1:
                gjobs.append(("ada", 2 * ND + i // 2))

        def g_load(i, t, b):
            kind, i = gjobs[i]
            if kind == "ada":
                return ada_load(i, t, b)
            f_, cb = divmod(i, ND)
            S_.dma("pool", slab_view(t), wsrc(w_in, (7 + f_) * D + cb * 512, 512), writes=[b])

        def b_load(i, t, b):
            h, which = divmod(i, 2)
            sv_ = slab_view(t)
            f0 = 3 if which == 0 else 5
            S_.dma("pool", sv_[:, :, 0:256], wsrc(w_in, f0 * D + h * 256, 256), writes=[b])
            S_.dma("pool", sv_[:, :, 256:512], wsrc(w_in, (f0 + 1) * D + h * 256, 256), writes=[b])

        def m_load(i, t, b):
            sv_ = slab_view(t)
            S_.dma("pool", sv_[:, :, 0:256], wsrc(w_a, i * 256, 256), writes=[b])
            S_.dma("pool", sv_[:, :, 256:512], wsrc(w_b, i * 256, 256), writes=[b])

        b_c = Buf("consts")
        ident_f = sb(g, "ident_f", [128, 128], F32)
        junk = sb(g, "junk", [128, D], BF16)
        modT = sb(g, "modT", [128, 3 * DC], F32)
        Acoef = sb(g, "Acoef", [128, DC], F32)
        mhalf = sb(g, "mhalf", [128, 16], F32)
        b_small = Buf("small")
        S_.dma("sp", ident_f[:], c_ident, writes=[b_c])
        S_.op("dve", lambda e: e.memset(mhalf[:], -0.5), writes=[b_small])

        b_mod = Buf("mod")
        b_mod2 = Buf("mod_gate")
        b_A = Buf("Acoef")

        with ExitStack() as st_h:
            ident_b = sb(st_h, "ident_b", [128, 128], BF16)
            maskT_b = sb(st_h, "maskT_b", [128, 128], BF16)
            laug = sb(st_h, "laug", [128, LLEN], BF16)
            kaug = sb(st_h, "kaug", [128, H * 128], BF16)
            C2T = sb(st_h, "C2T", [128, DC, 128], F32)
            wsT = sb(st_h, "wsT", [128, G, 128], BF16)
            G2 = sb(st_h, "G2", [128, 256], F32)
            lng_s = sb(st_h, "lng_s", [128, DC], F32)
            lnb_s = sb(st_h, "lnb_s", [128, DC], F32)
            neglam = sb(st_h, "neglam", [128, 1], F32)
            btab = sb(st_h, "btab", [128, H * 16], F32)
            S_.dma("sp", btab[:], c_btab, writes=[b_c])
            b_cp = Buf("consts_pool")
            S_.dma("pool", ident_b[:], c_ident, writes=[b_cp])
            S_.dma("pool", maskT_b[:], c_maskT, writes=[b_cp])
            S_.dma("pool", laug[:], c_laug, writes=[b_cp])
            S_.dma("pool", kaug[:], c_kaug, writes=[b_cp])
            S_.dma("sp", lng_s[:], lngT, writes=[b_c])
            S_.dma("sp", lnb_s[:], lnbT, writes=[b_c])
            scb = sb(st_h, "scb", [128, DC], BF16)
            badT = sb(st_h, "badT", [128, 3 * DC], F32)
            hT = sb(st_h, "hT", [128, DC, S], BF16)
            b_hT = [Buf("hT%d" % i) for i in range(NB)]

            with ExitStack() as ph:
                cs = sb(ph, "cs", [128, DC], F32)
                ng_s = sb(ph, "ng_s", [128, DC], F32)
                lvec = sb(ph, "lvec", [128, 4 * 128], F32)
                lj = sb(ph, "lj", [128, 128], F32)
                ld = sb(ph, "ld", [128, 4], F32)
                wtmp = sb(ph, "wtmp", [128, 128], F32)
                wsTf = sb(ph, "wsTf", [128, 128], F32)
                tril = sb(ph, "tril", [128, 128], F32)
                ones_f = sb(ph, "ones_f", [128, 128], F32)
                bs_row = sb(ph, "bs_row", [1, 128], F32)
                b_bsr = Buf("bs_row")
                BSb = sb(ph, "BSb", [128, 128], F32)
                NXT = 2
                xt = [sb(ph, "xt%d" % i, [128, D], F32) for i in range(NXT)]
                xs_all = sb(ph, "xs_all", [128, NT, D], BF16)
                ss = sb(ph, "ss", [128, NT], F32)
                rstd = sb(ph, "rstd", [128, NT], F32)
                b_xt = [Buf("xt%d" % i) for i in range(NXT)]
                b_xs = [Buf("xs%d" % i) for i in range(NT)]
                b_p0 = Buf("p0c")
                b_sc = Buf("sc")
                b_ss = Buf("ss")

                S_.dma("sp", cs[:], cT, writes=[b_p0])
                S_.dma("sp", badT[:], b_adaT, writes=[b_p0])
                S_.dma("sp", ng_s[:], ngT, writes=[b_p0])
                S_.dma("sp", lvec[:], lamv.partition_broadcast(128), writes=[b_p0])
                S_.dma("sp", tril[:], c_tril, writes=[b_p0])
                S_.dma("sp", G2[:], sublng.partition_broadcast(128), writes=[b_p0])
                S_.op("act", lambda e: e.activation(out=scb[:], in_=cs[:], func=AF.Silu),
                      reads=[b_p0], writes=[b_sc])
                S_.op("dve", lambda e: e.memset(ones_f[:], 1.0), writes=[b_small])

                def load_xt(tt):
                    S_.dma("sp", xt[tt % NXT][:], x[tt * 128:(tt + 1) * 128, :], writes=[b_xt[tt % NXT]])
                for tt in range(min(NXT, NT)):
                    load_xt(tt)

                def p1a(tt):
                    xb, bx = xt[tt % NXT], b_xt[tt % NXT]
                    S_.op("act", lambda e: e.activation(out=junk[:], in_=xb[:], func=AF.Square, accum_out=ss[:, tt:tt + 1]),
                          reads=[bx], writes=[b_ss])
                    S_.op("dve", lambda e: e.tensor_scalar(out=ss[:, tt:tt + 1], in0=ss[:, tt:tt + 1], scalar1=1.0 / D, scalar2=EPS,
                                                           op0=ALU.mult, op1=ALU.add), reads=[b_ss], writes=[b_ss])
                    S_.op("pool", lambda e: e.tensor_tensor(out=rstd[:, tt:tt + 1], in0=ss[:, tt:tt + 1], in1=mhalf[:, 0:1], op=ALU.pow),
                          reads=[b_ss, b_small], writes=[b_ss])
                    S_.op("dve", lambda e: e.tensor_scalar(out=xs_all[:, tt, :], in0=xb[:], scalar1=rstd[:, tt:tt + 1],
                                                           scalar2=None, op0=ALU.mult), reads=[bx, b_ss], writes=[b_xs[tt]])
                    if tt + NXT < NT:
                        load_xt(tt + NXT)
                p1a_state = {"tt": 0}
                p1a_per_job = -(-NT // (2 * ND))

                def ada_compute_into(bmod, with_p1a=False):
                    def ada_compute(i, t, b):
                        sv_ = slab_view(t)
                        pm, b_pm = bank()
                        fns = []
                        for jj in range(4):
                            for k in range(DC):
                                fns.append(lambda e, jj=jj, k=k: e.matmul(
                                    pm[:, jj:jj + 1], lhsT=sv_[:, k, jj * 128:(jj + 1) * 128],
                                    rhs=scb[:, k:k + 1], start=(k == 0), stop=(k == DC - 1)))
                        S_.group("pe", fns, reads=[b, b_sc], writes=[b_pm])
                        S_.op("dve", lambda e: e.tensor_tensor(out=modT[:, i * 4:(i + 1) * 4], in0=pm[:, 0:4],
                                                               in1=badT[:, i * 4:(i + 1) * 4], op=ALU.add),
                              reads=[b_pm, b_p0], writes=[bmod])
                        if with_p1a:
                            for _ in range(p1a_per_job):
                                if p1a_state["tt"] < NT:
                                    p1a(p1a_state["tt"])
                                    p1a_state["tt"] += 1
                    return ada_compute

                b_l = Buf("lam")
                for i in range(2):
                    S_.op("dve", lambda e, i=i: e.scalar_tensor_tensor(
                        out=lj[:], in0=lvec[:, (2 * i) * 128:(2 * i + 1) * 128], scalar=1.0,
                        in1=lvec[:, (2 * i + 1) * 128:(2 * i + 2) * 128],
                        op0=ALU.mult, op1=ALU.mult, accum_out=ld[:, i:i + 1]), reads=[b_p0], writes=[b_l])
                S_.op("act", lambda e: e.activation(out=ld[:, 2:4], in_=ld[:, 0:2], func=AF.Exp),
                      reads=[b_l], writes=[b_l])
                S_.op("dve", lambda e: e.tensor_tensor(out=neglam[:], in0=ld[:, 3:4], in1=ld[:, 2:3], op=ALU.subtract),
                      reads=[b_l], writes=[b_l])
                S_.op("dve", lambda e: e.tensor_scalar(out=neglam[:], in0=neglam[:], scalar1=-LAM_INIT, scalar2=None,
                                                       op0=ALU.add), reads=[b_l], writes=[b_l])
                S_.op("dve", lambda e: e.tensor_scalar(out=G2[:], in0=G2[:], scalar1=1.0 - LAM_INIT, scalar2=None,
                                                       op0=ALU.mult), reads=[b_p0, b_l], writes=[b_l])

                b_w = Buf("wsp")
                b_ws = Buf("wsT")
                for gi in range(G):
                    S_.dma("sp", wtmp[:], w_sp[gi], reads=[], writes=[b_w])
                    S_.op("dve", lambda e: e.tensor_tensor(out=wtmp[:], in0=wtmp[:], in1=tril[:], op=ALU.mult),
                          reads=[b_p0], writes=[b_w])
                    pt, b_pt = bank()
                    S_.group("pe", [lambda e, pt=pt: e.transpose(pt[:, 0:128], wtmp[:], ident_f[:])],
                             reads=[b_w, b_c], writes=[b_pt])
                    S_.op("dve", lambda e, pt=pt: e.tensor_copy(out=wsTf[:], in_=pt[:, 0:128]),
                          reads=[b_pt], writes=[b_ws])
                    S_.op("act", lambda e, pt=pt, gi=gi: e.activation(out=wsT[:, gi, :], in_=pt[:, 0:128], func=AF.Copy),
                          reads=[b_pt], writes=[b_ws])
                    pr, b_pr = bank()
                    S_.dma("sp", bs_row[:], b_sp[0:1, gi * 128:(gi + 1) * 128], writes=[b_bsr])
                    S_.group("pe", [lambda e, pr=pr: e.matmul(pr[:, 0:128], lhsT=ones_f[:], rhs=wsTf[:], start=True, stop=True),
                                    lambda e, pr=pr, gi=gi: e.matmul(pr[:, 128:256], lhsT=ones_f[0:1, :],
                                                                    rhs=bs_row[0:1, :],
                                                                    start=True, stop=True)],
                             reads=[b_ws, b_small, b_p0, b_bsr], writes=[b_pr])
                    S_.op("dve", lambda e, pr=pr: e.tensor_copy(out=BSb[:], in_=pr[:, 128:256]),
                          reads=[b_pr], writes=[b_w])
                    for ci in range(2):
                        c = gi * 2 + ci
                        S_.op("dve", lambda e, pr=pr, c=c: e.scalar_tensor_tensor(
                            out=C2T[:, c, :], in0=pr[:, 0:128], scalar=lnb_s[:, c:c + 1], in1=BSb[:],
                            op0=ALU.mult, op1=ALU.add), reads=[b_pr, b_w, b_c], writes=[b_ws])

                scf = sb(ph, "scf", [128, DC], F32)
                b_scf = Buf("scf")
                S_.op("act", lambda e: e.activation(out=scf[:], in_=cs[:], func=AF.Silu), reads=[b_p0], writes=[b_scf])
                NH = 4 * ND
                b_ringh = [Buf("ringh%d" % i) for i in range(2)]
                b_Aq = [Buf("A%d" % q) for q in range(ND)]
                psb = psall[:].bitcast(BF16)
                p1a_per_job = -(-NT // 4)

                def ada_h_base(i):
                    q, r = divmod(i, 4)
                    return (0 if r < 2 else D) + q * 512 + (r % 2) * 256

                def f32_view(t):
                    return t[:].bitcast(F32)[:, 0:DC * 256].rearrange("p (k n) -> p k n", k=DC)

                def ada_h_load(i, t, b):
                    base = ada_h_base(i)
                    S_.dma("sp" if i % 2 == 0 else "act", f32_view(t),
                           w_ada[:, base:base + 256].rearrange("(k p) n -> p k n", p=128),
                           writes=[b], owner=b_ringh[b_ring.index(b)])

                def p1b_quarter(q):
                    for k in range(4 * q, 4 * q + 4):
                        for tb in range(NB):
                            bi_ = pstate["i"]
                            pt, b_pt = bank()
                            ptb = psb[:, bi_, :]
                            S_.group("pe", [lambda e, r=r: e.transpose(ptb[:, r * 128:(r + 1) * 128],
                                                                       xs_all[:, tb * 4 + r, k * 128:(k + 1) * 128], ident_b[:])
                                            for r in range(4)],
                                     reads=[b_xs[tb * 4 + r] for r in range(4)] + [b_cp], writes=[b_pt])
                            S_.op("dve", lambda e: e.tensor_scalar(
                                out=hT[:, k, tb * 512:(tb + 1) * 512], in0=ptb[:, 0:512], scalar1=Acoef[:, k:k + 1],
                                scalar2=modT[:, k:k + 1], op0=ALU.mult, op1=ALU.add),
                                reads=[b_pt, b_Aq[q]], writes=[b_hT[tb]])

                p1b_done = {"q": 0}

                def ada_h_compute(i, t, b):
                    v_ = f32_view(t)
                    col0 = ada_h_base(i) // 128
                    q = i // 4
                    pm, b_pm = bank()
                    fns = []
                    for jj in range(2):
                        for k in range(DC):
                            fns.append(lambda e, jj=jj, k=k: e.matmul(
                                pm[:, jj:jj + 1], lhsT=v_[:, k, jj * 128:(jj + 1) * 128],
                                rhs=scf[:, k:k + 1], start=(k == 0), stop=(k == DC - 1)))
                    S_.group("pe", fns, reads=[b, b_scf], writes=[b_pm])
                    S_.op("dve", lambda e: e.tensor_tensor(out=modT[:, col0:col0 + 2], in0=pm[:, 0:2],
                                                           in1=badT[:, col0:col0 + 2], op=ALU.add),
                          reads=[b_pm, b_p0], writes=[b_mod])
                    for _ in range(p1a_per_job):
                        if p1a_state["tt"] < NT:
                            p1a(p1a_state["tt"])
                            p1a_state["tt"] += 1
                    if i % 4 == 3:
                        sl = slice(4 * q, 4 * q + 4)
                        S_.op("dve", lambda e: e.scalar_tensor_tensor(out=Acoef[:, sl], in0=modT[:, DC + 4 * q:DC + 4 * q + 4],
                                                                      scalar=1.0, in1=ng_s[:, sl], op0=ALU.add, op1=ALU.mult),
                              reads=[b_mod, b_p0], writes=[b_Aq[q]])
                    last = (i == NH - 1)
                    if last:
                        while p1a_state["tt"] < NT:
                            p1a(p1a_state["tt"])
                            p1a_state["tt"] += 1
                        preload(v_load)
                    while p1b_done["q"] < ND and (4 * p1b_done["q"] + 7 <= i or last):
                        p1b_quarter(p1b_done["q"])
                        p1b_done["q"] += 1

                stream(NH, ada_h_load, ada_h_compute)
            S_.barrier()

            with ExitStack() as ph:
                XY = sb(ph, "XY", [128, NT, D], BF16)
                b_xy = [Buf("xy%d" % i) for i in range(NT)]
                gsum = sb(ph, "gsum", [128, NT * ND], F32)
                gsq = sb(ph, "gsq", [128, NT], F32)
                gsq2 = sb(ph, "gsq2", [128, NT * ND], F32)
                mean = sb(ph, "mean", [128, NT], F32)
                var = sb(ph, "var", [128, NT], F32)
                rs2 = sb(ph, "rs2", [128, NT], F32)
                nb2 = sb(ph, "nb2", [128, NT], F32)
                gu = [sb(ph, "gu%d" % i, [128, 512], F32) for i in range(2)]
                sz = [sb(ph, "sz%d" % i, [128, 512], F32) for i in range(2)]
                svb = [sb(ph, "svb%d" % i, [128, 512], F32) for i in range(2)]
                yst = [sb(ph, "yst%d" % i, [128, S], BF16) for i in range(2)]
                b_gu = [Buf() for _ in range(2)]
                b_sz = [Buf() for _ in range(2)]
                b_svb = [Buf() for _ in range(2)]
                b_yst = [Buf() for _ in range(2)]
                b_st = Buf("stats")

                def v_compute(i, t, b):
                    sv_ = slab_view(t)
                    for tt in range(NT):
                        p_, bp = bank()
                        S_.group("pe", [lambda e, k=k, tt=tt, p_=p_: e.matmul(
                            p_[:], lhsT=hT[:, k, tt * 128:(tt + 1) * 128], rhs=sv_[:, k, :],
                            start=(k == 0), stop=(k == DC - 1)) for k in range(DC)],
                            reads=[b, b_hT[tt // 4]], writes=[bp])
                        S_.op("act", lambda e, tt=tt, p_=p_, i=i: e.activation(
                            out=XY[:, tt, i * 512:(i + 1) * 512], in_=p_[:], func=AF.Gelu_apprx_tanh,
                            accum_out=gsum[:, tt * ND + i:tt * ND + i + 1]),
                            reads=[bp], writes=[b_xy[tt], b_st])
                        S_.op("act", lambda e, tt=tt, i=i: e.activation(
                            out=junk[:, 0:512], in_=XY[:, tt, i * 512:(i + 1) * 512], func=AF.Square,
                            accum_out=gsq2[:, tt * ND + i:tt * ND + i + 1]),
                            reads=[b_xy[tt]], writes=[b_st])

                stream(ND, v_load, v_compute)
                preload(a_load, 1)
                S_.op("dve", lambda e: e.tensor_reduce(out=gsq[:], in_=gsq2[:].rearrange("p (t c) -> p t c", c=ND),
                                                       axis=AX.X, op=ALU.add), reads=[b_st], writes=[b_st])
                S_.op("dve", lambda e: e.tensor_reduce(out=mean[:], in_=gsum[:].rearrange("p (t c) -> p t c", c=ND),
                                                       axis=AX.X, op=ALU.add), reads=[b_st], writes=[b_st])
                S_.op("dve", lambda e: e.tensor_scalar(out=mean[:], in0=mean[:], scalar1=1.0 / D, scalar2=None, op0=ALU.mult),
                      reads=[b_st], writes=[b_st])
                S_.op("dve", lambda e: e.tensor_tensor(out=var[:], in0=mean[:], in1=mean[:], op=ALU.mult),
                      reads=[b_st], writes=[b_st])
                S_.op("dve", lambda e: e.scalar_tensor_tensor(out=var[:], in0=gsq[:], scalar=1.0 / D, in1=var[:],
                                                              op0=ALU.mult, op1=ALU.subtract), reads=[b_st], writes=[b_st])
                S_.op("dve", lambda e: e.tensor_scalar(out=var[:], in0=var[:], scalar1=EPS, scalar2=None, op0=ALU.add),
                      reads=[b_st], writes=[b_st])
                S_.op("pool", lambda e: e.tensor_tensor(out=rs2[:], in0=var[:], in1=mhalf[:, 0:NT], op=ALU.pow),
                      reads=[b_st, b_small], writes=[b_st])
                S_.op("dve", lambda e: e.scalar_tensor_tensor(out=nb2[:], in0=mean[:], scalar=-1.0, in1=rs2[:],
                                                              op0=ALU.mult, op1=ALU.mult), reads=[b_st], writes=[b_st])
                for tt in range(NT):
                    if tt % 2 == 0:
                        S_.op("act", lambda e, tt=tt: e.activation(out=XY[:, tt, :], in_=XY[:, tt, :], func=AF.Identity,
                                                                   scale=rs2[:, tt:tt + 1], bias=nb2[:, tt:tt + 1]),
                              reads=[b_st], writes=[b_xy[tt]])
                    else:
                        S_.op("dve", lambda e, tt=tt: e.tensor_scalar(out=XY[:, tt, :], in0=XY[:, tt, :], scalar1=rs2[:, tt:tt + 1],
                                                                      scalar2=nb2[:, tt:tt + 1], op0=ALU.mult, op1=ALU.add),
                              reads=[b_st], writes=[b_xy[tt]])

                NP = D // 256
                cnt = {"t": 0, "y": 0}

                def a_compute(i, t, b):
                    sv_ = slab_view(t)
                    for ci in range(2):
                        c = 2 * i + ci
                        gi = c // 2
                        yi = cnt["y"] % 2
                        cnt["y"] += 1
                        for hf in range(0, NB, 2):
                            tbs = list(range(hf, min(hf + 2, NB)))
                            banks = {}
                            for kind in ("u", "z"):
                                off = ci * 128 + (256 if kind == "z" else 0)
                                for tb in tbs:
                                    p_, bp = bank()
                                    banks[(kind, tb)] = (p_, bp)
                                    S_.group("pe", [lambda e, k=k, tb=tb, p_=p_, off=off: e.matmul(
                                        p_[:], lhsT=sv_[:, k, off:off + 128], rhs=hT[:, k, tb * 512:(tb + 1) * 512],
                                        start=(k == 0), stop=(k == DC - 1)) for k in range(DC)],
                                        reads=[b, b_hT[tb]], writes=[bp])
                            for tb in tbs:
                                p_, bp = bank()
                                banks[("s", tb)] = (p_, bp)
                                S_.group("pe", [lambda e, n=n, tb=tb, p_=p_: e.matmul(
                                    p_[:, n * 128:(n + 1) * 128], lhsT=XY[:, tb * 4 + n, c * 128:(c + 1) * 128],
                                    rhs=wsT[:, gi, :], start=True, stop=True) for n in range(4)],
                                    reads=[b_xy[tb * 4 + n] for n in range(4)] + [b_ws], writes=[bp])
                            slots = {}
                            for tb in tbs:
                                slots[tb] = cnt["t"] % 2
                                cnt["t"] += 1
                            for tb in tbs:
                                p_, bp = banks[("u", tb)]
                                s_ = slots[tb]
                                S_.op("act", lambda e, p_=p_, s_=s_: e.activation(out=gu[s_][:], in_=p_[:], func=AF.Gelu_apprx_tanh),
                                      reads=[bp], writes=[b_gu[s_]])
                            for tb in tbs:
                                p_, bp = banks[("z", tb)]
                                s_ = slots[tb]
                                S_.op("act", lambda e, p_=p_, s_=s_: e.activation(out=sz[s_][:], in_=p_[:], func=AF.Silu),
                                      reads=[bp], writes=[b_sz[s_]])
                            for tb in tbs:
                                p_, bp = banks[("s", tb)]
                                s_ = slots[tb]
                                S_.op("dve", lambda e, p_=p_, s_=s_: e.scalar_tensor_tensor(
                                    out=svb[s_][:].rearrange("p (n t) -> p n t", n=4),
                                    in0=p_[:].rearrange("p (n t) -> p n t", n=4), scalar=lng_s[:, c:c + 1],
                                    in1=C2T[:, c:c + 1, :].to_broadcast([128, 4, 128]),
                                    op0=ALU.mult, op1=ALU.add), reads=[bp, b_ws, b_c], writes=[b_svb[s_]])
                                S_.op("dve", lambda e, s_=s_: e.tensor_tensor(out=gu[s_][:], in0=gu[s_][:], in1=sz[s_][:], op=ALU.mult),
                                      reads=[b_sz[s_]], writes=[b_gu[s_]])
                                S_.op("dve", lambda e, s_=s_, tb=tb: e.tensor_tensor(
                                    out=yst[yi][:, tb * 512:(tb + 1) * 512], in0=gu[s_][:], in1=svb[s_][:], op=ALU.mult),
                                    reads=[b_gu[s_], b_svb[s_]], writes=[b_yst[yi]])
                        S_.dma("sp", yaT[c * 128:(c + 1) * 128, :], yst[yi][:], reads=[b_yst[yi]], owner=b_yst[yi])

                stream(NP, a_load, a_compute)
            preload(g_load, 1)
            S_.barrier()
            st_gb = ExitStack()
            ring.append(sb(st_gb, "ring2", [128, 8192], BF16))
            b_ring.append(Buf("ring2"))
            rstate["i"] = (b_ring.index(pre["slots"][0][1]) + 1) % 3

            if True:
                gst = [sb(st_gb, "gst%d" % i, [128, S], BF16) for i in range(2)]
                b_gst = [Buf() for _ in range(2)]
                cnt = {"y": 0}

                ada_gate = ada_compute_into(b_mod2)

                def g_compute(i, t, b):
                    kind, i = gjobs[i]
                    if kind == "ada":
                        return ada_gate(i, t, b)
                    f_, cb = divmod(i, ND)
                    sv_ = slab_view(t)
                    for jj in range(4):
                        n_ = cb * 4 + jj
                        yi = cnt["y"] % 2
                        cnt["y"] += 1
                        for tb in range(NB):
                            p_, bp = bank()
                            S_.group("pe", [lambda e, k=k, tb=tb, p_=p_, jj=jj: e.matmul(
                                p_[:], lhsT=sv_[:, k, jj * 128:(jj + 1) * 128], rhs=hT[:, k, tb * 512:(tb + 1) * 512],
                                start=(k == 0), stop=(k == DC - 1)) for k in range(DC)],
                                reads=[b, b_hT[tb]], writes=[bp])
                            S_.op("act", lambda e, p_=p_, tb=tb, yi=yi: e.activation(
                                out=gst[yi][:, tb * 512:(tb + 1) * 512], in_=p_[:], func=AF.Sigmoid),
                                reads=[bp], writes=[b_gst[yi]])
                        S_.dma("sp", sgT[f_][n_ * 128:(n_ + 1) * 128, :], gst[yi][:], reads=[b_gst[yi]], owner=b_gst[yi])

                stream(len(gjobs), g_load, g_compute)

            with ExitStack() as ph:
                qT = sb(ph, "qT", [128, 2, S], BF16)
                kT = sb(ph, "kT", [128, 2, S], BF16)
                vau = sb(ph, "vau", [128, NT, 258], BF16)
                szT = sb(ph, "szT", [128, 2, S], BF16)
                E = [sb(ph, "E%d" % i, [128, 512], BF16) for i in range(5)]
                o0 = sb(ph, "o0", [128, 4, 256], F32)
                od = sb(ph, "od", [128, 4, 256], F32)
                ybn = sb(ph, "ybn", [128, 4, 256], BF16)
                jf = sb(ph, "jf", [128, 256], F32)
                rinv = sb(ph, "rinv", [128, 8], F32)
                ssq = sb(ph, "ssq", [128, 4], F32)
                rsb = sb(ph, "rsb", [128, 4], F32)
                ybst = [sb(ph, "ybst%d" % i, [128, S], BF16) for i in range(2)]
                b_q = [Buf() for _ in range(NB)]
                b_k = [Buf() for _ in range(NB)]
                b_v = [Buf() for _ in range(NT)]
                b_szT = [Buf() for _ in range(NB)]
                b_E = [Buf() for _ in range(5)]
                b_o0 = [Buf() for _ in range(4)]; b_od = [Buf() for _ in range(4)]; b_ybn = Buf(); b_ri = Buf(); b_sq = Buf()
                b_ybst = [Buf() for _ in range(2)]
                b_one = Buf()
                S_.op("dve", lambda e: e.memset(vau[:, :, 256:258], 1.0), writes=[b_one])
                ecnt = {"e": 0, "s": 0}
                qscale = 1.0 / math.sqrt(128.0)

                def proj_fm(sv_, b, off, tb):
                    p_, bp = bank()
                    S_.group("pe", [lambda e, k=k, p_=p_: e.matmul(
                        p_[:], lhsT=sv_[:, k, off:off + 128], rhs=hT[:, k, tb * 512:(tb + 1) * 512],
                        start=(k == 0), stop=(k == DC - 1)) for k in range(DC)],
                        reads=[b, b_hT[tb]], writes=[bp])
                    return p_, bp

                LOOK = 3
                NE = 5

                def attention(h):
                    tiles = [(tb, c, j) for tb in range(NB) for c in range(2) for j in range(4 * tb + 4)]
                    n = len(tiles)
                    info = {}
                    pending = []

                    def sbank():
                        i_ = 4 + ecnt["s"] % 4
                        ecnt["s"] += 1
                        return ps[i_], b_ps[i_]

                    def emit_S(idx):
                        tb, c, j = tiles[idx]
                        r0 = max(0, j - 4 * tb)
                        c0 = r0 * 128
                        off = LOFF - 128 * (j - 4 * tb)
                        p_, bp = sbank()
                        diag = j >= 4 * tb
                        use_aug = SLOPES[h] > 1.0 / 16.0 + 1e-9
                        fns = [lambda e: e.matmul(
                            p_[:, c0:512], lhsT=kT[:, c, j * 128:(j + 1) * 128],
                            rhs=qT[:, c, tb * 512 + c0:(tb + 1) * 512], start=True, stop=not (diag or use_aug))]
                        if diag:
                            fns.append(lambda e: e.matmul(
                                p_[:, c0:c0 + 128], lhsT=ident_b[:], rhs=maskT_b[:], start=False, stop=not use_aug))
                        if use_aug:
                            fns.append(lambda e: e.matmul(
                                p_[:, c0:512], lhsT=kaug[:, h * 128:(h + 1) * 128],
                                rhs=laug[:, off + c0:off + 512], start=False, stop=True))
                        S_.group("pe", fns, reads=[b_k[j // 4], b_q[tb], b_cp], writes=[bp])
                        ei = ecnt["e"] % NE
                        ecnt["e"] += 1
                        if use_aug:
                            S_.op("act", lambda e: e.activation(out=E[ei][:, c0:512], in_=p_[:, c0:512], func=AF.Exp),
                                  reads=[bp], writes=[b_E[ei]])
                        else:
                            bc = h * 16 + (j - 4 * tb) + 12
                            S_.op("act", lambda e: e.activation(out=E[ei][:, c0:512], in_=p_[:, c0:512], func=AF.Exp,
                                                                bias=btab[:, bc:bc + 1], scale=1.0),
                                  reads=[bp, b_c], writes=[b_E[ei]])
                        info[idx] = (ei, r0)

                    def emit_PV(idx):
                        tb, c, j = tiles[idx]
                        ei, r0 = info.pop(idx)
                        if j == 0:
                            for r in range(r0, 4):
                                S_.group("pe", [lambda e, r=r: e.matmul(
                                    ps[r][:, 0:257], lhsT=E[ei][:, r * 128:(r + 1) * 128], rhs=vau[:, j, 0:257],
                                    start=True, stop=(j == 4 * tb + r))],
                                    reads=[b_E[ei], b_v[j], b_one], writes=[b_ps[r]])
                        else:
                            S_.group("pe", [lambda e, r=r: e.matmul(
                                ps[r][:, 0:257], lhsT=E[ei][:, r * 128:(r + 1) * 128], rhs=vau[:, j, 0:257],
                                start=(j == 0), stop=(j == 4 * tb + r)) for r in range(r0, 4)],
                                reads=[b_E[ei], b_v[j], b_one], writes=[b_ps[r] for r in range(r0, 4)])
                        if j == 4 * tb + 3:
                            evac(idx, tb, c)

                    def evac(idx, tb, c):
                        bpo = [b_ps[r] for r in range(4)]
                        cs = slice(c * 4, c * 4 + 4)
                        S_.op("dve", lambda e: e.reciprocal(out=rinv[:, cs], in_=psall[:, 0:4, 256]),
                              reads=bpo, writes=[b_ri])
                        if c == 1:
                            S_.op("dve", lambda e: e.tensor_scalar(out=rinv[:, cs], in0=rinv[:, cs], scalar1=neglam[:, 0:1],
                                                                   scalar2=None, op0=ALU.mult), reads=[b_ri, b_l], writes=[b_ri])
                        for r in range(4):
                            ci = c * 4 + r
                            if c == 0:
                                S_.op("dve", lambda e, r=r, ci=ci: e.tensor_scalar(
                                    out=o0[:, r, :], in0=ps[r][:, 0:256], scalar1=rinv[:, ci:ci + 1], scalar2=None, op0=ALU.mult),
                                    reads=[b_ps[r], b_ri], writes=[b_o0[r]])
                            else:
                                S_.op("dve", lambda e, r=r, ci=ci: e.scalar_tensor_tensor(
                                    out=od[:, r, :], in0=ps[r][:, 0:256], scalar=rinv[:, ci:ci + 1], in1=o0[:, r, :],
                                    op0=ALU.mult, op1=ALU.add), reads=[b_ps[r], b_ri, b_o0[r]], writes=[b_od[r]])
                        if c == 0:
                            return
                        for r in range(4):
                            S_.op("dve", lambda e, r=r: e.scalar_tensor_tensor(
                                out=jf[:], in0=od[:, r, :], scalar=1.0, in1=od[:, r, :],
                                op0=ALU.mult, op1=ALU.mult, accum_out=ssq[:, r:r + 1]), reads=[b_od[r]], writes=[b_sq])
                        S_.op("dve", lambda e: e.tensor_scalar(out=ssq[:], in0=ssq[:], scalar1=1.0 / 256.0, scalar2=SUBLN_EPS,
                                                               op0=ALU.mult, op1=ALU.add), reads=[b_sq], writes=[b_sq])
                        S_.op("pool", lambda e: e.tensor_tensor(out=rsb[:], in0=ssq[:], in1=mhalf[:, 0:4], op=ALU.pow),
                              reads=[b_sq, b_small], writes=[b_sq])
                        for r in range(4):
                            S_.op("dve", lambda e, r=r: e.scalar_tensor_tensor(
                                out=ybn[:, r, :], in0=od[:, r, :], scalar=rsb[:, r:r + 1], in1=G2[:],
                                op0=ALU.mult, op1=ALU.mult), reads=[b_od[r], b_sq, b_l], writes=[b_ybn])

                        def transposes(tb=tb):
                            for e2 in range(2):
                                pt, b_pt = sbank()
                                ptb = pt.bitcast(BF16)
                                S_.group("pe", [lambda e, r=r: e.transpose(
                                    ptb[:, r * 128:(r + 1) * 128], ybn[:, r, e2 * 128:(e2 + 1) * 128], ident_b[:]) for r in range(4)],
                                    reads=[b_ybn, b_cp], writes=[b_pt])
                                S_.op("dve", lambda e: e.tensor_tensor(
                                    out=ybst[e2][:, tb * 512:(tb + 1) * 512], in0=ptb[:, 0:512], in1=szT[:, e2, tb * 512:(tb + 1) * 512],
                                    op=ALU.mult), reads=[b_pt, b_szT[tb]], writes=[b_ybst[e2]])
                        pending.append((idx + LOOK + 10, transposes))

                    for step in range(n + LOOK):
                        if step == n // 2 and h == H - 1:
                            rstate["i"] = 0
                            preload(m_load, 1)
                        if step < n:
                            emit_S(step)
                        if step - LOOK >= 0:
                            emit_PV(step - LOOK)
                        while pending and pending[0][0] <= step:
                            pending.pop(0)[1]()
                    pstate["i"] = 0

                    def tail():
                        while pending:
                            pending.pop(0)[1]()
                        for e2 in range(2):
                            r_ = h * 256 + e2 * 128
                            S_.dma("sp", ybT[r_:r_ + 128, :], ybst[e2][:], reads=[b_ybst[e2]], owner=b_ybst[e2])
                    carry.append(tail)

                carry = []

                def b_compute(i, t, b):
                    h, which = divmod(i, 2)
                    sv_ = slab_view(t)
                    if which == 0:
                        for c in range(2):
                            for tb in range(NB):
                                p_, bp = proj_fm(sv_, b, c * 128, tb)
                                S_.op("dve", lambda e, p_=p_, c=c, tb=tb: e.tensor_scalar(
                                    out=qT[:, c, tb * 512:(tb + 1) * 512], in0=p_[:], scalar1=qscale, scalar2=None, op0=ALU.mult),
                                    reads=[bp], writes=[b_q[tb]])
                            if c == 0:
                                while carry:
                                    carry.pop(0)()
                        for c in range(2):
                            for tb in range(NB):
                                p_, bp = proj_fm(sv_, b, 256 + c * 128, tb)
                                S_.op("dve", lambda e, p_=p_, c=c, tb=tb: e.tensor_copy(
                                    out=kT[:, c, tb * 512:(tb + 1) * 512], in_=p_[:]), reads=[bp], writes=[b_k[tb]])
                    else:
                        for tt in range(NT):
                            p_, bp = bank()
                            S_.group("pe", [lambda e, k=k, p_=p_, tt=tt: e.matmul(
                                p_[:, 0:256], lhsT=hT[:, k, tt * 128:(tt + 1) * 128], rhs=sv_[:, k, 0:256],
                                start=(k == 0), stop=(k == DC - 1)) for k in range(DC)],
                                reads=[b, b_hT[tt // 4]], writes=[bp])
                            S_.op("dve", lambda e, p_=p_, tt=tt: e.tensor_copy(out=vau[:, tt, 0:256], in_=p_[:, 0:256]),
                                  reads=[bp], writes=[b_v[tt]])
                        for e2 in range(2):
                            for tb in range(NB):
                                p_, bp = proj_fm(sv_, b, 256 + e2 * 128, tb)
                                S_.op("act", lambda e, p_=p_, e2=e2, tb=tb: e.activation(
                                    out=szT[:, e2, tb * 512:(tb + 1) * 512], in_=p_[:], func=AF.Silu),
                                    reads=[bp], writes=[b_szT[tb]])
                        attention(h)

                stream(2 * H, b_load, b_compute)
                while carry:
                    carry.pop(0)()
            ring.pop()
            b_ring.pop()
            rstate["i"] = 1
            st_gb.close()
        S_.barrier()

        with ExitStack() as ph:
            ya_s = sb(ph, "ya_s", [128, DC, S], BF16)
            yb_s = sb(ph, "yb_s", [128, DC, S], BF16)
            b_ya = [Buf() for _ in range(NB)]
            b_yb = [Buf() for _ in range(NB)]
            sg_s = [[sb(ph, "sg%d_%d" % (f_, i), [128, S], BF16) for i in range(2)] for f_ in range(2)]
            b_sg = [[Buf() for _ in range(2)] for _ in range(2)]
            t1 = [sb(ph, "t1_%d" % i, [128, 512], F32) for i in range(2)]
            t2 = [sb(ph, "t2_%d" % i, [128, 512], F32) for i in range(2)]
            b_t1 = [Buf() for _ in range(2)]
            b_t2 = [Buf() for _ in range(2)]
            mst = [sb(ph, "mst%d" % i, [128, S], BF16) for i in range(2)]
            b_mst = [Buf() for _ in range(2)]
            yaTv = yaT.rearrange("(k p) t -> p k t", p=128)
            ybTv = ybT.rearrange("(k p) t -> p k t", p=128)
            b_ch = [Buf("chain%d" % i) for i in range(3)]
            for tb in range(NB):
                S_.dma("sp", ya_s[:, :, tb * 512:(tb + 1) * 512], yaTv[:, :, tb * 512:(tb + 1) * 512],
                       writes=[b_ya[tb], b_ch[0]], owner=b_ya[tb])
                S_.dma("act", yb_s[:, :, tb * 512:(tb + 1) * 512], ybTv[:, :, tb * 512:(tb + 1) * 512],
                       writes=[b_yb[tb], b_ch[1]], owner=b_yb[tb])
            cnt = {"y": 0, "t": 0}

            def m_compute(i, t, b):
                sv_ = slab_view(t)
                for jj in range(2):
                    n_ = i * 2 + jj
                    for f2 in range(2):
                        S_.dma("sp", sg_s[f2][jj][:], sgT[f2][n_ * 128:(n_ + 1) * 128, :], writes=[b_sg[f2][jj]])
                for tb in range(NB):
                    for jj in range(2):
                        pa, bpa = bank()
                        S_.group("pe", [lambda e, k=k: e.matmul(
                            pa[:], lhsT=sv_[:, k, jj * 128:(jj + 1) * 128], rhs=ya_s[:, k, tb * 512:(tb + 1) * 512],
                            start=(k == 0), stop=(k == DC - 1)) for k in range(DC)],
                            reads=[b, b_ya[tb]], writes=[bpa])
                        pb, bpb = bank()
                        S_.group("pe", [lambda e, k=k: e.matmul(
                            pb[:], lhsT=sv_[:, k, 256 + jj * 128:256 + (jj + 1) * 128], rhs=yb_s[:, k, tb * 512:(tb + 1) * 512],
                            start=(k == 0), stop=(k == DC - 1)) for k in range(DC)],
                            reads=[b, b_yb[tb]], writes=[bpb])
                        ti = cnt["t"] % 2
                        cnt["t"] += 1
                        S_.op("dve", lambda e: e.tensor_tensor(
                            out=t1[ti][:], in0=pa[:], in1=sg_s[0][jj][:, tb * 512:(tb + 1) * 512], op=ALU.mult),
                            reads=[bpa, b_sg[0][jj]], writes=[b_t1[ti]])
                        S_.op("dve", lambda e: e.tensor_tensor(
                            out=t2[ti][:], in0=pb[:], in1=sg_s[1][jj][:, tb * 512:(tb + 1) * 512], op=ALU.mult),
                            reads=[bpb, b_sg[1][jj]], writes=[b_t2[ti]])
                        S_.op("pool", lambda e: e.tensor_tensor(
                            out=mst[jj][:, tb * 512:(tb + 1) * 512], in0=t1[ti][:], in1=t2[ti][:], op=ALU.add),
                            reads=[b_t1[ti], b_t2[ti]], writes=[b_mst[jj]])
                for jj in range(2):
                    n_ = i * 2 + jj
                    S_.dma("sp", mT[n_ * 128:(n_ + 1) * 128, :], mst[jj][:], reads=[b_mst[jj]], owner=b_mst[jj])

            stream(D // 256, m_load, m_compute)
        S_.barrier()

        with ExitStack() as ph:
            m_s = sb(ph, "m_s", [128, DC, S], BF16)
            wo_s = sb(ph, "wo_s", [128, DC, D], BF16)
            b_m = [Buf() for _ in range(NB)]
            b_wo = [Buf() for _ in range(ND)]
            gate_bc = sb(ph, "gate_bc", [128, D], F32)
            fng_bc = sb(ph, "fng_bc", [128, D], F32)
            xo = [sb(ph, "xo%d" % i, [128, D], F32)[:] for i in range(2)]
            for rt in ring:
                rf = rt[:].bitcast(F32)
                for q_ in range(min(2, 4096 // D)):
                    xo.append(rf[:, q_ * D:(q_ + 1) * D])
            NXO = len(xo)
            b_xo = [Buf() for _ in range(NXO)]
            tm = [sb(ph, "tm%d" % i, [128, 512], F32) for i in range(2)]
            b_tm = [Buf() for _ in range(2)]
            dg = sb(ph, "dg", [128, 128], F32)
            ones2 = sb(ph, "ones2", [128, 128], F32)
            s2 = sb(ph, "s2", [128, NT], F32)
            r2 = sb(ph, "r2", [128, NT], F32)
            b_dg = Buf(); b_gb = Buf(); b_s2 = Buf(); b_fg = Buf()
            mTv = mT.rearrange("(k p) t -> p k t", p=128)
            b_ch2 = [Buf("ochain%d" % i) for i in range(2)]
            for cb in range(ND):
                S_.dma("pool", wo_s[:, :, cb * 512:(cb + 1) * 512], wsrc(w_o, cb * 512, 512),
                       writes=[b_wo[cb], b_ch2[0]], owner=b_wo[cb])
            for tb in range(NB):
                S_.dma("act", m_s[:, :, tb * 512:(tb + 1) * 512], mTv[:, :, tb * 512:(tb + 1) * 512],
                       writes=[b_m[tb], b_ch2[1]], owner=b_m[tb])
            S_.dma("sp", fng_bc[:], fng.partition_broadcast(128), writes=[b_fg])
            S_.op("dve", lambda e: e.memset(ones2[:], 1.0), writes=[b_dg])
            for k in range(DC):
                S_.op("dve", lambda e, k=k: e.tensor_scalar(out=dg[:], in0=ident_f[:], scalar1=modT[:, 2 * DC + k:2 * DC + k + 1],
                                                            scalar2=None, op0=ALU.mult), reads=[b_c, b_mod2], writes=[b_dg])
                if k % 4 == 0:
                    pg, b_pg = bank()
                S_.group("pe", [lambda e, k=k, pg=pg: e.matmul(pg[:, (k % 4) * 128:(k % 4 + 1) * 128], lhsT=ones2[:], rhs=dg[:],
                                                              start=True, stop=True)], reads=[b_dg], writes=[b_pg])
                if k % 4 == 3:
                    kb = k // 4
                    S_.op("dve", lambda e, pg=pg, kb=kb: e.tensor_copy(out=gate_bc[:, kb * 512:(kb + 1) * 512], in_=pg[:]),
                          reads=[b_pg], writes=[b_gb])
            XA = min(3, NXO - 2)

            def load_xo(tt):
                S_.dma("sp", xo[tt % NXO][:], x[tt * 128:(tt + 1) * 128, :], writes=[b_xo[tt % NXO]])

            def epilogue(tt):
                xi = tt % NXO
                S_.op("act", lambda e: e.activation(out=junk[:], in_=xo[xi][:], func=AF.Square,
                                                    accum_out=s2[:, tt:tt + 1]), reads=[b_xo[xi]], writes=[b_s2])
                S_.op("dve", lambda e: e.tensor_scalar(out=s2[:, tt:tt + 1], in0=s2[:, tt:tt + 1], scalar1=1.0 / D, scalar2=EPS,
                                                       op0=ALU.mult, op1=ALU.add), reads=[b_s2], writes=[b_s2])
                S_.op("pool", lambda e: e.tensor_tensor(out=r2[:, tt:tt + 1], in0=s2[:, tt:tt + 1], in1=mhalf[:, 0:1], op=ALU.pow),
                      reads=[b_s2, b_small], writes=[b_s2])
                S_.op("dve", lambda e: e.scalar_tensor_tensor(
                    out=xo[xi][:], in0=xo[xi][:], scalar=r2[:, tt:tt + 1], in1=fng_bc[:], op0=ALU.mult, op1=ALU.mult),
                    reads=[b_s2, b_fg], writes=[b_xo[xi]])
                S_.dma("sp", y[tt * 128:(tt + 1) * 128, :], xo[xi][:], reads=[b_xo[xi]], owner=b_xo[xi])

            for tt in range(min(XA, NT)):
                load_xo(tt)
            for tt in range(NT):
                xi = tt % NXO
                if tt + XA < NT:
                    load_xo(tt + XA)
                for cb in range(ND):
                    p_, bp = bank()
                    S_.group("pe", [lambda e, k=k: e.matmul(
                        p_[:], lhsT=m_s[:, k, tt * 128:(tt + 1) * 128], rhs=wo_s[:, k, cb * 512:(cb + 1) * 512],
                        start=(k == 0), stop=(k == DC - 1)) for k in range(DC)],
                        reads=[b_m[tt // 4], b_wo[cb]], writes=[bp])
                    ti = (tt * ND + cb) % 2
                    S_.op("dve", lambda e: e.tensor_tensor(
                        out=tm[ti][:], in0=p_[:], in1=gate_bc[:, cb * 512:(cb + 1) * 512], op=ALU.mult),
                        reads=[bp, b_gb], writes=[b_tm[ti]])
                    S_.op("pool", lambda e: e.tensor_tensor(
                        out=xo[xi][:, cb * 512:(cb + 1) * 512], in0=xo[xi][:, cb * 512:(cb + 1) * 512], in1=tm[ti][:], op=ALU.add),
                        reads=[b_tm[ti]], writes=[b_xo[xi]])
                if tt >= 1:
                    epilogue(tt - 1)
            epilogue(NT - 1)
        S_.barrier(engines=("sp", "pe", "act", "dve", "pool"))
        build_program.stats = (S_.ninst, S_.nwait, len(S_.sems))
    return nc


def _consts(H):
    ident = np.eye(128, dtype=np.float32)
    tril = np.tril(np.ones((128, 128), dtype=np.float32))
    s_i = np.arange(128)[:, None]
    t_i = np.arange(128)[None, :]
    maskT = np.where(s_i <= t_i, 0.0, NEG).astype(np.float32)
    idx = np.arange(LLEN)
    N = LOFF - idx
    a = np.floor_divide(N, 256)
    b = N - 256 * a
    laug = np.zeros((128, LLEN), dtype=np.float32)
    laug[0] = 1.0
    laug[1] = 256.0 * a
    laug[2] = b
    slopes = [2.0 ** (-8.0 * (i + 1) / H) for i in range(H)]
    jv = np.arange(128, dtype=np.float64)
    kaug = np.zeros((128, H * 128), dtype=np.float32)
    for hh, sl in enumerate(slopes):
        kaug[0, hh * 128:(hh + 1) * 128] = sl * jv
        kaug[1, hh * 128:(hh + 1) * 128] = sl
        kaug[2, hh * 128:(hh + 1) * 128] = sl
    btab = np.zeros((128, H * 16), dtype=np.float32)
    for hh, sl in enumerate(slopes):
        for dl in range(-12, 4):
            btab[:, hh * 16 + dl + 12] = sl * (jv + 128.0 * dl)
    return ident, tril, maskT, laug, kaug, btab


def make_in_maps(inp, S, D):
    B = inp["x"].shape[0]
    DC = D // 128
    H = D // 256
    f = lambda a: np.ascontiguousarray(np.asarray(a, dtype=np.float32))
    colT = lambda v: f(np.asarray(v).reshape(-1, 128).T)
    ident, tril, maskT, laug, kaug, btab = _consts(H)
    shared = {
        "w_ada": f(inp["w_ada"][0]), "b_adaT": colT(inp["b_ada"][0]), "ngT": colT(inp["norm_gain"][0]),
        "w_in": f(inp["w_in"][0]), "lngT": colT(inp["ln_v_gain"][0]), "lnbT": colT(inp["ln_v_bias"][0]),
        "w_sp": f(inp["w_spatial"][0]), "b_sp": f(np.asarray(inp["b_spatial"][0]).reshape(1, -1)),
        "lamv": f(np.concatenate([np.asarray(inp[k][0]).reshape(-1) for k in
                                  ("lambda_q1", "lambda_k1", "lambda_q2", "lambda_k2")]).reshape(1, -1)),
        "sublng": f(np.asarray(inp["subln_gain"][0]).reshape(1, -1)),
        "w_a": f(inp["w_branch_a"][0]), "w_b": f(inp["w_branch_b"][0]), "w_o": f(inp["w_out"][0]),
        "fng": f(np.asarray(inp["final_norm_gain"]).reshape(1, -1)),
        "c_ident": ident, "c_tril": tril, "c_maskT": maskT, "c_laug": laug, "c_kaug": kaug, "c_btab": btab,
    }
    maps = []
    for b in range(B):
        m = dict(shared)
        m["x"] = f(inp["x"][b])
        m["cT"] = colT(inp["c"][b])
        maps.append(m)
    return maps


_CACHE = {}


def kernel(**inputs):
    x = np.asarray(inputs["x"])
    B, S, D = x.shape
    key = (S, D)
    if key not in _CACHE:
        _CACHE[key] = build_program(S, D)
    nc = _CACHE[key]
    in_maps = make_in_maps(inputs, S, D)
    res = run_bass_kernel_spmd(nc, in_maps, core_ids=list(range(B)))
    return np.stack([np.asarray(r["y"], dtype=np.float32) for r in res.results], axis=0)
```

```python
import math
from contextlib import ExitStack

import numpy as np
import concourse.bass as bass
import concourse.mybir as mybir
from concourse.bass_utils import run_bass_kernel_spmd

F32 = mybir.dt.float32
BF16 = mybir.dt.bfloat16
AF = mybir.ActivationFunctionType
ALU = mybir.AluOpType
AX = mybir.AxisListType

LAM_INIT = 0.8 - 0.6 * math.exp(-0.3 * 0)
EPS = 1e-6
SUBLN_EPS = 1e-5
NEG = -30000.0
LOFF = 384
LLEN = 2432


class Buf:
    _n = 0

    def __init__(self, name=""):
        Buf._n += 1
        self.id = Buf._n
        self.name = name
        self.w = None
        self.r = []
        self.dma_sem = None
        self.dma_cnt = 0


class Sch:
    def __init__(self, nc, stack):
        self.nc = nc
        self.stack = stack
        self.eng = {"pe": nc.tensor, "act": nc.scalar, "dve": nc.vector,
                    "pool": nc.gpsimd, "sp": nc.sync}
        self.sems = {}
        self.cnt = {}
        for e in ("pe", "act", "dve", "pool"):
            self.sems[e] = stack.enter_context(nc.semaphore("c_" + e))
            self.cnt[e] = 0
        self.seen = {e: {} for e in self.eng}
        self.owners = []
        self.ninst = 0
        self.nwait = 0

    def _deps(self, reads, writes):
        d = {}

        def add(ev):
            if ev is None:
                return
            k, v = ev
            if d.get(k, -1) < v:
                d[k] = v
        for b in reads:
            add(b.w)
        for b in writes:
            add(b.w)
            for ev in b.r:
                add(ev)
        return d

    def _wait(self, e, deps):
        eng = self.eng[e]
        seen = self.seen[e]
        for k, v in deps.items():
            if k == e and e == "pe":
                continue
            if seen.get(k, 0) >= v:
                continue
            eng.wait_ge(self.sems[k], v)
            seen[k] = v
            self.nwait += 1

    def _record(self, ev, reads, writes):
        for b in reads:
            b.r.append(ev)
            if len(b.r) > 64:
                best = {}
                for k, v in b.r:
                    if best.get(k, -1) < v:
                        best[k] = v
                b.r = list(best.items())
        for b in writes:
            b.w = ev
            b.r = []

    def op(self, e, fn, reads=(), writes=()):
        self._wait(e, self._deps(reads, writes))
        ins = fn(self.eng[e])
        self.cnt[e] += 1
        ins.then_inc(self.sems[e], 1)
        ev = (e, self.cnt[e])
        self._record(ev, reads, writes)
        self.ninst += 1
        return ev

    def group(self, e, fns, reads=(), writes=()):
        self._wait(e, self._deps(reads, writes))
        ins = None
        for fn in fns:
            ins = fn(self.eng[e])
            self.ninst += 1
        self.cnt[e] += 1
        ins.then_inc(self.sems[e], 1)
        ev = (e, self.cnt[e])
        self._record(ev, reads, writes)
        return ev

    def dma(self, q, out, in_, reads=(), writes=(), owner=None, **kw):
        if owner is None:
            owner = writes[0] if writes else reads[0]
        if owner.dma_sem is None:
            owner.dma_sem = self.stack.enter_context(self.nc.semaphore("d_%d" % owner.id))
            self.sems[("d", owner.id)] = owner.dma_sem
            self.owners.append(owner)
        self._wait(q, self._deps(reads, writes))
        ins = self.eng[q].dma_start(out=out, in_=in_, **kw)
        owner.dma_cnt += 1
        ins.then_inc(owner.dma_sem, 16)
        ev = (("d", owner.id), 16 * owner.dma_cnt)
        self._record(ev, reads, writes)
        self.ninst += 1
        return ev

    def barrier(self, engines=("pe", "act", "dve", "pool", "sp")):
        d = {}
        for e in ("pe", "act", "dve", "pool"):
            if self.cnt[e] > 0:
                d[e] = self.cnt[e]
        for o in self.owners:
            d[("d", o.id)] = 16 * o.dma_cnt
        for e in engines:
            eng = self.eng[e]
            seen = self.seen[e]
            for k, v in d.items():
                if k == e:
                    continue
                if seen.get(k, 0) >= v:
                    continue
                eng.wait_ge(self.sems[k], v)
                seen[k] = v
                self.nwait += 1


def build_program(S, D, debug=False):
    NT = S // 128
    NB = S // 512
    DC = D // 128
    ND = D // 512
    H = D // 256
    G = D // 256
    SLOPES = [2.0 ** (-8.0 * (i + 1) / H) for i in range(H)]
    assert S % 512 == 0 and D % 512 == 0 and NB <= 4
    nc = bass.Bass("TRN2", target_bir_lowering=False)

    def din(name, shape):
        return nc.dram_tensor(name, list(shape), F32, kind="ExternalInput").ap()

    x = din("x", [S, D])
    cT = din("cT", [128, DC])
    w_ada = din("w_ada", [D, 3 * D])
    b_adaT = din("b_adaT", [128, 3 * DC])
    ngT = din("ngT", [128, DC])
    w_in = din("w_in", [D, 9 * D])
    lngT = din("lngT", [128, DC])
    lnbT = din("lnbT", [128, DC])
    w_sp = din("w_sp", [G, 128, 128])
    b_sp = din("b_sp", [1, G * 128])
    lamv = din("lamv", [1, 4 * 128])
    sublng = din("sublng", [1, 256])
    w_a = din("w_a", [D, D])
    w_b = din("w_b", [D, D])
    w_o = din("w_o", [D, D])
    fng = din("fng", [1, D])
    c_ident = din("c_ident", [128, 128])
    c_tril = din("c_tril", [128, 128])
    c_maskT = din("c_maskT", [128, 128])
    c_laug = din("c_laug", [128, LLEN])
    c_kaug = din("c_kaug", [128, H * 128])
    c_btab = din("c_btab", [128, H * 16])
    y = nc.dram_tensor("y", [S, D], F32, kind="ExternalOutput").ap()
    skind = "ExternalOutput" if debug else "Internal"
    yaT = nc.dram_tensor("yaT", [D, S], BF16, kind=skind).ap()
    ybT = nc.dram_tensor("ybT", [D, S], BF16, kind=skind).ap()
    sgT = [nc.dram_tensor("sgT%d" % i, [D, S], BF16, kind=skind).ap() for i in range(2)]
    mT = nc.dram_tensor("mT", [D, S], BF16, kind=skind).ap()

    with ExitStack() as g:
        S_ = Sch(nc, g)
        sb = lambda st, name, shape, dt: st.enter_context(nc.sbuf_tensor(name, list(shape), dt))

        psall = g.enter_context(nc.psum_tensor("psall", [128, 8, 512], F32))
        ps = [psall[:, i, :] for i in range(8)]
        b_ps = [Buf("ps%d" % i) for i in range(8)]
        pstate = {"i": 0}

        def bank():
            i = pstate["i"]
            pstate["i"] = (i + 1) % 8
            return ps[i], b_ps[i]

        ring = [sb(g, "ring%d" % i, [128, 8192], BF16) for i in range(2)]
        b_ring = [Buf("ring%d" % i) for i in range(2)]
        rstate = {"i": 0}

        def ring_next():
            i = rstate["i"]
            rstate["i"] = (i + 1) % len(ring)
            return ring[i], b_ring[i]

        def slab_view(t):
            return t[:, 0:DC * 512].rearrange("p (k n) -> p k n", k=DC)

        def wsrc(w, c0, n):
            return w[:, c0:c0 + n].rearrange("(k p) n -> p k n", p=128)

        pre = {"slots": []}

        def preload(load, count=1):
            for i in range(count):
                sl = ring_next()
                load(i, *sl)
                pre["slots"].append(sl)

        def stream(n, load, compute):
            depth = len(ring) - 1
            slots = {}
            nxt = 0
            for sl in pre["slots"]:
                slots[nxt] = sl
                nxt += 1
            pre["slots"] = []
            for i in range(n):
                while nxt < n and nxt <= i + depth:
                    slots[nxt] = ring_next()
                    load(nxt, *slots[nxt])
                    nxt += 1
                compute(i, *slots[i])
                del slots[i]

        def ada_load(i, t, b):
            S_.dma("pool", slab_view(t), wsrc(w_ada, i * 512, 512), writes=[b])

        def v_load(i, t, b):
            S_.dma("pool", slab_view(t), wsrc(w_in, D + i * 512, 512), writes=[b])

        def a_load(i, t, b):
            sv_ = slab_view(t)
            S_.dma("pool", sv_[:, :, 0:256], wsrc(w_in, i * 256, 256), writes=[b])
            S_.dma("pool", sv_[:, :, 256:512], wsrc(w_in, 2 * D + i * 256, 256), writes=[b])

        gjobs = []
        for i in range(2 * ND):
            gjobs.append(("g", i))
            if i % 2 == 1:
                gjobs.append(("ada", 2 * ND + i // 2))

        def g_load(i, t, b):
            kind, i = gjobs[i]
            if kind == "ada":
                return ada_load(i, t, b)
            f_, cb = divmod(i, ND)
            S_.dma("pool", slab_view(t), wsrc(w_in, (7 + f_) * D + cb * 512, 512), writes=[b])

        def b_load(i, t, b):
            h, which = divmod(i, 2)
            sv_ = slab_view(t)
            f0 = 3 if which == 0 else 5
            S_.dma("pool", sv_[:, :, 0:256], wsrc(w_in, f0 * D + h * 256, 256), writes=[b])
            S_.dma("pool", sv_[:, :, 256:512], wsrc(w_in, (f0 + 1) * D + h * 256, 256), writes=[b])

        def m_load(i, t, b):
            sv_ = slab_view(t)
            S_.dma("pool", sv_[:, :, 0:256], wsrc(w_a, i * 256, 256), writes=[b])
            S_.dma("pool", sv_[:, :, 256:512], wsrc(w_b, i * 256, 256), writes=[b])

        b_c = Buf("consts")
        ident_f = sb(g, "ident_f", [128, 128], F32)
        junk = sb(g, "junk", [128, D], BF16)
        modT = sb(g, "modT", [128, 3 * DC], F32)
        Acoef = sb(g, "Acoef", [128, DC], F32)
        mhalf = sb(g, "mhalf", [128, 16], F32)
        b_small = Buf("small")
        S_.dma("sp", ident_f[:], c_ident, writes=[b_c])
        S_.op("dve", lambda e: e.memset(mhalf[:], -0.5), writes=[b_small])

        b_mod = Buf("mod")
        b_mod2 = Buf("mod_gate")
        b_A = Buf("Acoef")

        with ExitStack() as st_h:
            ident_b = sb(st_h, "ident_b", [128, 128], BF16)
            maskT_b = sb(st_h, "maskT_b", [128, 128], BF16)
            laug = sb(st_h, "laug", [128, LLEN], BF16)
            kaug = sb(st_h, "kaug", [128, H * 128], BF16)
            C2T = sb(st_h, "C2T", [128, DC, 128], F32)
            wsT = sb(st_h, "wsT", [128, G, 128], BF16)
            G2 = sb(st_h, "G2", [128, 256], F32)
            lng_s = sb(st_h, "lng_s", [128, DC], F32)
            lnb_s = sb(st_h, "lnb_s", [128, DC], F32)
            neglam = sb(st_h, "neglam", [128, 1], F32)
            btab = sb(st_h, "btab", [128, H * 16], F32)
            S_.dma("sp", btab[:], c_btab, writes=[b_c])
            b_cp = Buf("consts_pool")
            S_.dma("pool", ident_b[:], c_ident, writes=[b_cp])
            S_.dma("pool", maskT_b[:], c_maskT, writes=[b_cp])
            S_.dma("pool", laug[:], c_laug, writes=[b_cp])
            S_.dma("pool", kaug[:], c_kaug, writes=[b_cp])
            S_.dma("sp", lng_s[:], lngT, writes=[b_c])
            S_.dma("sp", lnb_s[:], lnbT, writes=[b_c])
            scb = sb(st_h, "scb", [128, DC], BF16)
            badT = sb(st_h, "badT", [128, 3 * DC], F32)
            hT = sb(st_h, "hT", [128, DC, S], BF16)
            b_hT = [Buf("hT%d" % i) for i in range(NB)]

            with ExitStack() as ph:
                cs = sb(ph, "cs", [128, DC], F32)
                ng_s = sb(ph, "ng_s", [128, DC], F32)
                lvec = sb(ph, "lvec", [128, 4 * 128], F32)
                lj = sb(ph, "lj", [128, 128], F32)
                ld = sb(ph, "ld", [128, 4], F32)
                wtmp = sb(ph, "wtmp", [128, 128], F32)
                wsTf = sb(ph, "wsTf", [128, 128], F32)
                tril = sb(ph, "tril", [128, 128], F32)
                ones_f = sb(ph, "ones_f", [128, 128], F32)
                bs_row = sb(ph, "bs_row", [1, 128], F32)
                b_bsr = Buf("bs_row")
                BSb = sb(ph, "BSb", [128, 128], F32)
                NXT = 2
                xt = [sb(ph, "xt%d" % i, [128, D], F32) for i in range(NXT)]
                xs_all = sb(ph, "xs_all", [128, NT, D], BF16)
                ss = sb(ph, "ss", [128, NT], F32)
                rstd = sb(ph, "rstd", [128, NT], F32)
                b_xt = [Buf("xt%d" % i) for i in range(NXT)]
                b_xs = [Buf("xs%d" % i) for i in range(NT)]
                b_p0 = Buf("p0c")
                b_sc = Buf("sc")
                b_ss = Buf("ss")

                S_.dma("sp", cs[:], cT, writes=[b_p0])
                S_.dma("sp", badT[:], b_adaT, writes=[b_p0])
                S_.dma("sp", ng_s[:], ngT, writes=[b_p0])
                S_.dma("sp", lvec[:], lamv.partition_broadcast(128), writes=[b_p0])
                S_.dma("sp", tril[:], c_tril, writes=[b_p0])
                S_.dma("sp", G2[:], sublng.partition_broadcast(128), writes=[b_p0])
                S_.op("act", lambda e: e.activation(out=scb[:], in_=cs[:], func=AF.Silu),
                      reads=[b_p0], writes=[b_sc])
                S_.op("dve", lambda e: e.memset(ones_f[:], 1.0), writes=[b_small])

                def load_xt(tt):
                    S_.dma("sp", xt[tt % NXT][:], x[tt * 128:(tt + 1) * 128, :], writes=[b_xt[tt % NXT]])
                for tt in range(min(NXT, NT)):
                    load_xt(tt)

                def p1a(tt):
                    xb, bx = xt[tt % NXT], b_xt[tt % NXT]
                    S_.op("act", lambda e: e.activation(out=junk[:], in_=xb[:], func=AF.Square, accum_out=ss[:, tt:tt + 1]),
                          reads=[bx], writes=[b_ss])
                    S_.op("dve", lambda e: e.tensor_scalar(out=ss[:, tt:tt + 1], in0=ss[:, tt:tt + 1], scalar1=1.0 / D, scalar2=EPS,
                                                           op0=ALU.mult, op1=ALU.add), reads=[b_ss], writes=[b_ss])
                    S_.op("pool", lambda e: e.tensor_tensor(out=rstd[:, tt:tt + 1], in0=ss[:, tt:tt + 1], in1=mhalf[:, 0:1], op=ALU.pow),
                          reads=[b_ss, b_small], writes=[b_ss])
                    S_.op("act", lambda e: e.activation(out=xs_all[:, tt, :], in_=xb[:], func=AF.Identity,
                                                        scale=rstd[:, tt:tt + 1]), reads=[bx, b_ss], writes=[b_xs[tt]])
                    if tt + NXT < NT:
                        load_xt(tt + NXT)
                p1a_state = {"tt": 0}
                p1a_per_job = -(-NT // (2 * ND))

                def ada_compute_into(bmod, with_p1a=False):
                    def ada_compute(i, t, b):
                        sv_ = slab_view(t)
                        pm, b_pm = bank()
                        fns = []
                        for jj in range(4):
                            for k in range(DC):
                                fns.append(lambda e, jj=jj, k=k: e.matmul(
                                    pm[:, jj:jj + 1], lhsT=sv_[:, k, jj * 128:(jj + 1) * 128],
                                    rhs=scb[:, k:k + 1], start=(k == 0), stop=(k == DC - 1)))
                        S_.group("pe", fns, reads=[b, b_sc], writes=[b_pm])
                        S_.op("dve", lambda e: e.tensor_tensor(out=modT[:, i * 4:(i + 1) * 4], in0=pm[:, 0:4],
                                                               in1=badT[:, i * 4:(i + 1) * 4], op=ALU.add),
                              reads=[b_pm, b_p0], writes=[bmod])
                        if with_p1a:
                            for _ in range(p1a_per_job):
                                if p1a_state["tt"] < NT:
                                    p1a(p1a_state["tt"])
                                    p1a_state["tt"] += 1
                    return ada_compute

                b_l = Buf("lam")
                for i in range(2):
                    S_.op("dve", lambda e, i=i: e.scalar_tensor_tensor(
                        out=lj[:], in0=lvec[:, (2 * i) * 128:(2 * i + 1) * 128], scalar=1.0,
                        in1=lvec[:, (2 * i + 1) * 128:(2 * i + 2) * 128],
                        op0=ALU.mult, op1=ALU.mult, accum_out=ld[:, i:i + 1]), reads=[b_p0], writes=[b_l])
                S_.op("act", lambda e: e.activation(out=ld[:, 2:4], in_=ld[:, 0:2], func=AF.Exp),
                      reads=[b_l], writes=[b_l])
                S_.op("dve", lambda e: e.tensor_tensor(out=neglam[:], in0=ld[:, 3:4], in1=ld[:, 2:3], op=ALU.subtract),
                      reads=[b_l], writes=[b_l])
                S_.op("dve", lambda e: e.tensor_scalar(out=neglam[:], in0=neglam[:], scalar1=-LAM_INIT, scalar2=None,
                                                       op0=ALU.add), reads=[b_l], writes=[b_l])
                S_.op("dve", lambda e: e.tensor_scalar(out=G2[:], in0=G2[:], scalar1=1.0 - LAM_INIT, scalar2=None,
                                                       op0=ALU.mult), reads=[b_p0, b_l], writes=[b_l])

                b_w = Buf("wsp")
                b_ws = Buf("wsT")
                for gi in range(G):
                    S_.dma("sp", wtmp[:], w_sp[gi], reads=[], writes=[b_w])
                    S_.op("dve", lambda e: e.tensor_tensor(out=wtmp[:], in0=wtmp[:], in1=tril[:], op=ALU.mult),
                          reads=[b_p0], writes=[b_w])
                    pt, b_pt = bank()
                    S_.group("pe", [lambda e, pt=pt: e.transpose(pt[:, 0:128], wtmp[:], ident_f[:])],
                             reads=[b_w, b_c], writes=[b_pt])
                    S_.op("dve", lambda e, pt=pt: e.tensor_copy(out=wsTf[:], in_=pt[:, 0:128]),
                          reads=[b_pt], writes=[b_ws])
                    S_.op("act", lambda e, pt=pt, gi=gi: e.activation(out=wsT[:, gi, :], in_=pt[:, 0:128], func=AF.Copy),
                          reads=[b_pt], writes=[b_ws])
                    pr, b_pr = bank()
                    S_.dma("sp", bs_row[:], b_sp[0:1, gi * 128:(gi + 1) * 128], writes=[b_bsr])
                    S_.group("pe", [lambda e, pr=pr: e.matmul(pr[:, 0:128], lhsT=ones_f[:], rhs=wsTf[:], start=True, stop=True),
                                    lambda e, pr=pr, gi=gi: e.matmul(pr[:, 128:256], lhsT=ones_f[0:1, :],
                                                                    rhs=bs_row[0:1, :],
                                                                    start=True, stop=True)],
                             reads=[b_ws, b_small, b_p0, b_bsr], writes=[b_pr])
                    S_.op("dve", lambda e, pr=pr: e.tensor_copy(out=BSb[:], in_=pr[:, 128:256]),
                          reads=[b_pr], writes=[b_w])
                    for ci in range(2):
                        c = gi * 2 + ci
                        S_.op("dve", lambda e, pr=pr, c=c: e.scalar_tensor_tensor(
                            out=C2T[:, c, :], in0=pr[:, 0:128], scalar=lnb_s[:, c:c + 1], in1=BSb[:],
                            op0=ALU.mult, op1=ALU.add), reads=[b_pr, b_w, b_c], writes=[b_ws])

                NQ = 8 * ND
                b_ringh = [Buf("ringh%d" % i) for i in range(2)]
                b_Aq = [Buf("A%d" % q) for q in range(ND)]
                psb = psall[:].bitcast(BF16)
                p1a_per_job = -(-NT // 8)

                def ada_q_base(i):
                    q, r = divmod(i, 8)
                    return (0 if r < 4 else D) + q * 512 + (r % 4) * 128

                def f32_view(t):
                    return t[:, 0:4096].bitcast(F32)[:, 0:DC * 128].rearrange("p (k n) -> p k n", k=DC)

                def bf_view(t):
                    return t[:, 4096:4096 + DC * 128].rearrange("p (k n) -> p k n", k=DC)

                def ada_q_load(i, t, b):
                    base = ada_q_base(i)
                    S_.dma("sp" if i % 2 == 0 else "act", f32_view(t),
                           w_ada[:, base:base + 128].rearrange("(k p) n -> p k n", p=128),
                           writes=[b], owner=b_ringh[b_ring.index(b)])

                def p1b_quarter(q):
                    for k in range(4 * q, 4 * q + 4):
                        for tb in range(NB):
                            bi_ = pstate["i"]
                            pt, b_pt = bank()
                            ptb = psb[:, bi_, :]
                            S_.group("pe", [lambda e, r=r: e.transpose(ptb[:, r * 128:(r + 1) * 128],
                                                                       xs_all[:, tb * 4 + r, k * 128:(k + 1) * 128], ident_b[:])
                                            for r in range(4)],
                                     reads=[b_xs[tb * 4 + r] for r in range(4)] + [b_cp], writes=[b_pt])
                            S_.op("dve", lambda e: e.tensor_scalar(
                                out=hT[:, k, tb * 512:(tb + 1) * 512], in0=ptb[:, 0:512], scalar1=Acoef[:, k:k + 1],
                                scalar2=modT[:, k:k + 1], op0=ALU.mult, op1=ALU.add),
                                reads=[b_pt, b_Aq[q]], writes=[b_hT[tb]])

                p1b_done = {"q": 0}

                def ada_q_compute(i, t, b):
                    fv, bv = f32_view(t), bf_view(t)
                    col = ada_q_base(i) // 128
                    q = i // 8
                    S_.op("dve", lambda e: e.tensor_copy(out=bv, in_=fv), reads=[b], writes=[b])
                    pm, b_pm = bank()
                    S_.group("pe", [lambda e, k=k: e.matmul(pm[:, 0:1], lhsT=bv[:, k, :], rhs=scb[:, k:k + 1],
                                                            start=(k == 0), stop=(k == DC - 1)) for k in range(DC)],
                             reads=[b, b_sc], writes=[b_pm])
                    S_.op("dve", lambda e: e.tensor_tensor(out=modT[:, col:col + 1], in0=pm[:, 0:1],
                                                           in1=badT[:, col:col + 1], op=ALU.add),
                          reads=[b_pm, b_p0], writes=[b_mod])
                    for _ in range(p1a_per_job):
                        if p1a_state["tt"] < NT:
                            p1a(p1a_state["tt"])
                            p1a_state["tt"] += 1
                    if i % 8 == 7:
                        sl = slice(4 * q, 4 * q + 4)
                        S_.op("dve", lambda e: e.scalar_tensor_tensor(out=Acoef[:, sl], in0=modT[:, DC + 4 * q:DC + 4 * q + 4],
                                                                      scalar=1.0, in1=ng_s[:, sl], op0=ALU.add, op1=ALU.mult),
                              reads=[b_mod, b_p0], writes=[b_Aq[q]])
                    last = (i == NQ - 1)
                    if last:
                        while p1a_state["tt"] < NT:
                            p1a(p1a_state["tt"])
                            p1a_state["tt"] += 1
                        preload(v_load)
                    while p1b_done["q"] < ND and (8 * p1b_done["q"] + 15 <= i or last):
                        p1b_quarter(p1b_done["q"])
                        p1b_done["q"] += 1

                stream(NQ, ada_q_load, ada_q_compute)
            S_.barrier()

            with ExitStack() as ph:
                XY = sb(ph, "XY", [128, NT, D], BF16)
                b_xy = [Buf("xy%d" % i) for i in range(NT)]
                gsum = sb(ph, "gsum", [128, NT * ND], F32)
                gsq = sb(ph, "gsq", [128, NT], F32)
                gsq2 = sb(ph, "gsq2", [128, NT * ND], F32)
                mean = sb(ph, "mean", [128, NT], F32)
                var = sb(ph, "var", [128, NT], F32)
                rs2 = sb(ph, "rs2", [128, NT], F32)
                nb2 = sb(ph, "nb2", [128, NT], F32)
                gu = [sb(ph, "gu%d" % i, [128, 512], F32) for i in range(2)]
                sz = [sb(ph, "sz%d" % i, [128, 512], F32) for i in range(2)]
                svb = [sb(ph, "svb%d" % i, [128, 512], F32) for i in range(2)]
                yst = [sb(ph, "yst%d" % i, [128, S], BF16) for i in range(2)]
                b_gu = [Buf() for _ in range(2)]
                b_sz = [Buf() for _ in range(2)]
                b_svb = [Buf() for _ in range(2)]
                b_yst = [Buf() for _ in range(2)]
                b_st = Buf("stats")

                def v_compute(i, t, b):
                    sv_ = slab_view(t)
                    for tt in range(NT):
                        p_, bp = bank()
                        S_.group("pe", [lambda e, k=k, tt=tt, p_=p_: e.matmul(
                            p_[:], lhsT=hT[:, k, tt * 128:(tt + 1) * 128], rhs=sv_[:, k, :],
                            start=(k == 0), stop=(k == DC - 1)) for k in range(DC)],
                            reads=[b, b_hT[tt // 4]], writes=[bp])
                        S_.op("act", lambda e, tt=tt, p_=p_, i=i: e.activation(
                            out=XY[:, tt, i * 512:(i + 1) * 512], in_=p_[:], func=AF.Gelu_apprx_tanh,
                            accum_out=gsum[:, tt * ND + i:tt * ND + i + 1]),
                            reads=[bp], writes=[b_xy[tt], b_st])
                        S_.op("act", lambda e, tt=tt, i=i: e.activation(
                            out=junk[:, 0:512], in_=XY[:, tt, i * 512:(i + 1) * 512], func=AF.Square,
                            accum_out=gsq2[:, tt * ND + i:tt * ND + i + 1]),
                            reads=[b_xy[tt]], writes=[b_st])

                stream(ND, v_load, v_compute)
                preload(a_load, 1)
                S_.op("dve", lambda e: e.tensor_reduce(out=gsq[:], in_=gsq2[:].rearrange("p (t c) -> p t c", c=ND),
                                                       axis=AX.X, op=ALU.add), reads=[b_st], writes=[b_st])
                S_.op("dve", lambda e: e.tensor_reduce(out=mean[:], in_=gsum[:].rearrange("p (t c) -> p t c", c=ND),
                                                       axis=AX.X, op=ALU.add), reads=[b_st], writes=[b_st])
                S_.op("dve", lambda e: e.tensor_scalar(out=mean[:], in0=mean[:], scalar1=1.0 / D, scalar2=None, op0=ALU.mult),
                      reads=[b_st], writes=[b_st])
                S_.op("dve", lambda e: e.tensor_tensor(out=var[:], in0=mean[:], in1=mean[:], op=ALU.mult),
                      reads=[b_st], writes=[b_st])
                S_.op("dve", lambda e: e.scalar_tensor_tensor(out=var[:], in0=gsq[:], scalar=1.0 / D, in1=var[:],
                                                              op0=ALU.mult, op1=ALU.subtract), reads=[b_st], writes=[b_st])
                S_.op("dve", lambda e: e.tensor_scalar(out=var[:], in0=var[:], scalar1=EPS, scalar2=None, op0=ALU.add),
                      reads=[b_st], writes=[b_st])
                S_.op("pool", lambda e: e.tensor_tensor(out=rs2[:], in0=var[:], in1=mhalf[:, 0:NT], op=ALU.pow),
                      reads=[b_st, b_small], writes=[b_st])
                S_.op("dve", lambda e: e.scalar_tensor_tensor(out=nb2[:], in0=mean[:], scalar=-1.0, in1=rs2[:],
                                                              op0=ALU.mult, op1=ALU.mult), reads=[b_st], writes=[b_st])
                for tt in range(NT):
                    if tt % 2 == 0:
                        S_.op("act", lambda e, tt=tt: e.activation(out=XY[:, tt, :], in_=XY[:, tt, :], func=AF.Identity,
                                                                   scale=rs2[:, tt:tt + 1], bias=nb2[:, tt:tt + 1]),
                              reads=[b_st], writes=[b_xy[tt]])
                    else:
                        S_.op("dve", lambda e, tt=tt: e.tensor_scalar(out=XY[:, tt, :], in0=XY[:, tt, :], scalar1=rs2[:, tt:tt + 1],
                                                                      scalar2=nb2[:, tt:tt + 1], op0=ALU.mult, op1=ALU.add),
                              reads=[b_st], writes=[b_xy[tt]])

                NP = D // 256
                cnt = {"t": 0, "y": 0}

                def a_compute(i, t, b):
                    sv_ = slab_view(t)
                    for ci in range(2):
                        c = 2 * i + ci
                        gi = c // 2
                        yi = cnt["y"] % 2
                        cnt["y"] += 1
                        for hf in range(0, NB, 2):
                            tbs = list(range(hf, min(hf + 2, NB)))
                            banks = {}
                            for kind in ("u", "z"):
                                off = ci * 128 + (256 if kind == "z" else 0)
                                for tb in tbs:
                                    p_, bp = bank()
                                    banks[(kind, tb)] = (p_, bp)
                                    S_.group("pe", [lambda e, k=k, tb=tb, p_=p_, off=off: e.matmul(
                                        p_[:], lhsT=sv_[:, k, off:off + 128], rhs=hT[:, k, tb * 512:(tb + 1) * 512],
                                        start=(k == 0), stop=(k == DC - 1)) for k in range(DC)],
                                        reads=[b, b_hT[tb]], writes=[bp])
                            for tb in tbs:
                                p_, bp = bank()
                                banks[("s", tb)] = (p_, bp)
                                S_.group("pe", [lambda e, n=n, tb=tb, p_=p_: e.matmul(
                                    p_[:, n * 128:(n + 1) * 128], lhsT=XY[:, tb * 4 + n, c * 128:(c + 1) * 128],
                                    rhs=wsT[:, gi, :], start=True, stop=True) for n in range(4)],
                                    reads=[b_xy[tb * 4 + n] for n in range(4)] + [b_ws], writes=[bp])
                            slots = {}
                            for tb in tbs:
                                slots[tb] = cnt["t"] % 2
                                cnt["t"] += 1
                            for tb in tbs:
                                p_, bp = banks[("u", tb)]
                                s_ = slots[tb]
                                S_.op("act", lambda e, p_=p_, s_=s_: e.activation(out=gu[s_][:], in_=p_[:], func=AF.Gelu_apprx_tanh),
                                      reads=[bp], writes=[b_gu[s_]])
                            for tb in tbs:
                                p_, bp = banks[("z", tb)]
                                s_ = slots[tb]
                                S_.op("act", lambda e, p_=p_, s_=s_: e.activation(out=sz[s_][:], in_=p_[:], func=AF.Silu),
                                      reads=[bp], writes=[b_sz[s_]])
                            for tb in tbs:
                                p_, bp = banks[("s", tb)]
                                s_ = slots[tb]
                                S_.op("dve", lambda e, p_=p_, s_=s_: e.scalar_tensor_tensor(
                                    out=svb[s_][:].rearrange("p (n t) -> p n t", n=4),
                                    in0=p_[:].rearrange("p (n t) -> p n t", n=4), scalar=lng_s[:, c:c + 1],
                                    in1=C2T[:, c:c + 1, :].to_broadcast([128, 4, 128]),
                                    op0=ALU.mult, op1=ALU.add), reads=[bp, b_ws, b_c], writes=[b_svb[s_]])
                                S_.op("dve", lambda e, s_=s_: e.tensor_tensor(out=gu[s_][:], in0=gu[s_][:], in1=sz[s_][:], op=ALU.mult),
                                      reads=[b_sz[s_]], writes=[b_gu[s_]])
                                S_.op("dve", lambda e, s_=s_, tb=tb: e.tensor_tensor(
                                    out=yst[yi][:, tb * 512:(tb + 1) * 512], in0=gu[s_][:], in1=svb[s_][:], op=ALU.mult),
                                    reads=[b_gu[s_], b_svb[s_]], writes=[b_yst[yi]])
                        S_.dma("sp", yaT[c * 128:(c + 1) * 128, :], yst[yi][:], reads=[b_yst[yi]], owner=b_yst[yi])

                stream(NP, a_load, a_compute)
            preload(g_load, 1)
            S_.barrier()
            st_gb = ExitStack()
            ring.append(sb(st_gb, "ring2", [128, 8192], BF16))
            b_ring.append(Buf("ring2"))
            rstate["i"] = (b_ring.index(pre["slots"][0][1]) + 1) % 3

            if True:
                gst = [sb(st_gb, "gst%d" % i, [128, S], BF16) for i in range(2)]
                b_gst = [Buf() for _ in range(2)]
                cnt = {"y": 0}

                ada_gate = ada_compute_into(b_mod2)

                def g_compute(i, t, b):
                    kind, i = gjobs[i]
                    if kind == "ada":
                        return ada_gate(i, t, b)
                    f_, cb = divmod(i, ND)
                    sv_ = slab_view(t)
                    for jj in range(4):
                        n_ = cb * 4 + jj
                        yi = cnt["y"] % 2
                        cnt["y"] += 1
                        for tb in range(NB):
                            p_, bp = bank()
                            S_.group("pe", [lambda e, k=k, tb=tb, p_=p_, jj=jj: e.matmul(
                                p_[:], lhsT=sv_[:, k, jj * 128:(jj + 1) * 128], rhs=hT[:, k, tb * 512:(tb + 1) * 512],
                                start=(k == 0), stop=(k == DC - 1)) for k in range(DC)],
                                reads=[b, b_hT[tb]], writes=[bp])
                            S_.op("act", lambda e, p_=p_, tb=tb, yi=yi: e.activation(
                                out=gst[yi][:, tb * 512:(tb + 1) * 512], in_=p_[:], func=AF.Sigmoid),
                                reads=[bp], writes=[b_gst[yi]])
                        S_.dma("sp", sgT[f_][n_ * 128:(n_ + 1) * 128, :], gst[yi][:], reads=[b_gst[yi]], owner=b_gst[yi])

                stream(len(gjobs), g_load, g_compute)

            with ExitStack() as ph:
                qT = sb(ph, "qT", [128, 2, S], BF16)
                kT = sb(ph, "kT", [128, 2, S], BF16)
                vau = sb(ph, "vau", [128, NT, 258], BF16)
                szT = sb(ph, "szT", [128, 2, S], BF16)
                E = [sb(ph, "E%d" % i, [128, 512], BF16) for i in range(5)]
                o0 = sb(ph, "o0", [128, 4, 256], F32)
                od = sb(ph, "od", [128, 4, 256], F32)
                ybn = sb(ph, "ybn", [128, 4, 256], BF16)
                jf = sb(ph, "jf", [128, 256], F32)
                rinv = sb(ph, "rinv", [128, 8], F32)
                ssq = sb(ph, "ssq", [128, 4], F32)
                rsb = sb(ph, "rsb", [128, 4], F32)
                ybst = [sb(ph, "ybst%d" % i, [128, S], BF16) for i in range(2)]
                b_q = [Buf() for _ in range(NB)]
                b_k = [Buf() for _ in range(NB)]
                b_v = [Buf() for _ in range(NT)]
                b_szT = [Buf() for _ in range(NB)]
                b_E = [Buf() for _ in range(5)]
                b_o0 = [Buf() for _ in range(4)]; b_od = [Buf() for _ in range(4)]; b_ybn = Buf(); b_ri = Buf(); b_sq = Buf()
                b_ybst = [Buf() for _ in range(2)]
                b_one = Buf()
                S_.op("dve", lambda e: e.memset(vau[:, :, 256:258], 1.0), writes=[b_one])
                ecnt = {"e": 0, "s": 0}
                qscale = 1.0 / math.sqrt(128.0)

                def proj_fm(sv_, b, off, tb):
                    p_, bp = bank()
                    S_.group("pe", [lambda e, k=k, p_=p_: e.matmul(
                        p_[:], lhsT=sv_[:, k, off:off + 128], rhs=hT[:, k, tb * 512:(tb + 1) * 512],
                        start=(k == 0), stop=(k == DC - 1)) for k in range(DC)],
                        reads=[b, b_hT[tb]], writes=[bp])
                    return p_, bp

                LOOK = 3
                NE = 5

                def attention(h):
                    tiles = [(tb, c, j) for tb in range(NB) for c in range(2) for j in range(4 * tb + 4)]
                    n = len(tiles)
                    info = {}
                    pending = []

                    def sbank():
                        i_ = 4 + ecnt["s"] % 4
                        ecnt["s"] += 1
                        return ps[i_], b_ps[i_]

                    def emit_S(idx):
                        tb, c, j = tiles[idx]
                        r0 = max(0, j - 4 * tb)
                        c0 = r0 * 128
                        off = LOFF - 128 * (j - 4 * tb)
                        p_, bp = sbank()
                        diag = j >= 4 * tb
                        use_aug = SLOPES[h] > 1.0 / 16.0 + 1e-9
                        fns = [lambda e: e.matmul(
                            p_[:, c0:512], lhsT=kT[:, c, j * 128:(j + 1) * 128],
                            rhs=qT[:, c, tb * 512 + c0:(tb + 1) * 512], start=True, stop=not (diag or use_aug))]
                        if diag:
                            fns.append(lambda e: e.matmul(
                                p_[:, c0:c0 + 128], lhsT=ident_b[:], rhs=maskT_b[:], start=False, stop=not use_aug))
                        if use_aug:
                            fns.append(lambda e: e.matmul(
                                p_[:, c0:512], lhsT=kaug[:, h * 128:(h + 1) * 128],
                                rhs=laug[:, off + c0:off + 512], start=False, stop=True))
                        S_.group("pe", fns, reads=[b_k[j // 4], b_q[tb], b_cp], writes=[bp])
                        ei = ecnt["e"] % NE
                        ecnt["e"] += 1
                        if use_aug:
                            S_.op("act", lambda e: e.activation(out=E[ei][:, c0:512], in_=p_[:, c0:512], func=AF.Exp),
                                  reads=[bp], writes=[b_E[ei]])
                        else:
                            bc = h * 16 + (j - 4 * tb) + 12
                            S_.op("act", lambda e: e.activation(out=E[ei][:, c0:512], in_=p_[:, c0:512], func=AF.Exp,
                                                                bias=btab[:, bc:bc + 1], scale=1.0),
                                  reads=[bp, b_c], writes=[b_E[ei]])
                        info[idx] = (ei, r0)

                    def emit_PV(idx):
                        tb, c, j = tiles[idx]
                        ei, r0 = info.pop(idx)
                        if j == 0:
                            for r in range(r0, 4):
                                S_.group("pe", [lambda e, r=r: e.matmul(
                                    ps[r][:, 0:257], lhsT=E[ei][:, r * 128:(r + 1) * 128], rhs=vau[:, j, 0:257],
                                    start=True, stop=(j == 4 * tb + r))],
                                    reads=[b_E[ei], b_v[j], b_one], writes=[b_ps[r]])
                        else:
                            S_.group("pe", [lambda e, r=r: e.matmul(
                                ps[r][:, 0:257], lhsT=E[ei][:, r * 128:(r + 1) * 128], rhs=vau[:, j, 0:257],
                                start=(j == 0), stop=(j == 4 * tb + r)) for r in range(r0, 4)],
                                reads=[b_E[ei], b_v[j], b_one], writes=[b_ps[r] for r in range(r0, 4)])
                        if j == 4 * tb + 3:
                            evac(idx, tb, c)

                    def evac(idx, tb, c):
                        bpo = [b_ps[r] for r in range(4)]
                        cs = slice(c * 4, c * 4 + 4)
                        S_.op("dve", lambda e: e.reciprocal(out=rinv[:, cs], in_=psall[:, 0:4, 256]),
                              reads=bpo, writes=[b_ri])
                        if c == 1:
                            S_.op("dve", lambda e: e.tensor_scalar(out=rinv[:, cs], in0=rinv[:, cs], scalar1=neglam[:, 0:1],
                                                                   scalar2=None, op0=ALU.mult), reads=[b_ri, b_l], writes=[b_ri])
                        for r in range(4):
                            ci = c * 4 + r
                            if c == 0:
                                S_.op("dve", lambda e, r=r, ci=ci: e.tensor_scalar(
                                    out=o0[:, r, :], in0=ps[r][:, 0:256], scalar1=rinv[:, ci:ci + 1], scalar2=None, op0=ALU.mult),
                                    reads=[b_ps[r], b_ri], writes=[b_o0[r]])
                            else:
                                S_.op("dve", lambda e, r=r, ci=ci: e.scalar_tensor_tensor(
                                    out=od[:, r, :], in0=ps[r][:, 0:256], scalar=rinv[:, ci:ci + 1], in1=o0[:, r, :],
                                    op0=ALU.mult, op1=ALU.add), reads=[b_ps[r], b_ri, b_o0[r]], writes=[b_od[r]])
                        if c == 0:
                            return
                        for r in range(4):
                            S_.op("dve", lambda e, r=r: e.scalar_tensor_tensor(
                                out=jf[:], in0=od[:, r, :], scalar=1.0, in1=od[:, r, :],
                                op0=ALU.mult, op1=ALU.mult, accum_out=ssq[:, r:r + 1]), reads=[b_od[r]], writes=[b_sq])
                        S_.op("dve", lambda e: e.tensor_scalar(out=ssq[:], in0=ssq[:], scalar1=1.0 / 256.0, scalar2=SUBLN_EPS,
                                                               op0=ALU.mult, op1=ALU.add), reads=[b_sq], writes=[b_sq])
                        S_.op("pool", lambda e: e.tensor_tensor(out=rsb[:], in0=ssq[:], in1=mhalf[:, 0:4], op=ALU.pow),
                              reads=[b_sq, b_small], writes=[b_sq])
                        for r in range(4):
                            S_.op("dve", lambda e, r=r: e.scalar_tensor_tensor(
                                out=ybn[:, r, :], in0=od[:, r, :], scalar=rsb[:, r:r + 1], in1=G2[:],
                                op0=ALU.mult, op1=ALU.mult), reads=[b_od[r], b_sq, b_l], writes=[b_ybn])

                        def transposes(tb=tb):
                            for e2 in range(2):
                                pt, b_pt = sbank()
                                ptb = pt.bitcast(BF16)
                                S_.group("pe", [lambda e, r=r: e.transpose(
                                    ptb[:, r * 128:(r + 1) * 128], ybn[:, r, e2 * 128:(e2 + 1) * 128], ident_b[:]) for r in range(4)],
                                    reads=[b_ybn, b_cp], writes=[b_pt])
                                S_.op("dve", lambda e: e.tensor_tensor(
                                    out=ybst[e2][:, tb * 512:(tb + 1) * 512], in0=ptb[:, 0:512], in1=szT[:, e2, tb * 512:(tb + 1) * 512],
                                    op=ALU.mult), reads=[b_pt, b_szT[tb]], writes=[b_ybst[e2]])
                        pending.append((idx + LOOK + 10, transposes))

                    for step in range(n + LOOK):
                        if step == n // 2 and h == H - 1:
                            rstate["i"] = 0
                            preload(m_load, 1)
                        if step < n:
                            emit_S(step)
                        if step - LOOK >= 0:
                            emit_PV(step - LOOK)
                        while pending and pending[0][0] <= step:
                            pending.pop(0)[1]()
                    pstate["i"] = 0

                    def tail():
                        while pending:
                            pending.pop(0)[1]()
                        for e2 in range(2):
                            r_ = h * 256 + e2 * 128
                            S_.dma("sp", ybT[r_:r_ + 128, :], ybst[e2][:], reads=[b_ybst[e2]], owner=b_ybst[e2])
                    carry.append(tail)

                carry = []

                def b_compute(i, t, b):
                    h, which = divmod(i, 2)
                    sv_ = slab_view(t)
                    if which == 0:
                        for c in range(2):
                            for tb in range(NB):
                                p_, bp = proj_fm(sv_, b, c * 128, tb)
                                S_.op("dve", lambda e, p_=p_, c=c, tb=tb: e.tensor_scalar(
                                    out=qT[:, c, tb * 512:(tb + 1) * 512], in0=p_[:], scalar1=qscale, scalar2=None, op0=ALU.mult),
                                    reads=[bp], writes=[b_q[tb]])
                            if c == 0:
                                while carry:
                                    carry.pop(0)()
                        for c in range(2):
                            for tb in range(NB):
                                p_, bp = proj_fm(sv_, b, 256 + c * 128, tb)
                                S_.op("dve", lambda e, p_=p_, c=c, tb=tb: e.tensor_copy(
                                    out=kT[:, c, tb * 512:(tb + 1) * 512], in_=p_[:]), reads=[bp], writes=[b_k[tb]])
                    else:
                        for tt in range(NT):
                            p_, bp = bank()
                            S_.group("pe", [lambda e, k=k, p_=p_, tt=tt: e.matmul(
                                p_[:, 0:256], lhsT=hT[:, k, tt * 128:(tt + 1) * 128], rhs=sv_[:, k, 0:256],
                                start=(k == 0), stop=(k == DC - 1)) for k in range(DC)],
                                reads=[b, b_hT[tt // 4]], writes=[bp])
                            S_.op("dve", lambda e, p_=p_, tt=tt: e.tensor_copy(out=vau[:, tt, 0:256], in_=p_[:, 0:256]),
                                  reads=[bp], writes=[b_v[tt]])
                        for e2 in range(2):
                            for tb in range(NB):
                                p_, bp = proj_fm(sv_, b, 256 + e2 * 128, tb)
                                S_.op("act", lambda e, p_=p_, e2=e2, tb=tb: e.activation(
                                    out=szT[:, e2, tb * 512:(tb + 1) * 512], in_=p_[:], func=AF.Silu),
                                    reads=[bp], writes=[b_szT[tb]])
                        attention(h)

                stream(2 * H, b_load, b_compute)
                while carry:
                    carry.pop(0)()
            ring.pop()
            b_ring.pop()
            rstate["i"] = 1
            st_gb.close()
        S_.barrier()

        with ExitStack() as ph:
            ya_s = sb(ph, "ya_s", [128, DC, S], BF16)
            yb_s = sb(ph, "yb_s", [128, DC, S], BF16)
            b_ya = [Buf() for _ in range(NB)]
            b_yb = [Buf() for _ in range(NB)]
            sg_s = [[sb(ph, "sg%d_%d" % (f_, i), [128, S], BF16) for i in range(2)] for f_ in range(2)]
            b_sg = [[Buf() for _ in range(2)] for _ in range(2)]
            t1 = [sb(ph, "t1_%d" % i, [128, 512], F32) for i in range(2)]
            t2 = [sb(ph, "t2_%d" % i, [128, 512], F32) for i in range(2)]
            b_t1 = [Buf() for _ in range(2)]
            b_t2 = [Buf() for _ in range(2)]
            mst = [sb(ph, "mst%d" % i, [128, S], BF16) for i in range(2)]
            b_mst = [Buf() for _ in range(2)]
            yaTv = yaT.rearrange("(k p) t -> p k t", p=128)
            ybTv = ybT.rearrange("(k p) t -> p k t", p=128)
            b_ch = [Buf("chain%d" % i) for i in range(3)]
            for tb in range(NB):
                S_.dma("sp", ya_s[:, :, tb * 512:(tb + 1) * 512], yaTv[:, :, tb * 512:(tb + 1) * 512],
                       writes=[b_ya[tb], b_ch[0]], owner=b_ya[tb])
                S_.dma("act", yb_s[:, :, tb * 512:(tb + 1) * 512], ybTv[:, :, tb * 512:(tb + 1) * 512],
                       writes=[b_yb[tb], b_ch[1]], owner=b_yb[tb])
            cnt = {"y": 0, "t": 0}

            def m_compute(i, t, b):
                sv_ = slab_view(t)
                for jj in range(2):
                    n_ = i * 2 + jj
                    for f2 in range(2):
                        S_.dma("sp", sg_s[f2][jj][:], sgT[f2][n_ * 128:(n_ + 1) * 128, :], writes=[b_sg[f2][jj]])
                for tb in range(NB):
                    for jj in range(2):
                        pa, bpa = bank()
                        S_.group("pe", [lambda e, k=k: e.matmul(
                            pa[:], lhsT=sv_[:, k, jj * 128:(jj + 1) * 128], rhs=ya_s[:, k, tb * 512:(tb + 1) * 512],
                            start=(k == 0), stop=(k == DC - 1)) for k in range(DC)],
                            reads=[b, b_ya[tb]], writes=[bpa])
                        pb, bpb = bank()
                        S_.group("pe", [lambda e, k=k: e.matmul(
                            pb[:], lhsT=sv_[:, k, 256 + jj * 128:256 + (jj + 1) * 128], rhs=yb_s[:, k, tb * 512:(tb + 1) * 512],
                            start=(k == 0), stop=(k == DC - 1)) for k in range(DC)],
                            reads=[b, b_yb[tb]], writes=[bpb])
                        ti = cnt["t"] % 2
                        cnt["t"] += 1
                        S_.op("dve", lambda e: e.tensor_tensor(
                            out=t1[ti][:], in0=pa[:], in1=sg_s[0][jj][:, tb * 512:(tb + 1) * 512], op=ALU.mult),
                            reads=[bpa, b_sg[0][jj]], writes=[b_t1[ti]])
                        S_.op("dve", lambda e: e.tensor_tensor(
                            out=t2[ti][:], in0=pb[:], in1=sg_s[1][jj][:, tb * 512:(tb + 1) * 512], op=ALU.mult),
                            reads=[bpb, b_sg[1][jj]], writes=[b_t2[ti]])
                        S_.op("pool", lambda e: e.tensor_tensor(
                            out=mst[jj][:, tb * 512:(tb + 1) * 512], in0=t1[ti][:], in1=t2[ti][:], op=ALU.add),
                            reads=[b_t1[ti], b_t2[ti]], writes=[b_mst[jj]])
                for jj in range(2):
                    n_ = i * 2 + jj
                    S_.dma("sp", mT[n_ * 128:(n_ + 1) * 128, :], mst[jj][:], reads=[b_mst[jj]], owner=b_mst[jj])

            stream(D // 256, m_load, m_compute)
        S_.barrier()

        with ExitStack() as ph:
            m_s = sb(ph, "m_s", [128, DC, S], BF16)
            wo_s = sb(ph, "wo_s", [128, DC, D], BF16)
            b_m = [Buf() for _ in range(NB)]
            b_wo = [Buf() for _ in range(ND)]
            gate_bc = sb(ph, "gate_bc", [128, D], F32)
            fng_bc = sb(ph, "fng_bc", [128, D], F32)
            xo = [sb(ph, "xo%d" % i, [128, D], F32)[:] for i in range(2)]
            for rt in ring:
                rf = rt[:].bitcast(F32)
                for q_ in range(min(2, 4096 // D)):
                    xo.append(rf[:, q_ * D:(q_ + 1) * D])
            NXO = len(xo)
            b_xo = [Buf() for _ in range(NXO)]
            tm = [sb(ph, "tm%d" % i, [128, 512], F32) for i in range(2)]
            b_tm = [Buf() for _ in range(2)]
            dg = sb(ph, "dg", [128, 128], F32)
            ones2 = sb(ph, "ones2", [128, 128], F32)
            s2 = sb(ph, "s2", [128, NT], F32)
            r2 = sb(ph, "r2", [128, NT], F32)
            b_dg = Buf(); b_gb = Buf(); b_s2 = Buf(); b_fg = Buf()
            mTv = mT.rearrange("(k p) t -> p k t", p=128)
            b_ch2 = [Buf("ochain%d" % i) for i in range(2)]
            for cb in range(ND):
                S_.dma("pool", wo_s[:, :, cb * 512:(cb + 1) * 512], wsrc(w_o, cb * 512, 512),
                       writes=[b_wo[cb], b_ch2[0]], owner=b_wo[cb])
            for tb in range(NB):
                S_.dma("act", m_s[:, :, tb * 512:(tb + 1) * 512], mTv[:, :, tb * 512:(tb + 1) * 512],
                       writes=[b_m[tb], b_ch2[1]], owner=b_m[tb])
            S_.dma("sp", fng_bc[:], fng.partition_broadcast(128), writes=[b_fg])
            S_.op("dve", lambda e: e.memset(ones2[:], 1.0), writes=[b_dg])
            for k in range(DC):
                S_.op("dve", lambda e, k=k: e.tensor_scalar(out=dg[:], in0=ident_f[:], scalar1=modT[:, 2 * DC + k:2 * DC + k + 1],
                                                            scalar2=None, op0=ALU.mult), reads=[b_c, b_mod2], writes=[b_dg])
                if k % 4 == 0:
                    pg, b_pg = bank()
                S_.group("pe", [lambda e, k=k, pg=pg: e.matmul(pg[:, (k % 4) * 128:(k % 4 + 1) * 128], lhsT=ones2[:], rhs=dg[:],
                                                              start=True, stop=True)], reads=[b_dg], writes=[b_pg])
                if k % 4 == 3:
                    kb = k // 4
                    S_.op("dve", lambda e, pg=pg, kb=kb: e.tensor_copy(out=gate_bc[:, kb * 512:(kb + 1) * 512], in_=pg[:]),
                          reads=[b_pg], writes=[b_gb])
            XA = min(3, NXO - 2)

            def load_xo(tt):
                S_.dma("sp", xo[tt % NXO][:], x[tt * 128:(tt + 1) * 128, :], writes=[b_xo[tt % NXO]])

            def epilogue(tt):
                xi = tt % NXO
                S_.op("act", lambda e: e.activation(out=junk[:], in_=xo[xi][:], func=AF.Square,
                                                    accum_out=s2[:, tt:tt + 1]), reads=[b_xo[xi]], writes=[b_s2])
                S_.op("dve", lambda e: e.tensor_scalar(out=s2[:, tt:tt + 1], in0=s2[:, tt:tt + 1], scalar1=1.0 / D, scalar2=EPS,
                                                       op0=ALU.mult, op1=ALU.add), reads=[b_s2], writes=[b_s2])
                S_.op("pool", lambda e: e.tensor_tensor(out=r2[:, tt:tt + 1], in0=s2[:, tt:tt + 1], in1=mhalf[:, 0:1], op=ALU.pow),
                      reads=[b_s2, b_small], writes=[b_s2])
                S_.op("dve", lambda e: e.scalar_tensor_tensor(
                    out=xo[xi][:], in0=xo[xi][:], scalar=r2[:, tt:tt + 1], in1=fng_bc[:], op0=ALU.mult, op1=ALU.mult),
                    reads=[b_s2, b_fg], writes=[b_xo[xi]])
                S_.dma("sp", y[tt * 128:(tt + 1) * 128, :], xo[xi][:], reads=[b_xo[xi]], owner=b_xo[xi])

            for tt in range(min(XA, NT)):
                load_xo(tt)
            for tt in range(NT):
                xi = tt % NXO
                if tt + XA < NT:
                    load_xo(tt + XA)
                for cb in range(ND):
                    p_, bp = bank()
                    S_.group("pe", [lambda e, k=k: e.matmul(
                        p_[:], lhsT=m_s[:, k, tt * 128:(tt + 1) * 128], rhs=wo_s[:, k, cb * 512:(cb + 1) * 512],
                        start=(k == 0), stop=(k == DC - 1)) for k in range(DC)],
                        reads=[b_m[tt // 4], b_wo[cb]], writes=[bp])
                    ti = (tt * ND + cb) % 2
                    S_.op("dve", lambda e: e.tensor_tensor(
                        out=tm[ti][:], in0=p_[:], in1=gate_bc[:, cb * 512:(cb + 1) * 512], op=ALU.mult),
                        reads=[bp, b_gb], writes=[b_tm[ti]])
                    S_.op("pool", lambda e: e.tensor_tensor(
                        out=xo[xi][:, cb * 512:(cb + 1) * 512], in0=xo[xi][:, cb * 512:(cb + 1) * 512], in1=tm[ti][:], op=ALU.add),
                        reads=[b_tm[ti]], writes=[b_xo[xi]])
                if tt >= 1:
                    epilogue(tt - 1)
            epilogue(NT - 1)
        S_.barrier(engines=("sp", "pe", "act", "dve", "pool"))
        build_program.stats = (S_.ninst, S_.nwait, len(S_.sems))
    return nc


def _consts(H):
    ident = np.eye(128, dtype=np.float32)
    tril = np.tril(np.ones((128, 128), dtype=np.float32))
    s_i = np.arange(128)[:, None]
    t_i = np.arange(128)[None, :]
    maskT = np.where(s_i <= t_i, 0.0, NEG).astype(np.float32)
    idx = np.arange(LLEN)
    N = LOFF - idx
    a = np.floor_divide(N, 256)
    b = N - 256 * a
    laug = np.zeros((128, LLEN), dtype=np.float32)
    laug[0] = 1.0
    laug[1] = 256.0 * a
    laug[2] = b
    slopes = [2.0 ** (-8.0 * (i + 1) / H) for i in range(H)]
    jv = np.arange(128, dtype=np.float64)
    kaug = np.zeros((128, H * 128), dtype=np.float32)
    for hh, sl in enumerate(slopes):
        kaug[0, hh * 128:(hh + 1) * 128] = sl * jv
        kaug[1, hh * 128:(hh + 1) * 128] = sl
        kaug[2, hh * 128:(hh + 1) * 128] = sl
    btab = np.zeros((128, H * 16), dtype=np.float32)
    for hh, sl in enumerate(slopes):
        for dl in range(-12, 4):
            btab[:, hh * 16 + dl + 12] = sl * (jv + 128.0 * dl)
    return ident, tril, maskT, laug, kaug, btab


def make_in_maps(inp, S, D):
    B = inp["x"].shape[0]
    DC = D // 128
    H = D // 256
    f = lambda a: np.ascontiguousarray(np.asarray(a, dtype=np.float32))
    colT = lambda v: f(np.asarray(v).reshape(-1, 128).T)
    ident, tril, maskT, laug, kaug, btab = _consts(H)
    shared = {
        "w_ada": f(inp["w_ada"][0]), "b_adaT": colT(inp["b_ada"][0]), "ngT": colT(inp["norm_gain"][0]),
        "w_in": f(inp["w_in"][0]), "lngT": colT(inp["ln_v_gain"][0]), "lnbT": colT(inp["ln_v_bias"][0]),
        "w_sp": f(inp["w_spatial"][0]), "b_sp": f(np.asarray(inp["b_spatial"][0]).reshape(1, -1)),
        "lamv": f(np.concatenate([np.asarray(inp[k][0]).reshape(-1) for k in
                                  ("lambda_q1", "lambda_k1", "lambda_q2", "lambda_k2")]).reshape(1, -1)),
        "sublng": f(np.asarray(inp["subln_gain"][0]).reshape(1, -1)),
        "w_a": f(inp["w_branch_a"][0]), "w_b": f(inp["w_branch_b"][0]), "w_o": f(inp["w_out"][0]),
        "fng": f(np.asarray(inp["final_norm_gain"]).reshape(1, -1)),
        "c_ident": ident, "c_tril": tril, "c_maskT": maskT, "c_laug": laug, "c_kaug": kaug, "c_btab": btab,
    }
    maps = []
    for b in range(B):
        m = dict(shared)
        m["x"] = f(inp["x"][b])
        m["cT"] = colT(inp["c"][b])
        maps.append(m)
    return maps


_CACHE = {}


def kernel(**inputs):
    x = np.asarray(inputs["x"])
    B, S, D = x.shape
    key = (S, D)
    if key not in _CACHE:
        _CACHE[key] = build_program(S, D)
    nc = _CACHE[key]
    in_maps = make_in_maps(inputs, S, D)
    res = run_bass_kernel_spmd(nc, in_maps, core_ids=list(range(B)))
    return np.stack([np.asarray(r["y"], dtype=np.float32) for r in res.results], axis=0)
```

```python
import math
from contextlib import ExitStack

import numpy as np
import concourse.bass as bass
import concourse.mybir as mybir
from concourse.bass_utils import run_bass_kernel_spmd

F32 = mybir.dt.float32
BF16 = mybir.dt.bfloat16
AF = mybir.ActivationFunctionType
ALU = mybir.AluOpType
AX = mybir.AxisListType

LAM_INIT = 0.8 - 0.6 * math.exp(-0.3 * 0)
EPS = 1e-6
SUBLN_EPS = 1e-5
NEG = -30000.0
LOFF = 384
LLEN = 2432


class Buf:
    _n = 0

    def __init__(self, name=""):
        Buf._n += 1
        self.id = Buf._n
        self.name = name
        self.w = None
        self.r = []
        self.dma_sem = None
        self.dma_cnt = 0


class Sch:
    def __init__(self, nc, stack):
        self.nc = nc
        self.stack = stack
        self.eng = {"pe": nc.tensor, "act": nc.scalar, "dve": nc.vector,
                    "pool": nc.gpsimd, "sp": nc.sync}
        self.sems = {}
        self.cnt = {}
        for e in ("pe", "act", "dve", "pool"):
            self.sems[e] = stack.enter_context(nc.semaphore("c_" + e))
            self.cnt[e] = 0
        self.seen = {e: {} for e in self.eng}
        self.owners = []
        self.ninst = 0
        self.nwait = 0

    def _deps(self, reads, writes):
        d = {}

        def add(ev):
            if ev is None:
                return
            k, v = ev
            if d.get(k, -1) < v:
                d[k] = v
        for b in reads:
            add(b.w)
        for b in writes:
            add(b.w)
            for ev in b.r:
                add(ev)
        return d

    def _wait(self, e, deps):
        eng = self.eng[e]
        seen = self.seen[e]
        for k, v in deps.items():
            if k == e and e == "pe":
                continue
            if seen.get(k, 0) >= v:
                continue
            eng.wait_ge(self.sems[k], v)
            seen[k] = v
            self.nwait += 1

    def _record(self, ev, reads, writes):
        for b in reads:
            b.r.append(ev)
            if len(b.r) > 64:
                best = {}
                for k, v in b.r:
                    if best.get(k, -1) < v:
                        best[k] = v
                b.r = list(best.items())
        for b in writes:
            b.w = ev
            b.r = []

    def op(self, e, fn, reads=(), writes=()):
        self._wait(e, self._deps(reads, writes))
        ins = fn(self.eng[e])
        self.cnt[e] += 1
        ins.then_inc(self.sems[e], 1)
        ev = (e, self.cnt[e])
        self._record(ev, reads, writes)
        self.ninst += 1
        return ev

    def group(self, e, fns, reads=(), writes=()):
        self._wait(e, self._deps(reads, writes))
        ins = None
        for fn in fns:
            ins = fn(self.eng[e])
            self.ninst += 1
        self.cnt[e] += 1
        ins.then_inc(self.sems[e], 1)
        ev = (e, self.cnt[e])
        self._record(ev, reads, writes)
        return ev

    def dma(self, q, out, in_, reads=(), writes=(), owner=None, **kw):
        if owner is None:
            owner = writes[0] if writes else reads[0]
        if owner.dma_sem is None:
            owner.dma_sem = self.stack.enter_context(self.nc.semaphore("d_%d" % owner.id))
            self.sems[("d", owner.id)] = owner.dma_sem
            self.owners.append(owner)
        self._wait(q, self._deps(reads, writes))
        ins = self.eng[q].dma_start(out=out, in_=in_, **kw)
        owner.dma_cnt += 1
        ins.then_inc(owner.dma_sem, 16)
        ev = (("d", owner.id), 16 * owner.dma_cnt)
        self._record(ev, reads, writes)
        self.ninst += 1
        return ev

    def barrier(self, engines=("pe", "act", "dve", "pool", "sp")):
        d = {}
        for e in ("pe", "act", "dve", "pool"):
            if self.cnt[e] > 0:
                d[e] = self.cnt[e]
        for o in self.owners:
            d[("d", o.id)] = 16 * o.dma_cnt
        for e in engines:
            eng = self.eng[e]
            seen = self.seen[e]
            for k, v in d.items():
                if k == e:
                    continue
                if seen.get(k, 0) >= v:
                    continue
                eng.wait_ge(self.sems[k], v)
                seen[k] = v
                self.nwait += 1


def build_program(S, D, debug=False):
    NT = S // 128
    NB = S // 512
    DC = D // 128
    ND = D // 512
    H = D // 256
    G = D // 256
    SLOPES = [2.0 ** (-8.0 * (i + 1) / H) for i in range(H)]
    assert S % 512 == 0 and D % 512 == 0 and NB <= 4
    nc = bass.Bass("TRN2", target_bir_lowering=False)

    def din(name, shape):
        return nc.dram_tensor(name, list(shape), F32, kind="ExternalInput").ap()

    x = din("x", [S, D])
    cT = din("cT", [128, DC])
    w_ada = din("w_ada", [D, 3 * D])
    b_adaT = din("b_adaT", [128, 3 * DC])
    ngT = din("ngT", [128, DC])
    w_in = din("w_in", [D, 9 * D])
    lngT = din("lngT", [128, DC])
    lnbT = din("lnbT", [128, DC])
    w_sp = din("w_sp", [G, 128, 128])
    b_sp = din("b_sp", [1, G * 128])
    lamv = din("lamv", [1, 4 * 128])
    sublng = din("sublng", [1, 256])
    w_a = din("w_a", [D, D])
    w_b = din("w_b", [D, D])
    w_o = din("w_o", [D, D])
    fng = din("fng", [1, D])
    c_ident = din("c_ident", [128, 128])
    c_tril = din("c_tril", [128, 128])
    c_maskT = din("c_maskT", [128, 128])
    c_laug = din("c_laug", [128, LLEN])
    c_kaug = din("c_kaug", [128, H * 128])
    c_btab = din("c_btab", [128, H * 16])
    y = nc.dram_tensor("y", [S, D], F32, kind="ExternalOutput").ap()
    skind = "ExternalOutput" if debug else "Internal"
    yaT = nc.dram_tensor("yaT", [D, S], BF16, kind=skind).ap()
    ybT = nc.dram_tensor("ybT", [D, S], BF16, kind=skind).ap()
    sgT = [nc.dram_tensor("sgT%d" % i, [D, S], BF16, kind=skind).ap() for i in range(2)]
    mT = nc.dram_tensor("mT", [D, S], BF16, kind=skind).ap()

    with ExitStack() as g:
        S_ = Sch(nc, g)
        sb = lambda st, name, shape, dt: st.enter_context(nc.sbuf_tensor(name, list(shape), dt))

        psall = g.enter_context(nc.psum_tensor("psall", [128, 8, 512], F32))
        ps = [psall[:, i, :] for i in range(8)]
        b_ps = [Buf("ps%d" % i) for i in range(8)]
        pstate = {"i": 0}

        def bank():
            i = pstate["i"]
            pstate["i"] = (i + 1) % 8
            return ps[i], b_ps[i]

        ring = [sb(g, "ring%d" % i, [128, 8192], BF16) for i in range(2)]
        b_ring = [Buf("ring%d" % i) for i in range(2)]
        rstate = {"i": 0}

        def ring_next():
            i = rstate["i"]
            rstate["i"] = (i + 1) % len(ring)
            return ring[i], b_ring[i]

        def slab_view(t):
            return t[:, 0:DC * 512].rearrange("p (k n) -> p k n", k=DC)

        def wsrc(w, c0, n):
            return w[:, c0:c0 + n].rearrange("(k p) n -> p k n", p=128)

        pre = {"slots": []}

        def preload(load, count=1):
            for i in range(count):
                sl = ring_next()
                load(i, *sl)
                pre["slots"].append(sl)

        def stream(n, load, compute):
            depth = len(ring) - 1
            slots = {}
            nxt = 0
            for sl in pre["slots"]:
                slots[nxt] = sl
                nxt += 1
            pre["slots"] = []
            for i in range(n):
                while nxt < n and nxt <= i + depth:
                    slots[nxt] = ring_next()
                    load(nxt, *slots[nxt])
                    nxt += 1
                compute(i, *slots[i])
                del slots[i]

        def ada_load(i, t, b):
            S_.dma("pool", slab_view(t), wsrc(w_ada, i * 512, 512), writes=[b])

        def v_load(i, t, b):
            S_.dma("pool", slab_view(t), wsrc(w_in, D + i * 512, 512), writes=[b])

        def a_load(i, t, b):
            sv_ = slab_view(t)
            S_.dma("pool", sv_[:, :, 0:256], wsrc(w_in, i * 256, 256), writes=[b])
            S_.dma("pool", sv_[:, :, 256:512], wsrc(w_in, 2 * D + i * 256, 256), writes=[b])

        gjobs = []
        for i in range(2 * ND):
            gjobs.append(("g", i))
            if i % 2 == 1:
                gjobs.append(("ada", 2 * ND + i // 2))

        def g_load(i, t, b):
            kind, i = gjobs[i]
            if kind == "ada":
                return ada_load(i, t, b)
            f_, cb = divmod(i, ND)
            S_.dma("pool", slab_view(t), wsrc(w_in, (7 + f_) * D + cb * 512, 512), writes=[b])

        def b_load(i, t, b):
            h, which = divmod(i, 2)
            sv_ = slab_view(t)
            f0 = 3 if which == 0 else 5
            S_.dma("pool", sv_[:, :, 0:256], wsrc(w_in, f0 * D + h * 256, 256), writes=[b])
            S_.dma("pool", sv_[:, :, 256:512], wsrc(w_in, (f0 + 1) * D + h * 256, 256), writes=[b])

        def m_load(i, t, b):
            sv_ = slab_view(t)
            S_.dma("pool", sv_[:, :, 0:256], wsrc(w_a, i * 256, 256), writes=[b])
            S_.dma("pool", sv_[:, :, 256:512], wsrc(w_b, i * 256, 256), writes=[b])

        b_c = Buf("consts")
        ident_f = sb(g, "ident_f", [128, 128], F32)
        junk = sb(g, "junk", [128, D], BF16)
        modT = sb(g, "modT", [128, 3 * DC], F32)
        Acoef = sb(g, "Acoef", [128, DC], F32)
        mhalf = sb(g, "mhalf", [128, 16], F32)
        b_small = Buf("small")
        S_.dma("sp", ident_f[:], c_ident, writes=[b_c])
        S_.op("dve", lambda e: e.memset(mhalf[:], -0.5), writes=[b_small])

        b_mod = Buf("mod")
        b_mod2 = Buf("mod_gate")
        b_A = Buf("Acoef")

        with ExitStack() as st_h:
            ident_b = sb(st_h, "ident_b", [128, 128], BF16)
            maskT_b = sb(st_h, "maskT_b", [128, 128], BF16)
            laug = sb(st_h, "laug", [128, LLEN], BF16)
            kaug = sb(st_h, "kaug", [128, H * 128], BF16)
            C2T = sb(st_h, "C2T", [128, DC, 128], F32)
            wsT = sb(st_h, "wsT", [128, G, 128], BF16)
            G2 = sb(st_h, "G2", [128, 256], F32)
            lng_s = sb(st_h, "lng_s", [128, DC], F32)
            lnb_s = sb(st_h, "lnb_s", [128, DC], F32)
            neglam = sb(st_h, "neglam", [128, 1], F32)
            btab = sb(st_h, "btab", [128, H * 16], F32)
            S_.dma("sp", btab[:], c_btab, writes=[b_c])
            b_cp = Buf("consts_pool")
            S_.dma("pool", ident_b[:], c_ident, writes=[b_cp])
            S_.dma("pool", maskT_b[:], c_maskT, writes=[b_cp])
            S_.dma("pool", laug[:], c_laug, writes=[b_cp])
            S_.dma("pool", kaug[:], c_kaug, writes=[b_cp])
            S_.dma("sp", lng_s[:], lngT, writes=[b_c])
            S_.dma("sp", lnb_s[:], lnbT, writes=[b_c])
            scb = sb(st_h, "scb", [128, DC], BF16)
            badT = sb(st_h, "badT", [128, 3 * DC], F32)
            hT = sb(st_h, "hT", [128, DC, S], BF16)
            b_hT = [Buf("hT%d" % i) for i in range(NB)]

            with ExitStack() as ph:
                cs = sb(ph, "cs", [128, DC], F32)
                ng_s = sb(ph, "ng_s", [128, DC], F32)
                lvec = sb(ph, "lvec", [128, 4 * 128], F32)
                lj = sb(ph, "lj", [128, 128], F32)
                ld = sb(ph, "ld", [128, 4], F32)
                wtmp = sb(ph, "wtmp", [128, 128], F32)
                wsTf = sb(ph, "wsTf", [128, 128], F32)
                tril = sb(ph, "tril", [128, 128], F32)
                ones_f = sb(ph, "ones_f", [128, 128], F32)
                bs_row = sb(ph, "bs_row", [1, 128], F32)
                b_bsr = Buf("bs_row")
                BSb = sb(ph, "BSb", [128, 128], F32)
                NXT = 2
                xt = [sb(ph, "xt%d" % i, [128, D], F32) for i in range(NXT)]
                xs_all = sb(ph, "xs_all", [128, NT, D], BF16)
                ss = sb(ph, "ss", [128, NT], F32)
                rstd = sb(ph, "rstd", [128, NT], F32)
                b_xt = [Buf("xt%d" % i) for i in range(NXT)]
                b_xs = [Buf("xs%d" % i) for i in range(NT)]
                b_p0 = Buf("p0c")
                b_sc = Buf("sc")
                b_ss = Buf("ss")

                S_.dma("sp", cs[:], cT, writes=[b_p0])
                S_.dma("sp", badT[:], b_adaT, writes=[b_p0])
                S_.dma("sp", ng_s[:], ngT, writes=[b_p0])
                S_.dma("sp", lvec[:], lamv.partition_broadcast(128), writes=[b_p0])
                S_.dma("sp", tril[:], c_tril, writes=[b_p0])
                S_.dma("sp", G2[:], sublng.partition_broadcast(128), writes=[b_p0])
                S_.op("act", lambda e: e.activation(out=scb[:], in_=cs[:], func=AF.Silu),
                      reads=[b_p0], writes=[b_sc])
                S_.op("dve", lambda e: e.memset(ones_f[:], 1.0), writes=[b_small])

                def load_xt(tt):
                    S_.dma("sp", xt[tt % NXT][:], x[tt * 128:(tt + 1) * 128, :], writes=[b_xt[tt % NXT]])
                for tt in range(min(NXT, NT)):
                    load_xt(tt)

                def p1a(tt):
                    xb, bx = xt[tt % NXT], b_xt[tt % NXT]
                    S_.op("act", lambda e: e.activation(out=junk[:], in_=xb[:], func=AF.Square, accum_out=ss[:, tt:tt + 1]),
                          reads=[bx], writes=[b_ss])
                    S_.op("dve", lambda e: e.tensor_scalar(out=ss[:, tt:tt + 1], in0=ss[:, tt:tt + 1], scalar1=1.0 / D, scalar2=EPS,
                                                           op0=ALU.mult, op1=ALU.add), reads=[b_ss], writes=[b_ss])
                    S_.op("pool", lambda e: e.tensor_tensor(out=rstd[:, tt:tt + 1], in0=ss[:, tt:tt + 1], in1=mhalf[:, 0:1], op=ALU.pow),
                          reads=[b_ss, b_small], writes=[b_ss])
                    S_.op("dve", lambda e: e.tensor_scalar(out=xs_all[:, tt, :], in0=xb[:], scalar1=rstd[:, tt:tt + 1],
                                                           scalar2=None, op0=ALU.mult), reads=[bx, b_ss], writes=[b_xs[tt]])
                    if tt + NXT < NT:
                        load_xt(tt + NXT)
                p1a_state = {"tt": 0}
                p1a_per_job = -(-NT // (2 * ND))

                def ada_compute_into(bmod, with_p1a=False):
                    def ada_compute(i, t, b):
                        sv_ = slab_view(t)
                        pm, b_pm = bank()
                        fns = []
                        for jj in range(4):
                            for k in range(DC):
                                fns.append(lambda e, jj=jj, k=k: e.matmul(
                                    pm[:, jj:jj + 1], lhsT=sv_[:, k, jj * 128:(jj + 1) * 128],
                                    rhs=scb[:, k:k + 1], start=(k == 0), stop=(k == DC - 1)))
                        S_.group("pe", fns, reads=[b, b_sc], writes=[b_pm])
                        S_.op("dve", lambda e: e.tensor_tensor(out=modT[:, i * 4:(i + 1) * 4], in0=pm[:, 0:4],
                                                               in1=badT[:, i * 4:(i + 1) * 4], op=ALU.add),
                              reads=[b_pm, b_p0], writes=[bmod])
                        if with_p1a:
                            for _ in range(p1a_per_job):
                                if p1a_state["tt"] < NT:
                                    p1a(p1a_state["tt"])
                                    p1a_state["tt"] += 1
                    return ada_compute

                b_l = Buf("lam")
                for i in range(2):
                    S_.op("dve", lambda e, i=i: e.scalar_tensor_tensor(
                        out=lj[:], in0=lvec[:, (2 * i) * 128:(2 * i + 1) * 128], scalar=1.0,
                        in1=lvec[:, (2 * i + 1) * 128:(2 * i + 2) * 128],
                        op0=ALU.mult, op1=ALU.mult, accum_out=ld[:, i:i + 1]), reads=[b_p0], writes=[b_l])
                S_.op("act", lambda e: e.activation(out=ld[:, 2:4], in_=ld[:, 0:2], func=AF.Exp),
                      reads=[b_l], writes=[b_l])
                S_.op("dve", lambda e: e.tensor_tensor(out=neglam[:], in0=ld[:, 3:4], in1=ld[:, 2:3], op=ALU.subtract),
                      reads=[b_l], writes=[b_l])
                S_.op("dve", lambda e: e.tensor_scalar(out=neglam[:], in0=neglam[:], scalar1=-LAM_INIT, scalar2=None,
                                                       op0=ALU.add), reads=[b_l], writes=[b_l])
                S_.op("dve", lambda e: e.tensor_scalar(out=G2[:], in0=G2[:], scalar1=1.0 - LAM_INIT, scalar2=None,
                                                       op0=ALU.mult), reads=[b_p0, b_l], writes=[b_l])

                b_w = Buf("wsp")
                b_ws = Buf("wsT")
                for gi in range(G):
                    S_.dma("sp", wtmp[:], w_sp[gi], reads=[], writes=[b_w])
                    S_.op("dve", lambda e: e.tensor_tensor(out=wtmp[:], in0=wtmp[:], in1=tril[:], op=ALU.mult),
                          reads=[b_p0], writes=[b_w])
                    pt, b_pt = bank()
                    S_.group("pe", [lambda e, pt=pt: e.transpose(pt[:, 0:128], wtmp[:], ident_f[:])],
                             reads=[b_w, b_c], writes=[b_pt])
                    S_.op("dve", lambda e, pt=pt: e.tensor_copy(out=wsTf[:], in_=pt[:, 0:128]),
                          reads=[b_pt], writes=[b_ws])
                    S_.op("act", lambda e, pt=pt, gi=gi: e.activation(out=wsT[:, gi, :], in_=pt[:, 0:128], func=AF.Copy),
                          reads=[b_pt], writes=[b_ws])
                    pr, b_pr = bank()
                    S_.dma("sp", bs_row[:], b_sp[0:1, gi * 128:(gi + 1) * 128], writes=[b_bsr])
                    S_.group("pe", [lambda e, pr=pr: e.matmul(pr[:, 0:128], lhsT=ones_f[:], rhs=wsTf[:], start=True, stop=True),
                                    lambda e, pr=pr, gi=gi: e.matmul(pr[:, 128:256], lhsT=ones_f[0:1, :],
                                                                    rhs=bs_row[0:1, :],
                                                                    start=True, stop=True)],
                             reads=[b_ws, b_small, b_p0, b_bsr], writes=[b_pr])
                    S_.op("dve", lambda e, pr=pr: e.tensor_copy(out=BSb[:], in_=pr[:, 128:256]),
                          reads=[b_pr], writes=[b_w])
                    for ci in range(2):
                        c = gi * 2 + ci
                        S_.op("dve", lambda e, pr=pr, c=c: e.scalar_tensor_tensor(
                            out=C2T[:, c, :], in0=pr[:, 0:128], scalar=lnb_s[:, c:c + 1], in1=BSb[:],
                            op0=ALU.mult, op1=ALU.add), reads=[b_pr, b_w, b_c], writes=[b_ws])

                ada_order = []
                for q in range(ND):
                    ada_order += [q, ND + q]
                b_Aq = [Buf("A%d" % q) for q in range(ND)]
                psb = psall[:].bitcast(BF16)
                p1a_per_job = -(-NT // 2)

                def p1b_quarter(q):
                    for k in range(4 * q, 4 * q + 4):
                        for tb in range(NB):
                            bi_ = pstate["i"]
                            pt, b_pt = bank()
                            ptb = psb[:, bi_, :]
                            S_.group("pe", [lambda e, r=r: e.transpose(ptb[:, r * 128:(r + 1) * 128],
                                                                       xs_all[:, tb * 4 + r, k * 128:(k + 1) * 128], ident_b[:])
                                            for r in range(4)],
                                     reads=[b_xs[tb * 4 + r] for r in range(4)] + [b_cp], writes=[b_pt])
                            if (k + tb) % 2 == 0:
                                S_.op("dve", lambda e: e.tensor_scalar(
                                    out=hT[:, k, tb * 512:(tb + 1) * 512], in0=ptb[:, 0:512], scalar1=Acoef[:, k:k + 1],
                                    scalar2=modT[:, k:k + 1], op0=ALU.mult, op1=ALU.add),
                                    reads=[b_pt, b_Aq[q]], writes=[b_hT[tb]])
                            else:
                                S_.op("act", lambda e: e.activation(
                                    out=hT[:, k, tb * 512:(tb + 1) * 512], in_=ptb[:, 0:512], func=AF.Identity,
                                    scale=Acoef[:, k:k + 1], bias=modT[:, k:k + 1]),
                                    reads=[b_pt, b_Aq[q]], writes=[b_hT[tb]])

                p1b_done = {"q": 0}

                def ada_job_load(i, t, b):
                    ada_load(ada_order[i], t, b)

                def ada_job_compute(i, t, b):
                    q = i // 2
                    ada_compute_into(b_mod, with_p1a=True)(ada_order[i], t, b)
                    if i % 2 == 1:
                        sl = slice(4 * q, 4 * q + 4)
                        S_.op("dve", lambda e: e.scalar_tensor_tensor(out=Acoef[:, sl], in0=modT[:, DC + 4 * q:DC + 4 * q + 4],
                                                                      scalar=1.0, in1=ng_s[:, sl], op0=ALU.add, op1=ALU.mult),
                              reads=[b_mod, b_p0], writes=[b_Aq[q]])
                    last = (i == 2 * ND - 1)
                    if last:
                        while p1a_state["tt"] < NT:
                            p1a(p1a_state["tt"])
                            p1a_state["tt"] += 1
                        preload(v_load)
                    while p1b_done["q"] < ND and (2 * p1b_done["q"] + 3 <= i or last):
                        p1b_quarter(p1b_done["q"])
                        p1b_done["q"] += 1

                stream(2 * ND, ada_job_load, ada_job_compute)
            S_.barrier()

            with ExitStack() as ph:
                XY = sb(ph, "XY", [128, NT, D], BF16)
                b_xy = [Buf("xy%d" % i) for i in range(NT)]
                gsum = sb(ph, "gsum", [128, NT * ND], F32)
                gsq = sb(ph, "gsq", [128, NT], F32)
                gsq2 = sb(ph, "gsq2", [128, NT * ND], F32)
                mean = sb(ph, "mean", [128, NT], F32)
                var = sb(ph, "var", [128, NT], F32)
                rs2 = sb(ph, "rs2", [128, NT], F32)
                nb2 = sb(ph, "nb2", [128, NT], F32)
                gu = [sb(ph, "gu%d" % i, [128, 512], F32) for i in range(2)]
                sz = [sb(ph, "sz%d" % i, [128, 512], F32) for i in range(2)]
                svb = [sb(ph, "svb%d" % i, [128, 512], F32) for i in range(2)]
                yst = [sb(ph, "yst%d" % i, [128, S], BF16) for i in range(2)]
                b_gu = [Buf() for _ in range(2)]
                b_sz = [Buf() for _ in range(2)]
                b_svb = [Buf() for _ in range(2)]
                b_yst = [Buf() for _ in range(2)]
                b_st = Buf("stats")

                def v_compute(i, t, b):
                    sv_ = slab_view(t)
                    for tt in range(NT):
                        p_, bp = bank()
                        S_.group("pe", [lambda e, k=k, tt=tt, p_=p_: e.matmul(
                            p_[:], lhsT=hT[:, k, tt * 128:(tt + 1) * 128], rhs=sv_[:, k, :],
                            start=(k == 0), stop=(k == DC - 1)) for k in range(DC)],
                            reads=[b, b_hT[tt // 4]], writes=[bp])
                        S_.op("act", lambda e, tt=tt, p_=p_, i=i: e.activation(
                            out=XY[:, tt, i * 512:(i + 1) * 512], in_=p_[:], func=AF.Gelu_apprx_tanh,
                            accum_out=gsum[:, tt * ND + i:tt * ND + i + 1]),
                            reads=[bp], writes=[b_xy[tt], b_st])
                        S_.op("act", lambda e, tt=tt, i=i: e.activation(
                            out=junk[:, 0:512], in_=XY[:, tt, i * 512:(i + 1) * 512], func=AF.Square,
                            accum_out=gsq2[:, tt * ND + i:tt * ND + i + 1]),
                            reads=[b_xy[tt]], writes=[b_st])

                stream(ND, v_load, v_compute)
                preload(a_load, 1)
                S_.op("dve", lambda e: e.tensor_reduce(out=gsq[:], in_=gsq2[:].rearrange("p (t c) -> p t c", c=ND),
                                                       axis=AX.X, op=ALU.add), reads=[b_st], writes=[b_st])
                S_.op("dve", lambda e: e.tensor_reduce(out=mean[:], in_=gsum[:].rearrange("p (t c) -> p t c", c=ND),
                                                       axis=AX.X, op=ALU.add), reads=[b_st], writes=[b_st])
                S_.op("dve", lambda e: e.tensor_scalar(out=mean[:], in0=mean[:], scalar1=1.0 / D, scalar2=None, op0=ALU.mult),
                      reads=[b_st], writes=[b_st])
                S_.op("dve", lambda e: e.tensor_tensor(out=var[:], in0=mean[:], in1=mean[:], op=ALU.mult),
                      reads=[b_st], writes=[b_st])
                S_.op("dve", lambda e: e.scalar_tensor_tensor(out=var[:], in0=gsq[:], scalar=1.0 / D, in1=var[:],
                                                              op0=ALU.mult, op1=ALU.subtract), reads=[b_st], writes=[b_st])
                S_.op("dve", lambda e: e.tensor_scalar(out=var[:], in0=var[:], scalar1=EPS, scalar2=None, op0=ALU.add),
                      reads=[b_st], writes=[b_st])
                S_.op("pool", lambda e: e.tensor_tensor(out=rs2[:], in0=var[:], in1=mhalf[:, 0:NT], op=ALU.pow),
                      reads=[b_st, b_small], writes=[b_st])
                S_.op("dve", lambda e: e.scalar_tensor_tensor(out=nb2[:], in0=mean[:], scalar=-1.0, in1=rs2[:],
                                                              op0=ALU.mult, op1=ALU.mult), reads=[b_st], writes=[b_st])
                for tt in range(NT):
                    if tt % 2 == 0:
                        S_.op("act", lambda e, tt=tt: e.activation(out=XY[:, tt, :], in_=XY[:, tt, :], func=AF.Identity,
                                                                   scale=rs2[:, tt:tt + 1], bias=nb2[:, tt:tt + 1]),
                              reads=[b_st], writes=[b_xy[tt]])
                    else:
                        S_.op("dve", lambda e, tt=tt: e.tensor_scalar(out=XY[:, tt, :], in0=XY[:, tt, :], scalar1=rs2[:, tt:tt + 1],
                                                                      scalar2=nb2[:, tt:tt + 1], op0=ALU.mult, op1=ALU.add),
                              reads=[b_st], writes=[b_xy[tt]])

                NP = D // 256
                cnt = {"t": 0, "y": 0}

                def a_compute(i, t, b):
                    sv_ = slab_view(t)
                    for ci in range(2):
                        c = 2 * i + ci
                        gi = c // 2
                        yi = cnt["y"] % 2
                        cnt["y"] += 1
                        for hf in range(0, NB, 2):
                            tbs = list(range(hf, min(hf + 2, NB)))
                            banks = {}
                            for kind in ("u", "z"):
                                off = ci * 128 + (256 if kind == "z" else 0)
                                for tb in tbs:
                                    p_, bp = bank()
                                    banks[(kind, tb)] = (p_, bp)
                                    S_.group("pe", [lambda e, k=k, tb=tb, p_=p_, off=off: e.matmul(
                                        p_[:], lhsT=sv_[:, k, off:off + 128], rhs=hT[:, k, tb * 512:(tb + 1) * 512],
                                        start=(k == 0), stop=(k == DC - 1)) for k in range(DC)],
                                        reads=[b, b_hT[tb]], writes=[bp])
                            for tb in tbs:
                                p_, bp = bank()
                                banks[("s", tb)] = (p_, bp)
                                S_.group("pe", [lambda e, n=n, tb=tb, p_=p_: e.matmul(
                                    p_[:, n * 128:(n + 1) * 128], lhsT=XY[:, tb * 4 + n, c * 128:(c + 1) * 128],
                                    rhs=wsT[:, gi, :], start=True, stop=True) for n in range(4)],
                                    reads=[b_xy[tb * 4 + n] for n in range(4)] + [b_ws], writes=[bp])
                            slots = {}
                            for tb in tbs:
                                slots[tb] = cnt["t"] % 2
                                cnt["t"] += 1
                            for tb in tbs:
                                p_, bp = banks[("u", tb)]
                                s_ = slots[tb]
                                S_.op("act", lambda e, p_=p_, s_=s_: e.activation(out=gu[s_][:], in_=p_[:], func=AF.Gelu_apprx_tanh),
                                      reads=[bp], writes=[b_gu[s_]])
                            for tb in tbs:
                                p_, bp = banks[("z", tb)]
                                s_ = slots[tb]
                                S_.op("act", lambda e, p_=p_, s_=s_: e.activation(out=sz[s_][:], in_=p_[:], func=AF.Silu),
                                      reads=[bp], writes=[b_sz[s_]])
                            for tb in tbs:
                                p_, bp = banks[("s", tb)]
                                s_ = slots[tb]
                                S_.op("dve", lambda e, p_=p_, s_=s_: e.scalar_tensor_tensor(
                                    out=svb[s_][:].rearrange("p (n t) -> p n t", n=4),
                                    in0=p_[:].rearrange("p (n t) -> p n t", n=4), scalar=lng_s[:, c:c + 1],
                                    in1=C2T[:, c:c + 1, :].to_broadcast([128, 4, 128]),
                                    op0=ALU.mult, op1=ALU.add), reads=[bp, b_ws, b_c], writes=[b_svb[s_]])
                                S_.op("dve", lambda e, s_=s_: e.tensor_tensor(out=gu[s_][:], in0=gu[s_][:], in1=sz[s_][:], op=ALU.mult),
                                      reads=[b_sz[s_]], writes=[b_gu[s_]])
                                S_.op("dve", lambda e, s_=s_, tb=tb: e.tensor_tensor(
                                    out=yst[yi][:, tb * 512:(tb + 1) * 512], in0=gu[s_][:], in1=svb[s_][:], op=ALU.mult),
                                    reads=[b_gu[s_], b_svb[s_]], writes=[b_yst[yi]])
                        S_.dma("sp", yaT[c * 128:(c + 1) * 128, :], yst[yi][:], reads=[b_yst[yi]], owner=b_yst[yi])

                stream(NP, a_load, a_compute)
            preload(g_load, 1)
            S_.barrier()
            st_gb = ExitStack()
            ring.append(sb(st_gb, "ring2", [128, 8192], BF16))
            b_ring.append(Buf("ring2"))
            rstate["i"] = (b_ring.index(pre["slots"][0][1]) + 1) % 3

            if True:
                gst = [sb(st_gb, "gst%d" % i, [128, S], BF16) for i in range(2)]
                b_gst = [Buf() for _ in range(2)]
                cnt = {"y": 0}

                ada_gate = ada_compute_into(b_mod2)

                def g_compute(i, t, b):
                    kind, i = gjobs[i]
                    if kind == "ada":
                        return ada_gate(i, t, b)
                    f_, cb = divmod(i, ND)
                    sv_ = slab_view(t)
                    for jj in range(4):
                        n_ = cb * 4 + jj
                        yi = cnt["y"] % 2
                        cnt["y"] += 1
                        for tb in range(NB):
                            p_, bp = bank()
                            S_.group("pe", [lambda e, k=k, tb=tb, p_=p_, jj=jj: e.matmul(
                                p_[:], lhsT=sv_[:, k, jj * 128:(jj + 1) * 128], rhs=hT[:, k, tb * 512:(tb + 1) * 512],
                                start=(k == 0), stop=(k == DC - 1)) for k in range(DC)],
                                reads=[b, b_hT[tb]], writes=[bp])
                            S_.op("act", lambda e, p_=p_, tb=tb, yi=yi: e.activation(
                                out=gst[yi][:, tb * 512:(tb + 1) * 512], in_=p_[:], func=AF.Sigmoid),
                                reads=[bp], writes=[b_gst[yi]])
                        S_.dma("sp", sgT[f_][n_ * 128:(n_ + 1) * 128, :], gst[yi][:], reads=[b_gst[yi]], owner=b_gst[yi])

                stream(len(gjobs), g_load, g_compute)

            with ExitStack() as ph:
                qT = sb(ph, "qT", [128, 2, S], BF16)
                kT = sb(ph, "kT", [128, 2, S], BF16)
                vau = sb(ph, "vau", [128, NT, 258], BF16)
                szT = sb(ph, "szT", [128, 2, S], BF16)
                E = [sb(ph, "E%d" % i, [128, 512], BF16) for i in range(5)]
                o0 = sb(ph, "o0", [128, 4, 256], F32)
                od = sb(ph, "od", [128, 4, 256], F32)
                ybn = sb(ph, "ybn", [128, 4, 256], BF16)
                jf = sb(ph, "jf", [128, 256], F32)
                rinv = sb(ph, "rinv", [128, 8], F32)
                ssq = sb(ph, "ssq", [128, 4], F32)
                rsb = sb(ph, "rsb", [128, 4], F32)
                ybst = [sb(ph, "ybst%d" % i, [128, S], BF16) for i in range(2)]
                b_q = [Buf() for _ in range(NB)]
                b_k = [Buf() for _ in range(NB)]
                b_v = [Buf() for _ in range(NT)]
                b_szT = [Buf() for _ in range(NB)]
                b_E = [Buf() for _ in range(5)]
                b_o0 = [Buf() for _ in range(4)]; b_od = [Buf() for _ in range(4)]; b_ybn = Buf(); b_ri = Buf(); b_sq = Buf()
                b_ybst = [Buf() for _ in range(2)]
                b_one = Buf()
                S_.op("dve", lambda e: e.memset(vau[:, :, 256:258], 1.0), writes=[b_one])
                ecnt = {"e": 0, "s": 0}
                qscale = 1.0 / math.sqrt(128.0)

                def proj_fm(sv_, b, off, tb):
                    p_, bp = bank()
                    S_.group("pe", [lambda e, k=k, p_=p_: e.matmul(
                        p_[:], lhsT=sv_[:, k, off:off + 128], rhs=hT[:, k, tb * 512:(tb + 1) * 512],
                        start=(k == 0), stop=(k == DC - 1)) for k in range(DC)],
                        reads=[b, b_hT[tb]], writes=[bp])
                    return p_, bp

                LOOK = 3
                NE = 5

                def attention(h):
                    tiles = [(tb, c, j) for tb in range(NB) for c in range(2) for j in range(4 * tb + 4)]
                    n = len(tiles)
                    info = {}
                    pending = []

                    def sbank():
                        i_ = 4 + ecnt["s"] % 4
                        ecnt["s"] += 1
                        return ps[i_], b_ps[i_]

                    def emit_S(idx):
                        tb, c, j = tiles[idx]
                        r0 = max(0, j - 4 * tb)
                        c0 = r0 * 128
                        off = LOFF - 128 * (j - 4 * tb)
                        p_, bp = sbank()
                        diag = j >= 4 * tb
                        use_aug = SLOPES[h] > 1.0 / 16.0 + 1e-9
                        fns = [lambda e: e.matmul(
                            p_[:, c0:512], lhsT=kT[:, c, j * 128:(j + 1) * 128],
                            rhs=qT[:, c, tb * 512 + c0:(tb + 1) * 512], start=True, stop=not (diag or use_aug))]
                        if diag:
                            fns.append(lambda e: e.matmul(
                                p_[:, c0:c0 + 128], lhsT=ident_b[:], rhs=maskT_b[:], start=False, stop=not use_aug))
                        if use_aug:
                            fns.append(lambda e: e.matmul(
                                p_[:, c0:512], lhsT=kaug[:, h * 128:(h + 1) * 128],
                                rhs=laug[:, off + c0:off + 512], start=False, stop=True))
                        S_.group("pe", fns, reads=[b_k[j // 4], b_q[tb], b_cp], writes=[bp])
                        ei = ecnt["e"] % NE
                        ecnt["e"] += 1
                        if use_aug:
                            S_.op("act", lambda e: e.activation(out=E[ei][:, c0:512], in_=p_[:, c0:512], func=AF.Exp),
                                  reads=[bp], writes=[b_E[ei]])
                        else:
                            bc = h * 16 + (j - 4 * tb) + 12
                            S_.op("act", lambda e: e.activation(out=E[ei][:, c0:512], in_=p_[:, c0:512], func=AF.Exp,
                                                                bias=btab[:, bc:bc + 1], scale=1.0),
                                  reads=[bp, b_c], writes=[b_E[ei]])
                        info[idx] = (ei, r0)

                    def emit_PV(idx):
                        tb, c, j = tiles[idx]
                        ei, r0 = info.pop(idx)
                        if j == 0:
                            for r in range(r0, 4):
                                S_.group("pe", [lambda e, r=r: e.matmul(
                                    ps[r][:, 0:257], lhsT=E[ei][:, r * 128:(r + 1) * 128], rhs=vau[:, j, 0:257],
                                    start=True, stop=(j == 4 * tb + r))],
                                    reads=[b_E[ei], b_v[j], b_one], writes=[b_ps[r]])
                        else:
                            S_.group("pe", [lambda e, r=r: e.matmul(
                                ps[r][:, 0:257], lhsT=E[ei][:, r * 128:(r + 1) * 128], rhs=vau[:, j, 0:257],
                                start=(j == 0), stop=(j == 4 * tb + r)) for r in range(r0, 4)],
                                reads=[b_E[ei], b_v[j], b_one], writes=[b_ps[r] for r in range(r0, 4)])
                        if j == 4 * tb + 3:
                            evac(idx, tb, c)

                    def evac(idx, tb, c):
                        bpo = [b_ps[r] for r in range(4)]
                        cs = slice(c * 4, c * 4 + 4)
                        S_.op("dve", lambda e: e.reciprocal(out=rinv[:, cs], in_=psall[:, 0:4, 256]),
                              reads=bpo, writes=[b_ri])
                        if c == 1:
                            S_.op("dve", lambda e: e.tensor_scalar(out=rinv[:, cs], in0=rinv[:, cs], scalar1=neglam[:, 0:1],
                                                                   scalar2=None, op0=ALU.mult), reads=[b_ri, b_l], writes=[b_ri])
                        for r in range(4):
                            ci = c * 4 + r
                            if c == 0:
                                S_.op("dve", lambda e, r=r, ci=ci: e.tensor_scalar(
                                    out=o0[:, r, :], in0=ps[r][:, 0:256], scalar1=rinv[:, ci:ci + 1], scalar2=None, op0=ALU.mult),
                                    reads=[b_ps[r], b_ri], writes=[b_o0[r]])
                            else:
                                S_.op("dve", lambda e, r=r, ci=ci: e.scalar_tensor_tensor(
                                    out=od[:, r, :], in0=ps[r][:, 0:256], scalar=rinv[:, ci:ci + 1], in1=o0[:, r, :],
                                    op0=ALU.mult, op1=ALU.add), reads=[b_ps[r], b_ri, b_o0[r]], writes=[b_od[r]])
                        if c == 0:
                            return
                        for r in range(4):
                            S_.op("dve", lambda e, r=r: e.scalar_tensor_tensor(
                                out=jf[:], in0=od[:, r, :], scalar=1.0, in1=od[:, r, :],
                                op0=ALU.mult, op1=ALU.mult, accum_out=ssq[:, r:r + 1]), reads=[b_od[r]], writes=[b_sq])
                        S_.op("dve", lambda e: e.tensor_scalar(out=ssq[:], in0=ssq[:], scalar1=1.0 / 256.0, scalar2=SUBLN_EPS,
                                                               op0=ALU.mult, op1=ALU.add), reads=[b_sq], writes=[b_sq])
                        S_.op("pool", lambda e: e.tensor_tensor(out=rsb[:], in0=ssq[:], in1=mhalf[:, 0:4], op=ALU.pow),
                              reads=[b_sq, b_small], writes=[b_sq])
                        for r in range(4):
                            S_.op("dve", lambda e, r=r: e.scalar_tensor_tensor(
                                out=ybn[:, r, :], in0=od[:, r, :], scalar=rsb[:, r:r + 1], in1=G2[:],
                                op0=ALU.mult, op1=ALU.mult), reads=[b_od[r], b_sq, b_l], writes=[b_ybn])

                        def transposes(tb=tb):
                            for e2 in range(2):
                                pt, b_pt = sbank()
                                ptb = pt.bitcast(BF16)
                                S_.group("pe", [lambda e, r=r: e.transpose(
                                    ptb[:, r * 128:(r + 1) * 128], ybn[:, r, e2 * 128:(e2 + 1) * 128], ident_b[:]) for r in range(4)],
                                    reads=[b_ybn, b_cp], writes=[b_pt])
                                S_.op("dve", lambda e: e.tensor_tensor(
                                    out=ybst[e2][:, tb * 512:(tb + 1) * 512], in0=ptb[:, 0:512], in1=szT[:, e2, tb * 512:(tb + 1) * 512],
                                    op=ALU.mult), reads=[b_pt, b_szT[tb]], writes=[b_ybst[e2]])
                        pending.append((idx + LOOK + 10, transposes))

                    for step in range(n + LOOK):
                        if step == n // 2 and h == H - 1:
                            rstate["i"] = 0
                            preload(m_load, 1)
                        if step < n:
                            emit_S(step)
                        if step - LOOK >= 0:
                            emit_PV(step - LOOK)
                        while pending and pending[0][0] <= step:
                            pending.pop(0)[1]()
                    pstate["i"] = 0

                    def tail():
                        while pending:
                            pending.pop(0)[1]()
                        for e2 in range(2):
                            r_ = h * 256 + e2 * 128
                            S_.dma("sp", ybT[r_:r_ + 128, :], ybst[e2][:], reads=[b_ybst[e2]], owner=b_ybst[e2])
                    carry.append(tail)

                carry = []

                def b_compute(i, t, b):
                    h, which = divmod(i, 2)
                    sv_ = slab_view(t)
                    if which == 0:
                        for c in range(2):
                            for tb in range(NB):
                                p_, bp = proj_fm(sv_, b, c * 128, tb)
                                S_.op("dve", lambda e, p_=p_, c=c, tb=tb: e.tensor_scalar(
                                    out=qT[:, c, tb * 512:(tb + 1) * 512], in0=p_[:], scalar1=qscale, scalar2=None, op0=ALU.mult),
                                    reads=[bp], writes=[b_q[tb]])
                            if c == 0:
                                while carry:
                                    carry.pop(0)()
                        for c in range(2):
                            for tb in range(NB):
                                p_, bp = proj_fm(sv_, b, 256 + c * 128, tb)
                                S_.op("dve", lambda e, p_=p_, c=c, tb=tb: e.tensor_copy(
                                    out=kT[:, c, tb * 512:(tb + 1) * 512], in_=p_[:]), reads=[bp], writes=[b_k[tb]])
                    else:
                        for tt in range(NT):
                            p_, bp = bank()
                            S_.group("pe", [lambda e, k=k, p_=p_, tt=tt: e.matmul(
                                p_[:, 0:256], lhsT=hT[:, k, tt * 128:(tt + 1) * 128], rhs=sv_[:, k, 0:256],
                                start=(k == 0), stop=(k == DC - 1)) for k in range(DC)],
                                reads=[b, b_hT[tt // 4]], writes=[bp])
                            S_.op("dve", lambda e, p_=p_, tt=tt: e.tensor_copy(out=vau[:, tt, 0:256], in_=p_[:, 0:256]),
                                  reads=[bp], writes=[b_v[tt]])
                        for e2 in range(2):
                            for tb in range(NB):
                                p_, bp = proj_fm(sv_, b, 256 + e2 * 128, tb)
                                S_.op("act", lambda e, p_=p_, e2=e2, tb=tb: e.activation(
                                    out=szT[:, e2, tb * 512:(tb + 1) * 512], in_=p_[:], func=AF.Silu),
                                    reads=[bp], writes=[b_szT[tb]])
                        attention(h)

                stream(2 * H, b_load, b_compute)
                while carry:
                    carry.pop(0)()
            ring.pop()
            b_ring.pop()
            rstate["i"] = 1
            st_gb.close()
        S_.barrier()

        with ExitStack() as ph:
            ya_s = sb(ph, "ya_s", [128, DC, S], BF16)
            yb_s = sb(ph, "yb_s", [128, DC, S], BF16)
            b_ya = [Buf() for _ in range(NB)]
            b_yb = [Buf() for _ in range(NB)]
            sg_s = [[sb(ph, "sg%d_%d" % (f_, i), [128, S], BF16) for i in range(2)] for f_ in range(2)]
            b_sg = [[Buf() for _ in range(2)] for _ in range(2)]
            t1 = [sb(ph, "t1_%d" % i, [128, 512], F32) for i in range(2)]
            t2 = [sb(ph, "t2_%d" % i, [128, 512], F32) for i in range(2)]
            b_t1 = [Buf() for _ in range(2)]
            b_t2 = [Buf() for _ in range(2)]
            mst = [sb(ph, "mst%d" % i, [128, S], BF16) for i in range(2)]
            b_mst = [Buf() for _ in range(2)]
            yaTv = yaT.rearrange("(k p) t -> p k t", p=128)
            ybTv = ybT.rearrange("(k p) t -> p k t", p=128)
            b_ch = [Buf("chain%d" % i) for i in range(3)]
            for tb in range(NB):
                S_.dma("sp", ya_s[:, :, tb * 512:(tb + 1) * 512], yaTv[:, :, tb * 512:(tb + 1) * 512],
                       writes=[b_ya[tb], b_ch[0]], owner=b_ya[tb])
                S_.dma("act", yb_s[:, :, tb * 512:(tb + 1) * 512], ybTv[:, :, tb * 512:(tb + 1) * 512],
                       writes=[b_yb[tb], b_ch[1]], owner=b_yb[tb])
            cnt = {"y": 0, "t": 0}

            wo0 = {}

            def m_compute(i, t, b):
                sv_ = slab_view(t)
                if i == D // 256 - 1:
                    wt, wb = ring_next()
                    S_.dma("pool", slab_view(wt), wsrc(w_o, 0, 512), writes=[wb])
                    wo0["t"], wo0["b"] = wt, wb
                for jj in range(2):
                    n_ = i * 2 + jj
                    for f2 in range(2):
                        S_.dma("sp", sg_s[f2][jj][:], sgT[f2][n_ * 128:(n_ + 1) * 128, :], writes=[b_sg[f2][jj]])
                for tb in range(NB):
                    for jj in range(2):
                        pa, bpa = bank()
                        S_.group("pe", [lambda e, k=k: e.matmul(
                            pa[:], lhsT=sv_[:, k, jj * 128:(jj + 1) * 128], rhs=ya_s[:, k, tb * 512:(tb + 1) * 512],
                            start=(k == 0), stop=(k == DC - 1)) for k in range(DC)],
                            reads=[b, b_ya[tb]], writes=[bpa])
                        pb, bpb = bank()
                        S_.group("pe", [lambda e, k=k: e.matmul(
                            pb[:], lhsT=sv_[:, k, 256 + jj * 128:256 + (jj + 1) * 128], rhs=yb_s[:, k, tb * 512:(tb + 1) * 512],
                            start=(k == 0), stop=(k == DC - 1)) for k in range(DC)],
                            reads=[b, b_yb[tb]], writes=[bpb])
                        ti = cnt["t"] % 2
                        cnt["t"] += 1
                        S_.op("dve", lambda e: e.tensor_tensor(
                            out=t1[ti][:], in0=pa[:], in1=sg_s[0][jj][:, tb * 512:(tb + 1) * 512], op=ALU.mult),
                            reads=[bpa, b_sg[0][jj]], writes=[b_t1[ti]])
                        S_.op("dve", lambda e: e.tensor_tensor(
                            out=t2[ti][:], in0=pb[:], in1=sg_s[1][jj][:, tb * 512:(tb + 1) * 512], op=ALU.mult),
                            reads=[bpb, b_sg[1][jj]], writes=[b_t2[ti]])
                        S_.op("pool", lambda e: e.tensor_tensor(
                            out=mst[jj][:, tb * 512:(tb + 1) * 512], in0=t1[ti][:], in1=t2[ti][:], op=ALU.add),
                            reads=[b_t1[ti], b_t2[ti]], writes=[b_mst[jj]])
                for jj in range(2):
                    n_ = i * 2 + jj
                    S_.dma("sp", mT[n_ * 128:(n_ + 1) * 128, :], mst[jj][:], reads=[b_mst[jj]], owner=b_mst[jj])

            stream(D // 256, m_load, m_compute)
        S_.barrier()

        with ExitStack() as ph:
            m_s = sb(ph, "m_s", [128, DC, S], BF16)
            wo_s = sb(ph, "wo_s", [128, DC, D], BF16)
            b_m = [Buf() for _ in range(NB)]
            b_wo = [Buf() for _ in range(ND)]
            gate_bc = sb(ph, "gate_bc", [128, D], F32)
            fng_bc = sb(ph, "fng_bc", [128, D], F32)
            xo = [sb(ph, "xo%d" % i, [128, D], F32)[:] for i in range(2)]
            for rt in ring:
                if rt is wo0["t"]:
                    continue
                rf = rt[:].bitcast(F32)
                for q_ in range(min(2, 4096 // D)):
                    xo.append(rf[:, q_ * D:(q_ + 1) * D])
            NXO = len(xo)
            b_xo = [Buf() for _ in range(NXO)]
            tm = [sb(ph, "tm%d" % i, [128, 512], F32) for i in range(2)]
            b_tm = [Buf() for _ in range(2)]
            dg = sb(ph, "dg", [128, 128], F32)
            ones2 = sb(ph, "ones2", [128, 128], F32)
            s2 = sb(ph, "s2", [128, NT], F32)
            r2 = sb(ph, "r2", [128, NT], F32)
            b_dg = Buf(); b_gb = Buf(); b_s2 = Buf(); b_fg = Buf()
            mTv = mT.rearrange("(k p) t -> p k t", p=128)
            b_ch2 = [Buf("ochain%d" % i) for i in range(2)]
            b_wo[0] = wo0["b"]
            wo0v = slab_view(wo0["t"])

            def wo_view(k, cb):
                return wo0v[:, k, :] if cb == 0 else wo_s[:, k, cb * 512:(cb + 1) * 512]
            for cb in range(1, ND):
                S_.dma("pool", wo_s[:, :, cb * 512:(cb + 1) * 512], wsrc(w_o, cb * 512, 512),
                       writes=[b_wo[cb], b_ch2[0]], owner=b_wo[cb])
            for tb in range(NB):
                S_.dma("act", m_s[:, :, tb * 512:(tb + 1) * 512], mTv[:, :, tb * 512:(tb + 1) * 512],
                       writes=[b_m[tb], b_ch2[1]], owner=b_m[tb])
            S_.dma("sp", fng_bc[:], fng.partition_broadcast(128), writes=[b_fg])
            S_.op("dve", lambda e: e.memset(ones2[:], 1.0), writes=[b_dg])
            for k in range(DC):
                S_.op("dve", lambda e, k=k: e.tensor_scalar(out=dg[:], in0=ident_f[:], scalar1=modT[:, 2 * DC + k:2 * DC + k + 1],
                                                            scalar2=None, op0=ALU.mult), reads=[b_c, b_mod2], writes=[b_dg])
                if k % 4 == 0:
                    pg, b_pg = bank()
                S_.group("pe", [lambda e, k=k, pg=pg: e.matmul(pg[:, (k % 4) * 128:(k % 4 + 1) * 128], lhsT=ones2[:], rhs=dg[:],
                                                              start=True, stop=True)], reads=[b_dg], writes=[b_pg])
                if k % 4 == 3:
                    kb = k // 4
                    S_.op("dve", lambda e, pg=pg, kb=kb: e.tensor_copy(out=gate_bc[:, kb * 512:(kb + 1) * 512], in_=pg[:]),
                          reads=[b_pg], writes=[b_gb])
            XA = min(3, NXO - 2)

            def load_xo(tt):
                S_.dma("sp", xo[tt % NXO][:], x[tt * 128:(tt + 1) * 128, :], writes=[b_xo[tt % NXO]])

            def epilogue(tt):
                xi = tt % NXO
                S_.op("act", lambda e: e.activation(out=junk[:], in_=xo[xi][:], func=AF.Square,
                                                    accum_out=s2[:, tt:tt + 1]), reads=[b_xo[xi]], writes=[b_s2])
                S_.op("dve", lambda e: e.tensor_scalar(out=s2[:, tt:tt + 1], in0=s2[:, tt:tt + 1], scalar1=1.0 / D, scalar2=EPS,
                                                       op0=ALU.mult, op1=ALU.add), reads=[b_s2], writes=[b_s2])
                S_.op("pool", lambda e: e.tensor_tensor(out=r2[:, tt:tt + 1], in0=s2[:, tt:tt + 1], in1=mhalf[:, 0:1], op=ALU.pow),
                      reads=[b_s2, b_small], writes=[b_s2])
                S_.op("dve", lambda e: e.scalar_tensor_tensor(
                    out=xo[xi][:], in0=xo[xi][:], scalar=r2[:, tt:tt + 1], in1=fng_bc[:], op0=ALU.mult, op1=ALU.mult),
                    reads=[b_s2, b_fg], writes=[b_xo[xi]])
                S_.dma("sp", y[tt * 128:(tt + 1) * 128, :], xo[xi][:], reads=[b_xo[xi]], owner=b_xo[xi])

            for tt in range(min(XA, NT)):
                load_xo(tt)
            def tile_block(tt, cb):
                xi = tt % NXO
                p_, bp = bank()
                S_.group("pe", [lambda e, k=k: e.matmul(
                    p_[:], lhsT=m_s[:, k, tt * 128:(tt + 1) * 128], rhs=wo_view(k, cb),
                    start=(k == 0), stop=(k == DC - 1)) for k in range(DC)],
                    reads=[b_m[tt // 4], b_wo[cb]], writes=[bp])
                ti = cnt_o["t"] % 2
                cnt_o["t"] += 1
                S_.op("dve", lambda e: e.tensor_tensor(
                    out=tm[ti][:], in0=p_[:], in1=gate_bc[:, cb * 512:(cb + 1) * 512], op=ALU.mult),
                    reads=[bp, b_gb], writes=[b_tm[ti]])
                S_.op("pool", lambda e: e.tensor_tensor(
                    out=xo[xi][:, cb * 512:(cb + 1) * 512], in0=xo[xi][:, cb * 512:(cb + 1) * 512], in1=tm[ti][:], op=ALU.add),
                    reads=[b_tm[ti]], writes=[b_xo[xi]])

            cnt_o = {"t": 0}
            NF = min(2, NT)
            for tt in range(NF, min(NF + XA, NT)):
                load_xo(tt)
            for cb in range(ND):
                for tt in range(NF):
                    tile_block(tt, cb)
            for tt in range(NF, NT):
                for cb in range(ND):
                    tile_block(tt, cb)
                if tt == NF:
                    for t0 in range(NF - 1):
                        epilogue(t0)
                epilogue(tt - 1)
                if tt + XA < NT and tt + XA >= NF + XA:
                    load_xo(tt + XA)
            epilogue(NT - 1)
        S_.barrier(engines=("sp", "pe", "act", "dve", "pool"))
        build_program.stats = (S_.ninst, S_.nwait, len(S_.sems))
    return nc


def _consts(H):
    ident = np.eye(128, dtype=np.float32)
    tril = np.tril(np.ones((128, 128), dtype=np.float32))
    s_i = np.arange(128)[:, None]
    t_i = np.arange(128)[None, :]
    maskT = np.where(s_i <= t_i, 0.0, NEG).astype(np.float32)
    idx = np.arange(LLEN)
    N = LOFF - idx
    a = np.floor_divide(N, 256)
    b = N - 256 * a
    laug = np.zeros((128, LLEN), dtype=np.float32)
    laug[0] = 1.0
    laug[1] = 256.0 * a
    laug[2] = b
    slopes = [2.0 ** (-8.0 * (i + 1) / H) for i in range(H)]
    jv = np.arange(128, dtype=np.float64)
    kaug = np.zeros((128, H * 128), dtype=np.float32)
    for hh, sl in enumerate(slopes):
        kaug[0, hh * 128:(hh + 1) * 128] = sl * jv
        kaug[1, hh * 128:(hh + 1) * 128] = sl
        kaug[2, hh * 128:(hh + 1) * 128] = sl
    btab = np.zeros((128, H * 16), dtype=np.float32)
    for hh, sl in enumerate(slopes):
        for dl in range(-12, 4):
            btab[:, hh * 16 + dl + 12] = sl * (jv + 128.0 * dl)
    return ident, tril, maskT, laug, kaug, btab


def make_in_maps(inp, S, D):
    B = inp["x"].shape[0]
    DC = D // 128
    H = D // 256
    f = lambda a: np.ascontiguousarray(np.asarray(a, dtype=np.float32))
    colT = lambda v: f(np.asarray(v).reshape(-1, 128).T)
    ident, tril, maskT, laug, kaug, btab = _consts(H)
    shared = {
        "w_ada": f(inp["w_ada"][0]), "b_adaT": colT(inp["b_ada"][0]), "ngT": colT(inp["norm_gain"][0]),
        "w_in": f(inp["w_in"][0]), "lngT": colT(inp["ln_v_gain"][0]), "lnbT": colT(inp["ln_v_bias"][0]),
        "w_sp": f(inp["w_spatial"][0]), "b_sp": f(np.asarray(inp["b_spatial"][0]).reshape(1, -1)),
        "lamv": f(np.concatenate([np.asarray(inp[k][0]).reshape(-1) for k in
                                  ("lambda_q1", "lambda_k1", "lambda_q2", "lambda_k2")]).reshape(1, -1)),
        "sublng": f(np.asarray(inp["subln_gain"][0]).reshape(1, -1)),
        "w_a": f(inp["w_branch_a"][0]), "w_b": f(inp["w_branch_b"][0]), "w_o": f(inp["w_out"][0]),
        "fng": f(np.asarray(inp["final_norm_gain"]).reshape(1, -1)),
        "c_ident": ident, "c_tril": tril, "c_maskT": maskT, "c_laug": laug, "c_kaug": kaug, "c_btab": btab,
    }
    maps = []
    for b in range(B):
        m = dict(shared)
        m["x"] = f(inp["x"][b])
        m["cT"] = colT(inp["c"][b])
        maps.append(m)
    return maps


_CACHE = {}


def kernel(**inputs):
    x = np.asarray(inputs["x"])
    B, S, D = x.shape
    key = (S, D)
    if key not in _CACHE:
        _CACHE[key] = build_program(S, D)
    nc = _CACHE[key]
    in_maps = make_in_maps(inputs, S, D)
    res = run_bass_kernel_spmd(nc, in_maps, core_ids=list(range(B)))
    return np.stack([np.asarray(r["y"], dtype=np.float32) for r in res.results], axis=0)
```
